# Optimizing a Trainium2 kernel written in Bass

```python
import math
import jax, jax.numpy as jnp
from jax import lax
import numpy as np

D_MODEL = 2048
BATCH = 4
SEQ = 4096
DEPTH = 2

N_META = 16
EPS = 1e-6
D_CONV = D_MODEL // 2
CONV_WIDTH = 3
N_HEADS = D_MODEL // 128
QK_NOPE = 128
QK_ROPE = 64
QK_HEAD = QK_NOPE + QK_ROPE
V_HEAD = 128
Q_LORA = 512
KV_LORA = 512
ROPE_THETA = 10000.0
Q_BLOCK = 128
D_POOL = D_MODEL // 2
POOL_WINDOWS = (2, 4, 8, 16)
POOL_GROUP = D_POOL // len(POOL_WINDOWS)
N_BRANCH = 3
D_FF = 4 * D_MODEL
D_IN = 3 * D_CONV + Q_LORA + KV_LORA + QK_ROPE + D_POOL + N_BRANCH * D_MODEL
IN_SPLITS = (3 * D_CONV,
             3 * D_CONV + Q_LORA,
             3 * D_CONV + Q_LORA + KV_LORA,
             3 * D_CONV + Q_LORA + KV_LORA + QK_ROPE,
             3 * D_CONV + Q_LORA + KV_LORA + QK_ROPE + D_POOL)

kernel_name = 'hybrid_gated_conv_mla_pool_block'


def rms_norm(x, g):
    xf = x.astype(jnp.float32)
    y = xf * lax.rsqrt(jnp.mean(xf * xf, axis=-1, keepdims=True) + EPS)
    return (y * g.astype(jnp.float32)).astype(x.dtype)


def rope_tables(T, dtype):
    pos = jnp.arange(T, dtype=jnp.float32)
    inv = ROPE_THETA ** (-jnp.arange(0, QK_ROPE, 2, dtype=jnp.float32) / QK_ROPE)
    ang = pos[:, None] * inv[None, :]
    return jnp.cos(ang).astype(dtype), jnp.sin(ang).astype(dtype)


def apply_rope_tail(x, cos, sin):
    x_nope, x_rope = x[..., :QK_NOPE], x[..., QK_NOPE:]
    x1, x2 = jnp.split(x_rope, 2, axis=-1)
    c, s = cos[None, :, None, :], sin[None, :, None, :]
    return jnp.concatenate([x_nope, x1 * c - x2 * s, x2 * c + x1 * s], axis=-1)


def causal_short_conv(u, w):
    T = u.shape[1]
    up = jnp.pad(u, ((0, 0), (CONV_WIDTH - 1, 0), (0, 0)))
    return sum(w[j] * up[:, j:j + T] for j in range(CONV_WIDTH))


def causal_block_attention(q, k, v):
    B, T, H, dk = q.shape
    Tp = -(-T // Q_BLOCK) * Q_BLOCK
    pad = ((0, 0), (0, Tp - T), (0, 0), (0, 0))
    q, k, v = jnp.pad(q, pad), jnp.pad(k, pad), jnp.pad(v, pad)
    nb = Tp // Q_BLOCK
    q_blocks = q.reshape(B, nb, Q_BLOCK, H, dk).swapaxes(0, 1)
    k_pos = jnp.arange(Tp)
    scale = dk ** -0.5

    def one_block(args):
        qb, blk = args
        s = jnp.einsum('bqhd,bkhd->bhqk', qb, k).astype(jnp.float32) * scale
        q_pos = blk * Q_BLOCK + jnp.arange(Q_BLOCK)
        s = jnp.where(q_pos[:, None] >= k_pos[None, :], s, -jnp.inf)
        p = jax.nn.softmax(s, axis=-1).astype(v.dtype)
        return jnp.einsum('bhqk,bkhd->bqhd', p, v)

    out = lax.map(one_block, (q_blocks, jnp.arange(nb)))
    return out.swapaxes(0, 1).reshape(B, Tp, H, -1)[:, :T]


def multiscale_pool(u, pool_w, pool_scale):
    B, T, _ = u.shape
    uf = u.astype(jnp.float32)
    groups = jnp.split(uf, len(POOL_WINDOWS), axis=-1)
    seen = jnp.arange(1, T + 1, dtype=jnp.float32)
    outs = []
    for g, w in zip(groups, POOL_WINDOWS):
        cs = jnp.cumsum(g, axis=1)
        lagged = jnp.pad(cs, ((0, 0), (w, 0), (0, 0)))[:, :T]
        count = jnp.minimum(seen, float(w))[None, :, None]
        outs.append((cs - lagged) / count - g)
    pooled = jnp.stack(outs, axis=2).astype(u.dtype)
    mixed = jnp.einsum('btgc,gcd->btgd', pooled, pool_w).reshape(B, T, D_POOL)
    return mixed * pool_scale


def hybrid_layer(x, cos, sin, attn_norm, w_in, conv_w, q_lat_norm, kv_lat_norm, w_uq, w_ukv,
                 q_norm, k_norm, pool_w, pool_scale, w_branch_a, w_branch_b, w_branch_c, w_o,
                 mlp_norm, w_up, w_down):
    B, T, _ = x.shape
    h = rms_norm(x, attn_norm)
    proj = h @ w_in
    a_in, q_lat, kv_lat, k_rope, pool_in, gate_logits = jnp.split(proj, IN_SPLITS, axis=-1)

    u_a, b_a, c_a = jnp.split(a_in, 3, axis=-1)
    y_a = b_a * causal_short_conv(c_a * u_a, conv_w)

    q = (rms_norm(q_lat, q_lat_norm) @ w_uq).reshape(B, T, N_HEADS, QK_HEAD)
    kv = (rms_norm(kv_lat, kv_lat_norm) @ w_ukv).reshape(B, T, N_HEADS, QK_NOPE + V_HEAD)
    k_nope, v = kv[..., :QK_NOPE], kv[..., QK_NOPE:]
    k = jnp.concatenate(
        [k_nope, jnp.broadcast_to(k_rope[:, :, None, :], (B, T, N_HEADS, QK_ROPE))], axis=-1)
    q = apply_rope_tail(rms_norm(q, q_norm), cos, sin)
    k = apply_rope_tail(rms_norm(k, k_norm), cos, sin)
    y_b = causal_block_attention(q, k, v).reshape(B, T, N_HEADS * V_HEAD)

    y_c = multiscale_pool(pool_in, pool_w, pool_scale)

    gates = jax.nn.sigmoid(gate_logits).reshape(B, T, N_BRANCH, D_MODEL)
    merged = (gates[:, :, 0] * (y_a @ w_branch_a)
              + gates[:, :, 1] * (y_b @ w_branch_b)
              + gates[:, :, 2] * (y_c @ w_branch_c))
    x = x + merged @ w_o

    h2 = rms_norm(x, mlp_norm)
    return x + jnp.square(jax.nn.relu(h2 @ w_up)) @ w_down


def setup_inputs(seed: int = 0) -> dict:
    key = jax.random.key(seed)
    ks = jax.random.split(key, 24)
    f32 = jnp.float32

    def w(k, shape, fan_in):
        return jax.random.normal(k, shape, f32) * fan_in ** -0.5

    def gain(k, shape):
        return 1.0 + 0.05 * jax.random.normal(k, shape, f32)

    L = DEPTH
    return {
        'x': jax.random.normal(ks[0], (BATCH, SEQ, D_MODEL), f32),
        'meta_tokens': jax.random.normal(ks[1], (N_META, D_MODEL), f32),
        'attn_norm': gain(ks[2], (L, D_MODEL)),
        'w_in': w(ks[3], (L, D_MODEL, D_IN), D_MODEL),
        'conv_w': w(ks[4], (L, CONV_WIDTH, D_CONV), CONV_WIDTH),
        'q_lat_norm': gain(ks[5], (L, Q_LORA)),
        'kv_lat_norm': gain(ks[6], (L, KV_LORA)),
        'w_uq': w(ks[7], (L, Q_LORA, N_HEADS * QK_HEAD), Q_LORA),
        'w_ukv': w(ks[8], (L, KV_LORA, N_HEADS * (QK_NOPE + V_HEAD)), KV_LORA),
        'q_norm': gain(ks[9], (L, QK_HEAD)),
        'k_norm': gain(ks[10], (L, QK_HEAD)),
        'pool_w': w(ks[11], (L, len(POOL_WINDOWS), POOL_GROUP, POOL_GROUP), POOL_GROUP),
        'pool_scale': gain(ks[12], (L, D_POOL)),
        'w_branch_a': w(ks[13], (L, D_CONV, D_MODEL), D_CONV),
        'w_branch_b': w(ks[14], (L, N_HEADS * V_HEAD, D_MODEL), N_HEADS * V_HEAD),
        'w_branch_c': w(ks[15], (L, D_POOL, D_MODEL), D_POOL),
        'w_o': w(ks[16], (L, D_MODEL, D_MODEL), D_MODEL),
        'mlp_norm': gain(ks[17], (L, D_MODEL)),
        'w_up': w(ks[18], (L, D_MODEL, D_FF), D_MODEL),
        'w_down': w(ks[19], (L, D_FF, D_MODEL), D_FF),
    }


def reference(x, meta_tokens, attn_norm, w_in, conv_w, q_lat_norm, kv_lat_norm, w_uq, w_ukv,
              q_norm, k_norm, pool_w, pool_scale, w_branch_a, w_branch_b, w_branch_c, w_o,
              mlp_norm, w_up, w_down):
    B = x.shape[0]
    meta = jnp.broadcast_to(meta_tokens[None].astype(x.dtype), (B, N_META, D_MODEL))
    h = jnp.concatenate([meta, x], axis=1)
    cos, sin = rope_tables(h.shape[1], h.dtype)
    for l in range(DEPTH):
        h = hybrid_layer(h, cos, sin, attn_norm[l], w_in[l], conv_w[l], q_lat_norm[l],
                         kv_lat_norm[l], w_uq[l], w_ukv[l], q_norm[l], k_norm[l], pool_w[l],
                         pool_scale[l], w_branch_a[l], w_branch_b[l], w_branch_c[l], w_o[l],
                         mlp_norm[l], w_up[l], w_down[l])
    return h[:, N_META:]
```

```python
import contextlib
import numpy as np
import ml_dtypes
import concourse.bass as bass
import concourse.mybir as mybir
from concourse.bass_utils import run_bass_kernel_spmd

F32 = mybir.dt.float32
BF16 = mybir.dt.bfloat16
AF = mybir.ActivationFunctionType
ALU = mybir.AluOpType

D = 2048
NCD = 16
H = 16
SEQ = 4096
NMETA = 16
TALL = SEQ + NMETA
NTOK = TALL // 2
NPRE = NTOK
TT = 257
NT = NTOK // TT
NKEY = NPRE + NTOK
NKB = (NKEY + 127) // 128
SEG0_KB = 16
SEGK = 2064
D_IN = 11328
C_U, C_B, C_C, C_QL, C_KVL, C_KR, C_POOL, C_GATE = 0, 1024, 2048, 3072, 3584, 4096, 4160, 5184
DFF = 8192
EPS = 1e-6
SCALE = 192.0 ** -0.5
SLOT = 4096
NSLOT = 5
P_AN, P_MN, P_QLN, P_KVLN, P_QN, P_QR, P_KN, P_KR, P_CW, P_PS = 0, 16, 32, 36, 40, 41, 42, 43, 44, 68
NPARAM = 76


class Res:
    __slots__ = ("name", "excl", "w", "rs", "ov")

    def __init__(self, name, excl=False):
        self.name = name
        self.excl = excl
        self.w = None
        self.rs = {}
        self.ov = [self]


class Eng:
    def __init__(self, name, eng, sem):
        self.name, self.eng, self.sem = name, eng, sem
        self.cnt = 0
        self.seen = {}


class KB:
    def __init__(self, nc, es):
        self.nc = nc
        self.dry = False
        mk = lambda n: es.enter_context(nc.semaphore(n))
        self.E = {
            "pe": Eng("pe", nc.tensor, mk("s_pe")),
            "act": Eng("act", nc.scalar, mk("s_act")),
            "dve": Eng("dve", nc.vector, mk("s_dve")),
            "pool": Eng("pool", nc.gpsimd, mk("s_pool")),
            "sp": Eng("sp", nc.sync, mk("s_sp")),
        }
        self.dma_sems = {"sp": [mk(f"d_sp{i}") for i in range(12)], "pool": [mk(f"d_pl{i}") for i in range(8)]}
        self.dma_cnt = {}
        self.dma_rr = {"sp": 0, "pool": 0}
        self.n_wait = 0
        self.n_ins = 0

    def _wait(self, e, tk):
        sem, val = tk
        if e.seen.get(sem.num, 0) >= val:
            return
        e.eng.wait_ge(sem, val)
        e.seen[sem.num] = val
        self.n_wait += 1

    def _deps(self, reads, writes):
        tks = []
        for r in reads:
            for res in r.ov:
                if res.w is not None:
                    tks.append((res.w, False))
                if res.excl:
                    tks.extend((t, True) for t in res.rs.values())
        for w in writes:
            for res in w.ov:
                if res.w is not None:
                    tks.append((res.w, False))
                tks.extend((t, False) for t in res.rs.values())
        return tks

    def _mark(self, tk, reads, writes):
        for r in reads:
            r.rs[tk[0].num] = tk
        for w in writes:
            w.w = tk
            w.rs = {}

    def op(self, en, fn, reads=(), writes=()):
        if self.dry:
            return
        e = self.E[en]
        for tk, rr in self._deps(reads, writes):
            if tk[0] is e.sem and (rr or en == "pe"):
                continue
            self._wait(e, tk)
        ins = fn(e.eng)
        e.cnt += 1
        ins.then_inc(e.sem, 1)
        self.n_ins += 1
        self._mark((e.sem, e.cnt), reads, writes)

    def prewait(self, en, reads=(), writes=()):
        if self.dry:
            return
        e = self.E[en]
        best = {}
        for tk, rr in self._deps(reads, writes):
            if tk[0] is e.sem:
                continue
            if tk[0].num not in best or best[tk[0].num][1] < tk[1]:
                best[tk[0].num] = tk
        for tk in best.values():
            self._wait(e, tk)

    def mm(self, mms, reads, writes):
        if self.dry:
            return
        e = self.E["pe"]
        for tk, rr in self._deps(reads, writes):
            if tk[0] is e.sem:
                continue
            self._wait(e, tk)
        ins = None
        for (o, l, r, st, sp) in mms:
            ins = e.eng.matmul(o, lhsT=l, rhs=r, start=st, stop=sp)
            self.n_ins += 1
        e.cnt += 1
        ins.then_inc(e.sem, 1)
        self._mark((e.sem, e.cnt), reads, writes)

    def dma(self, q, out, in_, reads=(), writes=()):
        if self.dry:
            return
        e = self.E[q]
        for tk, rr in self._deps(reads, writes):
            self._wait(e, tk)
        pool = self.dma_sems[q]
        i = self.dma_rr[q]
        self.dma_rr[q] = (i + 1) % len(pool)
        sem = pool[i]
        prev = self.dma_cnt.get(sem.num, 0)
        if prev:
            self._wait(e, (sem, prev))
        ins = e.eng.dma_start(out=out, in_=in_)
        ins.then_inc(sem, 16)
        self.dma_cnt[sem.num] = prev + 16
        self.n_ins += 1
        self._mark((sem, prev + 16), reads, writes)

    def finish(self, res_list):
        e = self.E["sp"]
        for r in res_list:
            if r.w is not None:
                self._wait(e, r.w)


class Region:
    def __init__(self, tile, nelem):
        self.t = tile
        self.n = nelem
        self.off = 0
        self.phase = None
        self.bufs = []

    def set_phase(self, ph):
        self.phase = ph
        self.off = 0

    def take(self, name, free_shape, dtype):
        n = int(np.prod(free_shape))
        nb = n * (2 if dtype == F32 else 1)
        self.off = (self.off + 15) // 16 * 16
        off = self.off
        self.off += nb
        assert self.off <= self.n, (name, self.off, self.n)
        v = self.t[:, off:off + nb]
        if dtype == F32:
            v = v.bitcast(F32)
        if len(free_shape) == 2:
            v = v.rearrange("p (a b) -> p a b", a=free_shape[0])
        elif len(free_shape) == 3:
            v = v.rearrange("p (a b c) -> p a b c", a=free_shape[0], b=free_shape[1])
        r = Res(name)
        for (ph, r2) in self.bufs:
            if ph != self.phase:
                r.ov.append(r2)
                r2.ov.append(r)
        self.bufs.append((self.phase, r))
        return v, r


class WStream:
    def __init__(self, k, slots, slot_res, cache_fn):
        self.k = k
        self.slots = slots
        self.res = slot_res
        self.reqs = []
        self.issued = 0
        self.pos = 0
        self.cache_fn = cache_fn
        self.cache_idx = {}
        self.cache_res = {}
        self.loaded = set()

    def get(self, loader, keep=0):
        k = self.k
        i = self.pos
        self.pos += 1
        if k.dry:
            self.reqs.append(loader)
            if loader.key not in self.cache_idx:
                self.cache_idx[loader.key] = len(self.cache_idx)
                self.cache_res[loader.key] = Res("wc%d" % len(self.cache_idx))
            return self.slots[i % NSLOT], self.res[i % NSLOT]
        assert keep < NSLOT - 1
        lim = min(len(self.reqs), i - keep + NSLOT)
        while self.issued < lim:
            j = self.issued
            s, r = self.slots[j % NSLOT], self.res[j % NSLOT]
            ld = self.reqs[j]
            ci = self.cache_idx[ld.key]
            cr = self.cache_res[ld.key]
            if ld.key not in self.loaded:
                self.loaded.add(ld.key)
                for (o, src) in ld(s):
                    k.dma("pool", o, src, reads=(), writes=(r,))
                k.dma("sp", self.cache_fn(ci)[:, 0:ld.nel], s[:, 0:ld.nel], reads=(r,), writes=(cr,))
            else:
                k.dma("pool", s[:, 0:ld.nel], self.cache_fn(ci)[:, 0:ld.nel], reads=(cr,), writes=(r,))
            self.issued += 1
        return self.slots[i % NSLOT], self.res[i % NSLOT]


def slot_view(s, kc, ncols):
    return s[:, 0:kc * ncols].rearrange("p (k n) -> p k n", k=kc)


DEBUG = False
DBG_NAMES = []


def build_program(layers, fused):
    KV_ONLY0 = 0 if len(layers) > 1 else NT
    nc = bass.Bass("TRN2", target_bir_lowering=False)
    NL = len(layers)
    dt = lambda name, shape, dtype=F32, kind="ExternalInput": nc.dram_tensor(name, list(shape), dtype, kind=kind).ap()
    I = {}
    I["xall"] = dt("xall", [D, NKEY])
    I["w_in"] = [dt(f"w_in{l}", [D, D_IN]) for l in range(NL)]
    I["w_uq"] = [dt(f"w_uq{l}", [512, 3072]) for l in range(NL)]
    I["w_ukv"] = [dt(f"w_ukv{l}", [512, 4096]) for l in range(NL)]
    I["pool_w"] = [dt(f"pool_w{l}", [4, 256, 256]) for l in range(NL)]
    I["w_a"] = [dt(f"w_branch_a{l}", [1024, D]) for l in range(NL)]
    I["w_b"] = [dt(f"w_branch_b{l}", [D, D]) for l in range(NL)]
    I["w_c"] = [dt(f"w_branch_c{l}", [1024, D]) for l in range(NL)]
    I["w_o"] = [dt(f"w_o{l}", [D, D]) for l in range(NL)]
    I["w_up"] = [dt(f"w_up{l}", [D, DFF]) for l in range(NL)]
    I["w_down"] = [dt(f"w_down{l}", [DFF, D]) for l in range(NL)]
    I["params"] = dt("params", [NL, 128, NPARAM])
    I["ropeC"] = dt("ropeC", [64, NKEY])
    I["ropeS"] = dt("ropeS", [64, NKEY])
    I["invc"] = dt("invc", [128, 4, NKEY])
    I["kbias"] = dt("kbias", [128, NKB])
    I["flag"] = dt("flag", [128, 1])
    I["tri"] = dt("tri", [128, 128], BF16)
    I["ones"] = dt("ones", [128, 128], BF16)
    I["swap"] = dt("swap", [64, 64], BF16)
    OUT = dt("outT", [D, NTOK], F32, kind="ExternalOutput")
    KTn_d = [dt(f"ktn{l}", [H, 128, NKEY], BF16, kind="Internal") for l in range(NL)]
    KTr_d = [dt(f"ktr{l}", [H, 64, NKEY], BF16, kind="Internal") for l in range(NL)]
    V_d = [dt(f"v{l}", [H, 128, NKB, 128], BF16, kind="Internal") for l in range(NL)]
    X1_d = [dt(f"x1_{l}", [D, NKEY], F32, kind="Internal") for l in range(NL - 1)]

    with contextlib.ExitStack() as es:
        k = KB(nc, es)
        sb = lambda name, shape, dtype: es.enter_context(nc.sbuf_tensor(name, list(shape), dtype))
        xT = sb("xT_sb", [128, NCD, TT], F32); r_xT = Res("xT")
        hT = sb("hT_sb", [128, NCD, TT], BF16); r_hT = Res("hT")
        ybT = sb("ybT_sb", [128, H, TT], BF16); r_ybT = Res("ybT")
        wsl = [sb(f"wslot{i}", [128, SLOT], BF16) for i in range(NSLOT)]
        r_wsl = [Res(f"wslot{i}") for i in range(NSLOT)]
        params = sb("params_sb", [128, NL, NPARAM], F32); r_par = Res("params")
        tri = sb("tri_sb", [128, 128], BF16)
        ones = sb("ones_sb", [128, 128], BF16)
        swp = sb("swap_sb", [64, 64], BF16)
        kbias = sb("kbias_sb", [128, NKB], F32)
        flag = sb("flag_sb", [128, 1], F32)
        epsb = sb("eps_sb", [128, 1], F32)
        zb = sb("zb_sb", [128, 1], F32)
        r_const = Res("consts")
        rstd = [sb(f"rstd{i}", [128, TT], F32) for i in range(3)]
        r_rstd = [Res(f"rstd{i}") for i in range(3)]
        lnv = [sb(f"lnv{i}", [128, TT], F32) for i in range(2)]
        r_lnv = [Res(f"lnv{i}") for i in range(2)]
        ropeC = sb("ropeC_sb", [64, TT], F32)
        ropeS = sb("ropeS_sb", [64, TT], F32)
        r_rope = Res("rope")
        invc = sb("invc_sb", [128, 4, TT], F32); r_invc = Res("invc")
        PT = [sb(f"pt{i}", [128, TT], BF16) for i in range(9)]
        r_PT = [Res(f"pt{i}") for i in range(9)]
        rden = sb("rden", [128, TT], F32); r_rden = Res("rden")
        gate = [sb(f"gate{i}", [128, TT], F32) for i in range(3)]
        r_gate = [Res(f"gate{i}") for i in range(3)]
        mtmp = [sb(f"mtmp{i}", [128, TT], F32) for i in range(6)]
        r_mtmp = [Res(f"mtmp{i}") for i in range(6)]
        rl = [sb(f"relu{i}", [128, TT], F32) for i in range(2)]
        r_rl = [Res(f"relu{i}") for i in range(2)]
        halo_cu = sb("halo_cu", [128, 8, 2], BF16); r_hcu = Res("halo_cu")
        halo_pin = sb("halo_pin", [128, 8, 16], F32); r_hpin = Res("halo_pin")
        U1N = 24 * 1024
        U2N = 28 * 1024
        U1 = Region(sb("U1", [128, U1N], BF16), U1N)
        U2 = Region(sb("U2", [128, U2N], BF16), U2N)
        U1.set_phase("kv")
        kvl, r_kvl = U1.take("kvl", [5, TT], F32)
        sq5, r_sq5 = U1.take("sq5", [5, TT], BF16)
        kvnT, r_kvnT = U1.take("kvnT", [4, TT], BF16)
        KTn_all, r_KTn = U1.take("KTn_all", [H, TT], BF16)
        KTr_all, r_KTr = U1.take("KTr_all", [H, TT], BF16)
        Vsb, r_Vsb = U1.take("Vsb", [3, D], BF16)
        krg, r_krg = U1.take("krg", [TT], F32)
        kx, r_kx = U1.take("kx", [2, TT], F32)
        kxb, r_kxb = U1.take("kxb", [TT], BF16)
        sqh, r_sqh = U1.take("sqh", [2, 2, TT], BF16)
        U1.set_phase("mix")
        usb, r_usb = U1.take("usb", [2, TT], F32)
        cu, r_cu = U1.take("cu", [8, TT + 2], BF16)
        bsb, r_bsb = U1.take("bsb", [8, TT], BF16)
        ctmp, r_ctmp = U1.take("ctmp", [2, TT], F32)
        yaT, r_yaT = U1.take("yaT", [8, TT], BF16)
        pin, r_pin = U1.take("pin", [8, TT + 16], F32)
        pw1, r_pw1 = U1.take("pw1", [2, TT + 16], F32)
        pw2, r_pw2 = U1.take("pw2", [2, TT + 16], F32)
        pooled, r_pooled = U1.take("pooled", [8, TT], BF16)
        ycT, r_ycT = U1.take("ycT", [8, TT], BF16)
        merged, r_merged = U1.take("merged", [NCD, TT], BF16)
        U2.set_phase("att")
        qlat, r_qlat = U2.take("qlat", [4, TT], F32)
        sq4, r_sq4 = U2.take("sq4", [4, TT], BF16)
        qnT, r_qnT = U2.take("qnT", [4, TT], BF16)
        QTn, r_QTn = U2.take("QTn", [H, TT], BF16)
        QTr, r_QTr = U2.take("QTr", [H, TT], BF16)
        qx, r_qx = U2.take("qx", [2, TT], F32)
        qxb, r_qxb = U2.take("qxb", [TT], BF16)
        sqq, r_sqq = U2.take("sqq", [2, 2, TT], BF16)
        Kn_s = []; Kr_s = []; V_s = []
        for i in range(2):
            a, ra = U2.take(f"Kn_s{i}", [SEGK], BF16)
            b_, rb = U2.take(f"Kr_s{i}", [SEGK], BF16)
            c_, rc = U2.take(f"V_s{i}", [17, 128], BF16)
            Kn_s.append((a, ra)); Kr_s.append((b_, rb)); V_s.append((c_, rc))
        U2.set_phase("mlp")
        sq16, r_sq16 = U2.take("sq16", [NCD, TT], BF16)
        act, r_act = U2.take("act", [64, TT], BF16)
        banks = [es.enter_context(nc.psum_tensor(f"bank{i}", [128, 512], F32)) for i in range(8)]
        r_bank = [Res(f"bank{i}", excl=True) for i in range(8)]
        ring = {"set": list(range(8)), "p": 0}

        def ps_next():
            i = ring["set"][ring["p"] % len(ring["set"])]
            ring["p"] += 1
            return banks[i], r_bank[i]

        WC_d = [dt(f"wcache{i}", [150, 128, SLOT], BF16, kind="Internal") for i in range(NL)]
        ws = WStream(k, wsl, r_wsl, lambda ci: WC_d[ci // 150][ci % 150])
        cc_sem = es.enter_context(nc.semaphore("cc_sem"))

        def dump(name, ap, shape, dtype, r):
            if not DEBUG or k.dry:
                return
            dram = nc.dram_tensor("dbg_" + name, list(shape), dtype, kind="ExternalOutput").ap()
            DBG_NAMES.append("dbg_" + name)
            k.dma("sp", dram, ap, reads=(r,), writes=(Res("dbg_" + name),))

        r_kv_d = [[Res(f"kvd{l}_{t}") for t in range(2 * NT)] for l in range(NL)]
        r_x1 = [[Res(f"x1_{l}_{t}") for t in range(2 * NT)] for l in range(NL)]
        r_out = [Res(f"out_{t}") for t in range(NT)]

        def V(fn, **kw):
            return lambda e: fn(e, **kw)

        def load_consts():
            k.dma("sp", params[:], I["params"].rearrange("l p c -> p l c"), writes=(r_par,))
            k.dma("sp", tri[:], I["tri"], writes=(r_const,))
            k.dma("sp", ones[:], I["ones"], writes=(r_const,))
            k.dma("sp", swp[:], I["swap"], writes=(r_const,))
            k.dma("sp", kbias[:], I["kbias"], writes=(r_const,))
            k.dma("sp", flag[:], I["flag"], writes=(r_const,))
            k.op("dve", lambda e: e.memset(epsb[:], EPS), writes=(r_const,))
            k.op("dve", lambda e: e.memset(zb[:], 0.0), writes=(r_const,))

        def pcol(li, c):
            return params[:, li, c:c + 1]

        def rstd_from_ps(ps, r_ps, n, dim, slot):
            lv, r_lv = lnv[slot % 2], r_lnv[slot % 2]
            rs, r_rs = rstd[slot], r_rstd[slot]
            k.op("act", lambda e: e.activation(out=lv[:, :n], in_=ps[:, :n], func=AF.Ln, bias=epsb[:], scale=1.0 / dim),
                 reads=(r_ps, r_const), writes=(r_lv,))
            k.op("act", lambda e: e.activation(out=rs[:, :n], in_=lv[:, :n], func=AF.Exp, scale=-0.5),
                 reads=(r_lv,), writes=(r_rs,))
            return rs, r_rs

        def norm_fm(src, r_src, nch, n, sq, r_sq, li, gcol, dst, r_dst, dim, slot):
            for c in range(nch):
                k.op("act", lambda e, c=c: e.activation(out=sq[:, c, :n], in_=src[:, c, :n], func=AF.Square),
                     reads=(r_src,), writes=(r_sq,))
            ps, r_ps = ps_next()
            k.mm([(ps[:, :n], ones[:, :], sq[:, c, :n], c == 0, c == nch - 1) for c in range(nch)],
                 reads=(r_sq, r_const), writes=(r_ps,))
            rs, r_rs = rstd_from_ps(ps, r_ps, n, dim, slot)
            for c in range(nch):
                k.op("dve", lambda e, c=c: e.scalar_tensor_tensor(out=dst[:, c, :n], in0=src[:, c, :n], scalar=pcol(li, gcol + c),
                                                                  in1=rs[:, :n], op0=ALU.mult, op1=ALU.mult),
                     reads=(r_src, r_rs, r_par), writes=(r_dst,))

        def w_in_loader(li, c0, ncols):
            def f(s):
                return [(slot_view(s, NCD, ncols), I["w_in"][li][:, c0:c0 + ncols].rearrange("(k p) n -> p k n", p=128))]
            f.key = ("w_in", li, c0, ncols)
            f.nel = NCD * ncols
            return f

        def gen_loader(ap2d, kc, ncols, key):
            def f(s):
                return [(slot_view(s, kc, ncols), ap2d.rearrange("(k p) n -> p k n", p=128))]
            f.key = key
            f.nel = kc * ncols
            return f

        def rope_apply(xps, r_xps, n, li, gcolr, x_f, r_x_f, xb, r_xb, dst_fn):
            k.op("act", lambda e: e.mul(out=xb[0:64, :n], in_=xps[0:64, :n], mul=pcol(li, gcolr)[0:64, :]),
                 reads=(r_xps, r_par), writes=(r_xb,))
            k.op("dve", lambda e: e.scalar_tensor_tensor(out=x_f[0:64, 0, :n], in0=xps[0:64, :n], scalar=pcol(li, gcolr)[0:64, :],
                                                         in1=ropeC[0:64, :n], op0=ALU.mult, op1=ALU.mult),
                 reads=(r_xps, r_par, r_rope), writes=(r_x_f,))
            ps2, r_ps2 = ps_next()
            k.mm([(ps2[0:64, :n], swp[:, :], xb[0:64, :n], True, True)], reads=(r_xb, r_const), writes=(r_ps2,))
            k.op("dve", lambda e: e.tensor_tensor(out=x_f[0:64, 1, :n], in0=ps2[0:64, :n], in1=ropeS[0:64, :n], op=ALU.mult),
                 reads=(r_ps2, r_rope, r_x_f), writes=(r_x_f,))
            k.op("pool", lambda e: e.tensor_tensor(out=x_f[0:64, 0, :n], in0=x_f[0:64, 0, :n], in1=x_f[0:64, 1, :n], op=ALU.add),
                 reads=(r_x_f,), writes=(r_x_f,))

        def load_rope(key0, n):
            k.dma("sp", ropeC[:, :n], I["ropeC"][:, key0:key0 + n], writes=(r_rope,))
            k.dma("sp", ropeS[:, :n], I["ropeS"][:, key0:key0 + n], writes=(r_rope,))

        def kv_path(li, n, key0, r_kvdst):
            for g in range(2):
                s, r_s = ws.get(w_in_loader(li, C_KVL + 256 * g, 256))
                sv = slot_view(s, NCD, 256)
                for m in range(2):
                    ps, r_ps = ps_next()
                    k.mm([(ps[:, :n], sv[:, kc, m * 128:(m + 1) * 128], hT[:, kc, :n], kc == 0, kc == NCD - 1) for kc in range(NCD)],
                         reads=(r_s, r_hT), writes=(r_ps,))
                    c = 2 * g + m
                    k.op("dve", lambda e, c=c, ps=ps: e.tensor_copy(out=kvl[:, c, :n], in_=ps[:, :n]), reads=(r_ps,), writes=(r_kvl,))
            s, r_s = ws.get(w_in_loader(li, C_KR, 64))
            sv = slot_view(s, NCD, 64)
            psr, r_psr = ps_next()
            k.mm([(psr[0:64, :n], sv[:, kc, :], hT[:, kc, :n], kc == 0, kc == NCD - 1) for kc in range(NCD)],
                 reads=(r_s, r_hT), writes=(r_psr,))
            k.op("act", lambda e: e.activation(out=sq5[0:64, 4, :n], in_=psr[0:64, :n], func=AF.Square), reads=(r_psr,), writes=(r_sq5,))
            rope_apply(psr, r_psr, n, li, P_KR, kx, r_kx, kxb, r_kxb, None)
            k.op("pool", lambda e: e.tensor_copy(out=krg[0:64, :n], in_=kx[0:64, 0, :n]), reads=(r_kx,), writes=(r_krg,))
            norm_fm(kvl, r_kvl, 4, n, sq5, r_sq5, li, P_KVLN, kvnT, r_kvnT, 512.0, 2)
            ntb = (n + 127) // 128
            for hg in range(4):
                s, r_s = ws.get(gen_loader(I["w_ukv"][li][:, 1024 * hg:1024 * (hg + 1)], 4, 1024, ("w_ukv", li, hg)))
                sv = slot_view(s, 4, 1024)
                for hh in range(4):
                    h = 4 * hg + hh
                    ps, r_ps = ps_next()
                    k.mm([(ps[:, :n], sv[:, kc, 256 * hh:256 * hh + 128], kvnT[:, kc, :n], kc == 0, kc == 3) for kc in range(4)],
                         reads=(r_s, r_kvnT), writes=(r_ps,))
                    sq, r_sq = sqh[:, h % 2, 0, :], r_sqh
                    k.op("act", lambda e, ps=ps, sq=sq: e.activation(out=sq[:, :n], in_=ps[:, :n], func=AF.Square),
                         reads=(r_ps,), writes=(r_sq,))
                    pss, r_pss = ps_next()
                    k.mm([(pss[:, :n], ones[:, :], sq[:, :n], True, False),
                          (pss[:, :n], ones[0:64, :], sq5[0:64, 4, :n], False, True)],
                         reads=(r_sq, r_sq5, r_const), writes=(r_pss,))
                    rs, r_rs = rstd_from_ps(pss, r_pss, n, 192.0, h % 2)
                    k.op("dve", lambda e, ps=ps, rs=rs, h=h: e.scalar_tensor_tensor(out=KTn_all[:, h, :n], in0=ps[:, :n], scalar=pcol(li, P_KN),
                                                                                   in1=rs[:, :n], op0=ALU.mult, op1=ALU.mult),
                         reads=(r_ps, r_rs, r_par), writes=(r_KTn,))
                    k.op("pool", lambda e, rs=rs, h=h: e.tensor_tensor(out=KTr_all[0:64, h, :n], in0=krg[0:64, :n], in1=rs[0:64, :n], op=ALU.mult),
                         reads=(r_krg, r_rs), writes=(r_KTr,))
                for tb in range(ntb):
                    m = min(128, n - 128 * tb)
                    ps, r_ps = ps_next()
                    mms = []
                    for hh in range(4):
                        for kc in range(4):
                            mms.append((ps[0:m, 128 * hh:128 * (hh + 1)], kvnT[:, kc, 128 * tb:128 * tb + m],
                                        sv[:, kc, 256 * hh + 128:256 * hh + 256], kc == 0, kc == 3))
                    k.mm(mms, reads=(r_s, r_kvnT), writes=(r_ps,))
                    k.op("act", lambda e, ps=ps, m=m, tb=tb, hg=hg: e.activation(out=Vsb[0:m, tb, 512 * hg:512 * (hg + 1)], in_=ps[0:m, 0:512], func=AF.Copy),
                         reads=(r_ps,), writes=(r_Vsb,))
            k.dma("sp", KTn_d[li][:, :, key0:key0 + n].rearrange("h p t -> p h t"), KTn_all[:, :, :n], reads=(r_KTn,), writes=(r_kvdst,))
            k.dma("sp", KTr_d[li][:, :, key0:key0 + n].rearrange("h p t -> p h t"), KTr_all[0:64, :, :n], reads=(r_KTr,), writes=(r_kvdst,))
            cuts = sorted(set([0, n] + [128 * i for i in range(1, ntb)] + [g * 128 - key0 for g in range(key0 // 128 + 1, (key0 + n) // 128 + 1) if 0 < g * 128 - key0 < n]))
            for a_, b_ in zip(cuts[:-1], cuts[1:]):
                tb, lp0 = divmod(a_, 128)
                gb, gp0 = divmod(key0 + a_, 128)
                np_ = b_ - a_
                k.dma("sp", V_d[li][:, gp0:gp0 + np_, gb, :].rearrange("h p d -> p h d"),
                      Vsb[lp0:lp0 + np_, tb, :].rearrange("p (h d) -> p h d", h=H), reads=(r_Vsb,), writes=(r_kvdst,))

        def mixer_inputs(li, n, c0, with_b):
            nn = n - c0
            for pr in range(4):
                su, r_su = ws.get(w_in_loader(li, C_U + 256 * pr, 256))
                sc, r_sc = ws.get(w_in_loader(li, C_C + 256 * pr, 256), keep=1)
                for m in range(2):
                    ch = 2 * pr + m
                    ps, r_ps = ps_next()
                    svu = slot_view(su, NCD, 256)
                    k.mm([(ps[:, :nn], svu[:, kc, m * 128:(m + 1) * 128], hT[:, kc, c0:n], kc == 0, kc == NCD - 1) for kc in range(NCD)],
                         reads=(r_su, r_hT), writes=(r_ps,))
                    ub = usb[:, ch % 2, :]
                    k.op("act", lambda e, ps=ps, ub=ub: e.activation(out=ub[:, :nn], in_=ps[:, :nn], func=AF.Copy), reads=(r_ps,), writes=(r_usb,))
                    ps2, r_ps2 = ps_next()
                    svc = slot_view(sc, NCD, 256)
                    k.mm([(ps2[:, :nn], svc[:, kc, m * 128:(m + 1) * 128], hT[:, kc, c0:n], kc == 0, kc == NCD - 1) for kc in range(NCD)],
                         reads=(r_sc, r_hT), writes=(r_ps2,))
                    k.op("dve", lambda e, ps2=ps2, ub=ub, ch=ch: e.tensor_tensor(out=cu[:, ch, 2 + c0:2 + n], in0=ps2[:, :nn], in1=ub[:, :nn], op=ALU.mult),
                         reads=(r_ps2, r_usb), writes=(r_cu,))
                if with_b:
                    sbb, r_sbb = ws.get(w_in_loader(li, C_B + 256 * pr, 256))
                    svb = slot_view(sbb, NCD, 256)
                    for m in range(2):
                        ch = 2 * pr + m
                        ps3, r_ps3 = ps_next()
                        k.mm([(ps3[:, :nn], svb[:, kc, m * 128:(m + 1) * 128], hT[:, kc, c0:n], kc == 0, kc == NCD - 1) for kc in range(NCD)],
                             reads=(r_sbb, r_hT), writes=(r_ps3,))
                        k.op("act", lambda e, ps3=ps3, ch=ch: e.activation(out=bsb[:, ch, c0:n], in_=ps3[:, :nn], func=AF.Copy),
                             reads=(r_ps3,), writes=(r_bsb,))
            for pr in range(4):
                sp_, r_sp = ws.get(w_in_loader(li, C_POOL + 256 * pr, 256))
                svp = slot_view(sp_, NCD, 256)
                for m in range(2):
                    ch = 2 * pr + m
                    ps, r_ps = ps_next()
                    k.mm([(ps[:, :nn], svp[:, kc, m * 128:(m + 1) * 128], hT[:, kc, c0:n], kc == 0, kc == NCD - 1) for kc in range(NCD)],
                         reads=(r_sp, r_hT), writes=(r_ps,))
                    k.op("act", lambda e, ps=ps, ch=ch: e.activation(out=pin[:, ch, 16 + c0:16 + n], in_=ps[:, :nn], func=AF.Copy),
                         reads=(r_ps,), writes=(r_pin,))

        def save_halo(n):
            k.op("pool", lambda e: e.tensor_copy(out=halo_cu[:, :, :], in_=cu[:, :, n:n + 2]), reads=(r_cu,), writes=(r_hcu,))
            k.op("pool", lambda e: e.tensor_copy(out=halo_pin[:, :, 1:16], in_=pin[:, :, n + 1:n + 16]), reads=(r_pin,), writes=(r_hpin,))

        def restore_halo(scale_flag):
            if scale_flag:
                k.op("pool", lambda e: e.tensor_scalar(out=cu[:, :, 0:2], in0=halo_cu[:, :, :], scalar1=flag[:, 0:1], scalar2=None, op0=ALU.mult),
                     reads=(r_hcu, r_const), writes=(r_cu,))
                k.op("pool", lambda e: e.tensor_scalar(out=pin[:, :, 1:16], in0=halo_pin[:, :, 1:16], scalar1=flag[:, 0:1], scalar2=None, op0=ALU.mult),
                     reads=(r_hpin, r_const), writes=(r_pin,))
            else:
                k.op("pool", lambda e: e.tensor_copy(out=cu[:, :, 0:2], in_=halo_cu[:, :, :]), reads=(r_hcu,), writes=(r_cu,))
                k.op("pool", lambda e: e.tensor_copy(out=pin[:, :, 1:16], in_=halo_pin[:, :, 1:16]), reads=(r_hpin,), writes=(r_pin,))

        def q_path(li, n):
            for g in range(2):
                s, r_s = ws.get(w_in_loader(li, C_QL + 256 * g, 256))
                sv = slot_view(s, NCD, 256)
                for m in range(2):
                    ps, r_ps = ps_next()
                    k.mm([(ps[:, :n], sv[:, kc, m * 128:(m + 1) * 128], hT[:, kc, :n], kc == 0, kc == NCD - 1) for kc in range(NCD)],
                         reads=(r_s, r_hT), writes=(r_ps,))
                    c = 2 * g + m
                    k.op("dve", lambda e, c=c, ps=ps: e.tensor_copy(out=qlat[:, c, :n], in_=ps[:, :n]), reads=(r_ps,), writes=(r_qlat,))
            norm_fm(qlat, r_qlat, 4, n, sq4, r_sq4, li, P_QLN, qnT, r_qnT, 512.0, 2)
            for hg in range(4):
                s, r_s = ws.get(gen_loader(I["w_uq"][li][:, 768 * hg:768 * (hg + 1)], 4, 768, ("w_uq", li, hg)))
                sv = slot_view(s, 4, 768)
                for hh in range(4):
                    h = 4 * hg + hh
                    psn, r_psn = ps_next()
                    k.mm([(psn[:, :n], sv[:, kc, 192 * hh:192 * hh + 128], qnT[:, kc, :n], kc == 0, kc == 3) for kc in range(4)],
                         reads=(r_s, r_qnT), writes=(r_psn,))
                    psr, r_psr = ps_next()
                    k.mm([(psr[0:64, :n], sv[:, kc, 192 * hh + 128:192 * hh + 192], qnT[:, kc, :n], kc == 0, kc == 3) for kc in range(4)],
                         reads=(r_s, r_qnT), writes=(r_psr,))
                    sqn = sqq[:, h % 2, 0, :]
                    sqr = sqq[:, h % 2, 1, :]
                    k.op("act", lambda e, psn=psn, sqn=sqn: e.activation(out=sqn[:, :n], in_=psn[:, :n], func=AF.Square), reads=(r_psn,), writes=(r_sqq,))
                    k.op("act", lambda e, psr=psr, sqr=sqr: e.activation(out=sqr[0:64, :n], in_=psr[0:64, :n], func=AF.Square), reads=(r_psr,), writes=(r_sqq,))
                    pss, r_pss = ps_next()
                    k.mm([(pss[:, :n], ones[:, :], sqn[:, :n], True, False), (pss[:, :n], ones[0:64, :], sqr[0:64, :n], False, True)],
                         reads=(r_sqq, r_const), writes=(r_pss,))
                    rs, r_rs = rstd_from_ps(pss, r_pss, n, 192.0, h % 2)
                    k.op("dve", lambda e, psn=psn, rs=rs, h=h: e.scalar_tensor_tensor(out=QTn[:, h, :n], in0=psn[:, :n], scalar=pcol(li, P_QN),
                                                                                     in1=rs[:, :n], op0=ALU.mult, op1=ALU.mult),
                         reads=(r_psn, r_rs, r_par), writes=(r_QTn,))
                    rope_apply(psr, r_psr, n, li, P_QR, qx, r_qx, qxb, r_qxb, None)
                    k.op("pool", lambda e, rs=rs, h=h: e.tensor_tensor(out=QTr[0:64, h, :n], in0=qx[0:64, 0, :n], in1=rs[0:64, :n], op=ALU.mult),
                         reads=(r_qx, r_rs), writes=(r_QTr,))

        def attention(li, t, n):
            qs = t * TT
            qe = qs + n
            nkb = (qe + 127) // 128
            segs = [(0, min(nkb, SEG0_KB))]
            if nkb > SEG0_KB:
                segs.append((SEG0_KB, nkb))
            kv_reads = tuple(r_kv_d[li][0:t + 1])
            ring["set"] = [0, 1, 2, 3]
            loads = [(h, si) for h in range(H) for si in range(len(segs))]

            def issue_load(idx):
                h, si = loads[idx]
                b0, b1 = segs[si]
                k0 = 128 * b0
                k1 = min(128 * b1, qe)
                nk = k1 - k0
                (kn, r_kn), (kr, r_kr), (vv, r_vv) = Kn_s[idx % 2], Kr_s[idx % 2], V_s[idx % 2]
                k.dma("sp", kn[:, 0:nk], KTn_d[li][h, :, k0:k1], reads=kv_reads, writes=(r_kn,))
                k.dma("sp", kr[0:64, 0:nk], KTr_d[li][h, :, k0:k1], reads=kv_reads, writes=(r_kr,))
                nb_ = (nk + 127) // 128
                k.dma("sp", vv[:, 0:nb_, :], V_d[li][h, :, b0:b0 + nb_, :], reads=kv_reads, writes=(r_vv,))

            items = []
            for idx, (h, si) in enumerate(loads):
                b0, b1 = segs[si]
                for j in range(b0, b1):
                    items.append((idx, h, si, j))
            last_item_of_load = {}
            for n_, it in enumerate(items):
                last_item_of_load[it[0]] = n_
            G = 3
            issue_load(0)
            if len(loads) > 1:
                issue_load(1)
            stash = {}
            O, r_O = banks[6], r_bank[6]
            Dn, r_Dn = banks[7], r_bank[7]
            ring["set"] = [0, 1, 2, 3, 4, 5]

            def stage_s(n_, st, r_st):
                idx, h, si, j = items[n_]
                b0, b1 = segs[si]
                (kn, r_kn), (kr, r_kr), (vv, r_vv) = Kn_s[idx % 2], Kr_s[idx % 2], V_s[idx % 2]
                kp = min(128, qe - 128 * j, NKEY - 128 * j)
                kl = 128 * (j - b0)
                cst = max(0, 128 * j - qs)
                k.mm([(st[0:kp, cst:n], kn[:, kl:kl + kp], QTn[:, h, cst:n], True, False),
                      (st[0:kp, cst:n], kr[0:64, kl:kl + kp], QTr[0:64, h, cst:n], False, True)],
                     reads=(r_kn, r_kr, r_QTn, r_QTr), writes=(r_st,))
                pt_, r_pt = PT[n_ % 9], r_PT[n_ % 9]
                k.op("act", lambda e: e.activation(out=pt_[0:kp, cst:n], in_=st[0:kp, cst:n], func=AF.Exp,
                                                   bias=(kbias[0:kp, j:j + 1] if t >= NT else zb[0:kp, 0:1]), scale=SCALE),
                     reads=(r_st, r_const), writes=(r_pt,))
                dend = min(n, 128 * j + 128 - qs)
                if 128 * j + 128 > qs:
                    c0 = qs + cst - 128 * j
                    k.op("pool", lambda e: e.tensor_tensor(out=pt_[0:kp, cst:dend], in0=pt_[0:kp, cst:dend], in1=tri[0:kp, c0:c0 + dend - cst], op=ALU.mult),
                         reads=(r_pt, r_const), writes=(r_pt,))
                stash[n_] = (kp, cst, pt_, r_pt)

            def stage_pv(n_):
                idx, h, si, j = items[n_]
                b0, b1 = segs[si]
                (kn, r_kn), (kr, r_kr), (vv, r_vv) = Kn_s[idx % 2], Kr_s[idx % 2], V_s[idx % 2]
                kp, cst, pt_, r_pt = stash.pop(n_)
                first = (j == 0)
                last = (j == nkb - 1)
                k.mm([(O[:, cst:n], vv[0:kp, j - b0, :], pt_[0:kp, cst:n], first, last),
                      (Dn[:, cst:n], ones[0:kp, :], pt_[0:kp, cst:n], first, last)], reads=(r_vv, r_pt, r_const), writes=(r_O, r_Dn))
                if last:
                    k.op("dve", lambda e: e.reciprocal(out=rden[:, :n], in_=Dn[:, :n]), reads=(r_Dn,), writes=(r_rden,))
                    k.op("dve", lambda e: e.tensor_tensor(out=ybT[:, h, :n], in0=O[:, :n], in1=rden[:, :n], op=ALU.mult),
                         reads=(r_O, r_rden), writes=(r_ybT,))
                if last_item_of_load[idx] == n_ and idx + 2 < len(loads):
                    issue_load(idx + 2)

            ngrp = (len(items) + G - 1) // G
            for g_ in range(ngrp + 1):
                if g_ < ngrp:
                    grp = list(range(g_ * G, min(len(items), (g_ + 1) * G)))
                    bks = [ps_next() for _ in grp]
                    k.prewait("pe", reads=(), writes=tuple(rb for (_, rb) in bks))
                    for n_, (st, r_st) in zip(grp, bks):
                        stage_s(n_, st, r_st)
                if g_ >= 1:
                    grp = list(range((g_ - 1) * G, min(len(items), g_ * G)))
                    k.prewait("pe", reads=tuple(stash[n_][3] for n_ in grp), writes=())
                    for n_ in grp:
                        stage_pv(n_)
            ring["set"] = list(range(8))

        def conv_pool(li, t, n):
            k.dma("sp", invc[:, :, :n], I["invc"][:, :, t * TT:t * TT + n], writes=(r_invc,))
            for ch in range(8):
                tmp = ctmp[:, ch % 2, :]
                k.op("pool", lambda e, ch=ch, tmp=tmp: e.tensor_scalar(out=tmp[:, :n], in0=cu[:, ch, 0:n], scalar1=pcol(li, P_CW + 3 * ch), scalar2=None, op0=ALU.mult),
                     reads=(r_cu, r_par), writes=(r_ctmp,))
                k.op("dve", lambda e, ch=ch, tmp=tmp: e.scalar_tensor_tensor(out=tmp[:, :n], in0=cu[:, ch, 1:n + 1], scalar=pcol(li, P_CW + 3 * ch + 1),
                                                                             in1=tmp[:, :n], op0=ALU.mult, op1=ALU.add),
                     reads=(r_cu, r_par, r_ctmp), writes=(r_ctmp,))
                k.op("dve", lambda e, ch=ch, tmp=tmp: e.scalar_tensor_tensor(out=tmp[:, :n], in0=cu[:, ch, 2:n + 2], scalar=pcol(li, P_CW + 3 * ch + 2),
                                                                             in1=tmp[:, :n], op0=ALU.mult, op1=ALU.add),
                     reads=(r_cu, r_par, r_ctmp), writes=(r_ctmp,))
                k.op("dve", lambda e, ch=ch, tmp=tmp: e.tensor_tensor(out=yaT[:, ch, :n], in0=tmp[:, :n], in1=bsb[:, ch, :n], op=ALU.mult),
                     reads=(r_ctmp, r_bsb), writes=(r_yaT,))
            for g in range(4):
                src = pin[:, 2 * g:2 * g + 2, :]
                cur = src
                r_cur = r_pin
                lo = 1
                bufs = [(pw1, r_pw1), (pw2, r_pw2)]
                for step in range(g + 1):
                    sh = 1 << step
                    dstb, r_dst = bufs[step % 2]
                    lo2 = lo + sh
                    k.op("pool", lambda e, cur=cur, dstb=dstb, lo2=lo2, sh=sh: e.tensor_tensor(out=dstb[:, :, lo2:n + 16], in0=cur[:, :, lo2:n + 16],
                                                                                             in1=cur[:, :, lo2 - sh:n + 16 - sh], op=ALU.add),
                         reads=(r_cur,), writes=(r_dst,))
                    cur, r_cur, lo = dstb, r_dst, lo2
                for m in range(2):
                    ch = 2 * g + m
                    tmp = ctmp[:, m, :]
                    k.op("dve", lambda e, cur=cur, m=m, g=g, tmp=tmp: e.tensor_tensor(out=tmp[:, :n], in0=cur[:, m, 16:16 + n], in1=invc[:, g, :n], op=ALU.mult),
                         reads=(r_cur, r_invc), writes=(r_ctmp,))
                    k.op("dve", lambda e, ch=ch, tmp=tmp: e.tensor_tensor(out=pooled[:, ch, :n], in0=tmp[:, :n], in1=pin[:, ch, 16:16 + n], op=ALU.subtract),
                         reads=(r_ctmp, r_pin), writes=(r_pooled,))
            def pw_loader(s):
                return [(s[:, 0:2048].rearrange("p (g k n) -> p g k n", g=4, k=2), I["pool_w"][li].rearrange("g (k p) n -> p g k n", p=128))]
            pw_loader.key = ("pool_w", li)
            pw_loader.nel = 2048
            s, r_s = ws.get(pw_loader)
            sv = s[:, 0:2048].rearrange("p (g k n) -> p g k n", g=4, k=2)
            for g in range(4):
                for m in range(2):
                    ps, r_ps = ps_next()
                    k.mm([(ps[:, :n], sv[:, g, kc, m * 128:(m + 1) * 128], pooled[:, 2 * g + kc, :n], kc == 0, kc == 1) for kc in range(2)],
                         reads=(r_s, r_pooled), writes=(r_ps,))
                    ch = 2 * g + m
                    k.op("act", lambda e, ps=ps, ch=ch: e.mul(out=ycT[:, ch, :n], in_=ps[:, :n], mul=pcol(li, P_PS + ch)),
                         reads=(r_ps, r_par), writes=(r_ycT,))

        def merge_and_wo(li, n):
            for jj in range(8):
                for i in range(3):
                    sg, r_sg = ws.get(w_in_loader(li, C_GATE + 2048 * i + 256 * jj, 256))
                    svg = slot_view(sg, NCD, 256)
                    wsrc, nk, src, r_src = [(I["w_a"], 8, yaT, r_yaT), (I["w_b"], 16, ybT, r_ybT), (I["w_c"], 8, ycT, r_ycT)][i]
                    sw_, r_sw = ws.get(gen_loader(wsrc[li][:, 256 * jj:256 * (jj + 1)], nk, 256, ("w_br", li, i, jj)), keep=1)
                    svb = slot_view(sw_, nk, 256)
                    for m in range(2):
                        ps, r_ps = ps_next()
                        k.mm([(ps[:, :n], svg[:, kc, m * 128:(m + 1) * 128], hT[:, kc, :n], kc == 0, kc == NCD - 1) for kc in range(NCD)],
                             reads=(r_sg, r_hT), writes=(r_ps,))
                        k.op("act", lambda e, ps=ps, i=i: e.activation(out=gate[i][:, :n], in_=ps[:, :n], func=AF.Sigmoid), reads=(r_ps,), writes=(r_gate[i],))
                        ps2, r_ps2 = ps_next()
                        k.mm([(ps2[:, :n], svb[:, kc, m * 128:(m + 1) * 128], src[:, kc, :n], kc == 0, kc == nk - 1) for kc in range(nk)],
                             reads=(r_sw, r_src), writes=(r_ps2,))
                        mt, r_mt = mtmp[2 * i + m], r_mtmp[2 * i + m]
                        k.op("dve", lambda e, ps2=ps2, i=i, mt=mt: e.tensor_tensor(out=mt[:, :n], in0=ps2[:, :n], in1=gate[i][:, :n], op=ALU.mult),
                             reads=(r_ps2, r_gate[i]), writes=(r_mt,))
                for m in range(2):
                    oc = 2 * jj + m
                    k.op("pool", lambda e, m=m: e.tensor_tensor(out=mtmp[m][:, :n], in0=mtmp[m][:, :n], in1=mtmp[2 + m][:, :n], op=ALU.add),
                         reads=(r_mtmp[m], r_mtmp[2 + m]), writes=(r_mtmp[m],))
                    k.op("pool", lambda e, oc=oc, m=m: e.tensor_tensor(out=merged[:, oc, :n], in0=mtmp[m][:, :n], in1=mtmp[4 + m][:, :n], op=ALU.add),
                         reads=(r_mtmp[m], r_mtmp[4 + m]), writes=(r_merged,))
            for jj in range(8):
                s, r_s = ws.get(gen_loader(I["w_o"][li][:, 256 * jj:256 * (jj + 1)], 16, 256, ("w_o", li, jj)))
                sv = slot_view(s, 16, 256)
                for m in range(2):
                    oc = 2 * jj + m
                    ps, r_ps = ps_next()
                    k.mm([(ps[:, :n], sv[:, kc, m * 128:(m + 1) * 128], merged[:, kc, :n], kc == 0, kc == NCD - 1) for kc in range(NCD)],
                         reads=(r_s, r_merged), writes=(r_ps,))
                    k.op("dve", lambda e, ps=ps, oc=oc: e.tensor_tensor(out=xT[:, oc, :n], in0=ps[:, :n], in1=xT[:, oc, :n], op=ALU.add),
                         reads=(r_ps, r_xT), writes=(r_xT,))

        def mlp(li, n):
            norm_fm(xT, r_xT, NCD, n, sq16, r_sq16, li, P_MN, hT, r_hT, float(D), 2)
            for jj in range(32):
                s, r_s = ws.get(gen_loader(I["w_up"][li][:, 256 * jj:256 * (jj + 1)], 16, 256, ("w_up", li, jj)))
                sv = slot_view(s, 16, 256)
                for m in range(2):
                    oc = 2 * jj + m
                    ps, r_ps = ps_next()
                    k.mm([(ps[:, :n], sv[:, kc, m * 128:(m + 1) * 128], hT[:, kc, :n], kc == 0, kc == NCD - 1) for kc in range(NCD)],
                         reads=(r_s, r_hT), writes=(r_ps,))
                    rr, r_rr = rl[oc % 2], r_rl[oc % 2]
                    k.op("act", lambda e, ps=ps, rr=rr: e.activation(out=rr[:, :n], in_=ps[:, :n], func=AF.Relu), reads=(r_ps,), writes=(r_rr,))
                    k.op("pool", lambda e, rr=rr, oc=oc: e.tensor_tensor(out=act[:, oc, :n], in0=rr[:, :n], in1=rr[:, :n], op=ALU.mult),
                         reads=(r_rr,), writes=(r_act,))
            for oc in range(NCD):
                ps, r_ps = ps_next()
                for half in range(2):
                    s, r_s = ws.get(gen_loader(I["w_down"][li][4096 * half:4096 * (half + 1), 128 * oc:128 * (oc + 1)], 32, 128, ("w_down", li, oc, half)))
                    sv = slot_view(s, 32, 128)
                    k.mm([(ps[:, :n], sv[:, kc, :], act[:, 32 * half + kc, :n], (half == 0 and kc == 0), (half == 1 and kc == 31)) for kc in range(32)],
                         reads=(r_s, r_act), writes=(r_ps,))
                k.op("dve", lambda e, ps=ps, oc=oc: e.tensor_tensor(out=xT[:, oc, :n], in0=ps[:, :n], in1=xT[:, oc, :n], op=ALU.add),
                     reads=(r_ps, r_xT), writes=(r_xT,))

        def layer(li, src, r_src_list, kv_only_tiles, dst_fn):
            k.op("pool", lambda e: e.memset(halo_cu[:], 0.0), writes=(r_hcu,))
            k.op("pool", lambda e: e.memset(halo_pin[:], 0.0), writes=(r_hpin,))
            for t in range(2 * NT):
                n = TT
                k.dma("sp", xT[:, :, :n], src.rearrange("(c p) t -> p c t", p=128)[:, :, t * TT:t * TT + n],
                      reads=tuple(r_src_list[t:t + 1]), writes=(r_xT,))
                load_rope(t * TT, n)
                norm_fm(xT, r_xT, NCD, n, sq16, r_sq16, li, P_AN, hT, r_hT, float(D), 2)
                kv_path(li, n, t * TT, r_kv_d[li][t])
                if t < kv_only_tiles:
                    if t == kv_only_tiles - 1:
                        mixer_inputs(li, n, n - 16, False)
                        save_halo(n)
                    continue
                q_path(li, n)
                attention(li, t, n)
                restore_halo(t == NT)
                mixer_inputs(li, n, 0, True)
                save_halo(n)
                conv_pool(li, t, n)
                merge_and_wo(li, n)
                mlp(li, n)
                d_ap, r_d = dst_fn(t)
                k.dma("sp", d_ap, xT[:, :, :n], reads=(r_xT,), writes=(r_d,))

        def program():
            ring["p"] = 0
            ring["set"] = list(range(8))
            load_consts()
            for li in range(NL):
                last = (li == NL - 1)
                if li == 0:
                    src, r_src = I["xall"], []
                    kvo = KV_ONLY0
                else:
                    src, r_src = X1_d[li - 1], r_x1[li - 1]
                    kvo = NT
                if last:
                    dst_fn = lambda t: (OUT.rearrange("(c p) t -> p c t", p=128)[:, :, (t - NT) * TT:(t - NT + 1) * TT], r_out[t - NT])
                else:
                    dst_fn = lambda t, li=li: (X1_d[li].rearrange("(c p) t -> p c t", p=128)[:, :, t * TT:(t + 1) * TT], r_x1[li][t])
                layer(li, src, r_src, kvo, dst_fn)
            k.finish(r_out)

        k.dry = True
        program()
        ws.pos = 0
        k.dry = False
        program()
        print("[kernel] sbuf bytes remaining", nc.sbuf_bytes_remaining, flush=True)
        print(f"[kernel] instructions={k.n_ins} waits={k.n_wait} weight_slots={len(ws.reqs)}", flush=True)
    return nc


def _rope_tables():
    pos = np.arange(TALL, dtype=np.float32)
    inv = (np.float32(10000.0) ** (-np.arange(0, 64, 2, dtype=np.float32) / np.float32(64))).astype(np.float32)
    ang = (pos[:, None] * inv[None, :]).astype(np.float32)
    c, s = np.cos(ang).astype(np.float32), np.sin(ang).astype(np.float32)
    C = np.concatenate([c, c], 1).T
    S = np.concatenate([-s, s], 1).T
    return np.ascontiguousarray(C), np.ascontiguousarray(S)


def _params(inp, l):
    P = np.zeros((128, NPARAM), np.float32)
    P[:, P_AN:P_AN + 16] = inp["attn_norm"][l].reshape(16, 128).T
    P[:, P_MN:P_MN + 16] = inp["mlp_norm"][l].reshape(16, 128).T
    P[:, P_QLN:P_QLN + 4] = inp["q_lat_norm"][l].reshape(4, 128).T
    P[:, P_KVLN:P_KVLN + 4] = inp["kv_lat_norm"][l].reshape(4, 128).T
    P[:, P_QN] = inp["q_norm"][l][:128]
    P[:64, P_QR] = inp["q_norm"][l][128:]
    P[:, P_KN] = inp["k_norm"][l][:128]
    P[:64, P_KR] = inp["k_norm"][l][128:]
    cw = inp["conv_w"][l]
    for ch in range(8):
        for j in range(3):
            P[:, P_CW + 3 * ch + j] = cw[j, ch * 128:(ch + 1) * 128]
    P[:, P_PS:P_PS + 8] = inp["pool_scale"][l].reshape(8, 128).T
    return P


_NC_CACHE = {}


def _get_nc(layers, fused):
    key = (tuple(layers), fused)
    if key not in _NC_CACHE:
        _NC_CACHE[key] = build_program(list(layers), fused)
    return _NC_CACHE[key]


def _run(layers, inp, x_own, x_pre, fused):
    nc = _get_nc(layers, fused)
    C, S = _rope_tables()
    bf = ml_dtypes.bfloat16
    tri = (np.arange(128)[None, :] >= np.arange(128)[:, None]).astype(bf)
    ones = np.ones((128, 128), bf)
    swap = np.zeros((64, 64), bf)
    for i in range(32):
        swap[i + 32, i] = 1
        swap[i, i + 32] = 1
    ls = list(layers)
    common = {
        "params": np.stack([_params(inp, l) for l in ls]),
        "tri": tri, "ones": ones, "swap": swap,
    }
    for i, l in enumerate(ls):
        for nm in ["w_in", "w_uq", "w_ukv", "pool_w", "w_branch_a", "w_branch_b", "w_branch_c", "w_o", "w_up", "w_down"]:
            common[f"{nm}{i}"] = np.ascontiguousarray(inp[nm][l])
    in_maps = []
    for c in range(8):
        r = c % 2
        pos0 = r * NTOK
        ropeC = np.concatenate([C[:, 0:NPRE], C[:, pos0:pos0 + NTOK]], 1)
        ropeS = np.concatenate([S[:, 0:NPRE], S[:, pos0:pos0 + NTOK]], 1)
        tpos = np.concatenate([np.arange(0, NPRE), np.arange(pos0, pos0 + NTOK)]).astype(np.float32) + 1.0
        invc = np.stack([np.float32(1.0) / np.minimum(tpos, np.float32(w)) for w in (2, 4, 8, 16)]).astype(np.float32)
        invc = np.ascontiguousarray(np.broadcast_to(invc[None], (128, 4, NKEY)))
        kb = np.zeros((128, NKB), np.float32)
        if r == 0:
            keyidx = np.arange(NKB * 128).reshape(NKB, 128).T
            kb[keyidx < NPRE] = -30000.0
        m = dict(common)
        m.update({"xall": np.ascontiguousarray(np.concatenate([x_pre[c], x_own[c]], axis=1)),
                  "ropeC": np.ascontiguousarray(ropeC), "ropeS": np.ascontiguousarray(ropeS), "invc": invc,
                  "kbias": kb, "flag": np.full((128, 1), float(r), np.float32)})
        in_maps.append(m)
    res = run_bass_kernel_spmd(nc, in_maps, core_ids=list(range(8)))
    if DEBUG:
        global DBG_OUT
        DBG_OUT = [{nm: np.asarray(res.results[c][nm]) for nm in DBG_NAMES} for c in range(8)]
    return [np.asarray(res.results[c]["outT"]) for c in range(8)]


FUSED = True


def kernel(**inp):
    inp = {k_: np.asarray(v) for k_, v in inp.items()}
    x = inp["x"].astype(np.float32)
    B = x.shape[0]
    meta = np.broadcast_to(inp["meta_tokens"][None].astype(np.float32), (B, NMETA, D))
    hseq = np.concatenate([meta, x], axis=1)
    own = [np.ascontiguousarray(hseq[c // 2, (c % 2) * NTOK:(c % 2 + 1) * NTOK].T) for c in range(8)]
    zero = np.zeros((D, NPRE), np.float32)
    if FUSED:
        pre = [zero if c % 2 == 0 else own[c - 1] for c in range(8)]
        outs = _run([0, 1], inp, own, pre, True)
    else:
        cur = own
        for l in range(2):
            pre = [zero if c % 2 == 0 else cur[c - 1] for c in range(8)]
            cur = _run([l], inp, cur, pre, False)
        outs = cur
    full = np.stack([np.concatenate([outs[2 * b].T, outs[2 * b + 1].T], axis=0) for b in range(B)])
    return np.ascontiguousarray(full[:, NMETA:, :]).astype(np.float32)
```

```python
import contextlib
import numpy as np
import ml_dtypes
import concourse.bass as bass
import concourse.mybir as mybir
from concourse.bass_utils import run_bass_kernel_spmd

F32 = mybir.dt.float32
BF16 = mybir.dt.bfloat16
AF = mybir.ActivationFunctionType
ALU = mybir.AluOpType

D = 2048
NCD = 16
H = 16
SEQ = 4096
NMETA = 16
TALL = SEQ + NMETA
NTOK = TALL // 2
NPRE = NTOK
TT = 257
NT = NTOK // TT
NKEY = NPRE + NTOK
NKB = (NKEY + 127) // 128
SEG0_KB = 16
SEGK = 2064
D_IN = 11328
C_U, C_B, C_C, C_QL, C_KVL, C_KR, C_POOL, C_GATE = 0, 1024, 2048, 3072, 3584, 4096, 4160, 5184
DFF = 8192
EPS = 1e-6
SCALE = 192.0 ** -0.5
SLOT = 4096
NSLOT = 5
P_AN, P_MN, P_QLN, P_KVLN, P_QN, P_QR, P_KN, P_KR, P_CW, P_PS = 0, 16, 32, 36, 40, 41, 42, 43, 44, 68
NPARAM = 76


class Res:
    __slots__ = ("name", "excl", "w", "rs", "ov")

    def __init__(self, name, excl=False):
        self.name = name
        self.excl = excl
        self.w = None
        self.rs = {}
        self.ov = [self]


class Eng:
    def __init__(self, name, eng, sem):
        self.name, self.eng, self.sem = name, eng, sem
        self.cnt = 0
        self.seen = {}


class KB:
    def __init__(self, nc, es):
        self.nc = nc
        self.dry = False
        mk = lambda n: es.enter_context(nc.semaphore(n))
        self.E = {
            "pe": Eng("pe", nc.tensor, mk("s_pe")),
            "act": Eng("act", nc.scalar, mk("s_act")),
            "dve": Eng("dve", nc.vector, mk("s_dve")),
            "pool": Eng("pool", nc.gpsimd, mk("s_pool")),
            "sp": Eng("sp", nc.sync, mk("s_sp")),
        }
        self.dma_sems = {"sp": [mk(f"d_sp{i}") for i in range(12)], "pool": [mk(f"d_pl{i}") for i in range(8)]}
        self.dma_cnt = {}
        self.dma_rr = {"sp": 0, "pool": 0}
        self.n_wait = 0
        self.n_ins = 0

    def _wait(self, e, tk):
        sem, val = tk
        if e.seen.get(sem.num, 0) >= val:
            return
        e.eng.wait_ge(sem, val)
        e.seen[sem.num] = val
        self.n_wait += 1

    def _deps(self, reads, writes):
        tks = []
        for r in reads:
            for res in r.ov:
                if res.w is not None:
                    tks.append((res.w, False))
                if res.excl:
                    tks.extend((t, True) for t in res.rs.values())
        for w in writes:
            for res in w.ov:
                if res.w is not None:
                    tks.append((res.w, False))
                tks.extend((t, False) for t in res.rs.values())
        return tks

    def _mark(self, tk, reads, writes):
        for r in reads:
            r.rs[tk[0].num] = tk
        for w in writes:
            w.w = tk
            w.rs = {}

    def op(self, en, fn, reads=(), writes=()):
        if self.dry:
            return
        e = self.E[en]
        for tk, rr in self._deps(reads, writes):
            if tk[0] is e.sem and (rr or en == "pe"):
                continue
            self._wait(e, tk)
        ins = fn(e.eng)
        e.cnt += 1
        ins.then_inc(e.sem, 1)
        self.n_ins += 1
        self._mark((e.sem, e.cnt), reads, writes)

    def prewait(self, en, reads=(), writes=()):
        if self.dry:
            return
        e = self.E[en]
        best = {}
        for tk, rr in self._deps(reads, writes):
            if tk[0] is e.sem:
                continue
            if tk[0].num not in best or best[tk[0].num][1] < tk[1]:
                best[tk[0].num] = tk
        for tk in best.values():
            self._wait(e, tk)

    def mm(self, mms, reads, writes):
        if self.dry:
            return
        e = self.E["pe"]
        for tk, rr in self._deps(reads, writes):
            if tk[0] is e.sem:
                continue
            self._wait(e, tk)
        ins = None
        for (o, l, r, st, sp) in mms:
            ins = e.eng.matmul(o, lhsT=l, rhs=r, start=st, stop=sp)
            self.n_ins += 1
        e.cnt += 1
        ins.then_inc(e.sem, 1)
        self._mark((e.sem, e.cnt), reads, writes)

    def dma(self, q, out, in_, reads=(), writes=()):
        if self.dry:
            return
        e = self.E[q]
        for tk, rr in self._deps(reads, writes):
            self._wait(e, tk)
        pool = self.dma_sems[q]
        i = self.dma_rr[q]
        self.dma_rr[q] = (i + 1) % len(pool)
        sem = pool[i]
        prev = self.dma_cnt.get(sem.num, 0)
        if prev:
            self._wait(e, (sem, prev))
        ins = e.eng.dma_start(out=out, in_=in_)
        ins.then_inc(sem, 16)
        self.dma_cnt[sem.num] = prev + 16
        self.n_ins += 1
        self._mark((sem, prev + 16), reads, writes)

    def finish(self, res_list):
        e = self.E["sp"]
        for r in res_list:
            if r.w is not None:
                self._wait(e, r.w)


class Region:
    def __init__(self, tile, nelem):
        self.t = tile
        self.n = nelem
        self.off = 0
        self.phase = None
        self.bufs = []

    def set_phase(self, ph):
        self.phase = ph
        self.off = 0

    def take(self, name, free_shape, dtype):
        n = int(np.prod(free_shape))
        nb = n * (2 if dtype == F32 else 1)
        self.off = (self.off + 15) // 16 * 16
        off = self.off
        self.off += nb
        assert self.off <= self.n, (name, self.off, self.n)
        v = self.t[:, off:off + nb]
        if dtype == F32:
            v = v.bitcast(F32)
        if len(free_shape) == 2:
            v = v.rearrange("p (a b) -> p a b", a=free_shape[0])
        elif len(free_shape) == 3:
            v = v.rearrange("p (a b c) -> p a b c", a=free_shape[0], b=free_shape[1])
        r = Res(name)
        for (ph, r2) in self.bufs:
            if ph != self.phase:
                r.ov.append(r2)
                r2.ov.append(r)
        self.bufs.append((self.phase, r))
        return v, r


class WStream:
    def __init__(self, k, slots, slot_res, cache_fn):
        self.k = k
        self.slots = slots
        self.res = slot_res
        self.reqs = []
        self.issued = 0
        self.pos = 0
        self.cache_fn = cache_fn
        self.cache_idx = {}
        self.cache_res = {}
        self.loaded = set()

    def get(self, loader, keep=0):
        k = self.k
        i = self.pos
        self.pos += 1
        if k.dry:
            self.reqs.append(loader)
            if loader.key not in self.cache_idx:
                self.cache_idx[loader.key] = len(self.cache_idx)
                self.cache_res[loader.key] = Res("wc%d" % len(self.cache_idx))
            return self.slots[i % NSLOT], self.res[i % NSLOT]
        assert keep < NSLOT - 1
        lim = min(len(self.reqs), i - keep + NSLOT)
        while self.issued < lim:
            j = self.issued
            s, r = self.slots[j % NSLOT], self.res[j % NSLOT]
            ld = self.reqs[j]
            ci = self.cache_idx[ld.key]
            cr = self.cache_res[ld.key]
            if ld.key not in self.loaded:
                self.loaded.add(ld.key)
                for (o, src) in ld(s):
                    k.dma("pool", o, src, reads=(), writes=(r,))
                k.dma("sp", self.cache_fn(ci)[:, 0:ld.nel], s[:, 0:ld.nel], reads=(r,), writes=(cr,))
            else:
                k.dma("pool", s[:, 0:ld.nel], self.cache_fn(ci)[:, 0:ld.nel], reads=(cr,), writes=(r,))
            self.issued += 1
        return self.slots[i % NSLOT], self.res[i % NSLOT]


def slot_view(s, kc, ncols):
    return s[:, 0:kc * ncols].rearrange("p (k n) -> p k n", k=kc)


DEBUG = False
DBG_NAMES = []


def build_program(layers, fused):
    KV_ONLY0 = 0 if len(layers) > 1 else NT
    nc = bass.Bass("TRN2", target_bir_lowering=False)
    NL = len(layers)
    dt = lambda name, shape, dtype=F32, kind="ExternalInput": nc.dram_tensor(name, list(shape), dtype, kind=kind).ap()
    I = {}
    I["xall"] = dt("xall", [D, NKEY])
    I["w_in"] = [dt(f"w_in{l}", [D, D_IN]) for l in range(NL)]
    I["w_uq"] = [dt(f"w_uq{l}", [512, 3072]) for l in range(NL)]
    I["w_ukv"] = [dt(f"w_ukv{l}", [512, 4096]) for l in range(NL)]
    I["pool_w"] = [dt(f"pool_w{l}", [4, 256, 256]) for l in range(NL)]
    I["w_a"] = [dt(f"w_branch_a{l}", [1024, D]) for l in range(NL)]
    I["w_b"] = [dt(f"w_branch_b{l}", [D, D]) for l in range(NL)]
    I["w_c"] = [dt(f"w_branch_c{l}", [1024, D]) for l in range(NL)]
    I["w_o"] = [dt(f"w_o{l}", [D, D]) for l in range(NL)]
    I["w_up"] = [dt(f"w_up{l}", [D, DFF]) for l in range(NL)]
    I["w_down"] = [dt(f"w_down{l}", [DFF, D]) for l in range(NL)]
    I["params"] = dt("params", [NL, 128, NPARAM])
    I["ropeC"] = dt("ropeC", [64, NKEY])
    I["ropeS"] = dt("ropeS", [64, NKEY])
    I["invc"] = dt("invc", [128, 4, NKEY])
    I["kbias"] = dt("kbias", [128, NKB])
    I["flag"] = dt("flag", [128, 1])
    I["tri"] = dt("tri", [128, 128], BF16)
    I["ones"] = dt("ones", [128, 128], BF16)
    I["swap"] = dt("swap", [64, 64], BF16)
    OUT = dt("outT", [D, NTOK], F32, kind="ExternalOutput")
    KTn_d = [dt(f"ktn{l}", [H, 128, NKEY], BF16, kind="Internal") for l in range(NL)]
    KTr_d = [dt(f"ktr{l}", [H, 64, NKEY], BF16, kind="Internal") for l in range(NL)]
    V_d = [dt(f"v{l}", [H, 128, NKB, 128], BF16, kind="Internal") for l in range(NL)]
    X1_d = [dt(f"x1_{l}", [D, NKEY], F32, kind="Internal") for l in range(NL - 1)]

    with contextlib.ExitStack() as es:
        k = KB(nc, es)
        sb = lambda name, shape, dtype: es.enter_context(nc.sbuf_tensor(name, list(shape), dtype))
        xT = sb("xT_sb", [128, NCD, TT], F32); r_xT = Res("xT")
        hT = sb("hT_sb", [128, NCD, TT], BF16); r_hT = Res("hT")
        ybT = sb("ybT_sb", [128, H, TT], BF16); r_ybT = Res("ybT")
        wsl = [sb(f"wslot{i}", [128, SLOT], BF16) for i in range(NSLOT)]
        r_wsl = [Res(f"wslot{i}") for i in range(NSLOT)]
        params = sb("params_sb", [128, NL, NPARAM], F32); r_par = Res("params")
        tri = sb("tri_sb", [128, 128], BF16)
        ones = sb("ones_sb", [128, 128], BF16)
        swp = sb("swap_sb", [64, 64], BF16)
        kbias = sb("kbias_sb", [128, NKB], F32)
        flag = sb("flag_sb", [128, 1], F32)
        epsb = sb("eps_sb", [128, 1], F32)
        zb = sb("zb_sb", [128, 1], F32)
        r_const = Res("consts")
        rstd = [sb(f"rstd{i}", [128, TT], F32) for i in range(3)]
        r_rstd = [Res(f"rstd{i}") for i in range(3)]
        lnv = [sb(f"lnv{i}", [128, TT], F32) for i in range(2)]
        r_lnv = [Res(f"lnv{i}") for i in range(2)]
        ropeC = sb("ropeC_sb", [64, TT], F32)
        ropeS = sb("ropeS_sb", [64, TT], F32)
        r_rope = Res("rope")
        invc = sb("invc_sb", [128, 4, TT], F32); r_invc = Res("invc")
        PTall = sb("ptall", [128, 9, TT], BF16)
        PT = [PTall[:, i, :] for i in range(9)]
        r_PT = [Res(f"pt{i}") for i in range(9)]
        rden = sb("rden", [128, TT], F32); r_rden = Res("rden")
        gate = [sb(f"gate{i}", [128, TT], F32) for i in range(3)]
        r_gate = [Res(f"gate{i}") for i in range(3)]
        mtmp = [sb(f"mtmp{i}", [128, TT], F32) for i in range(6)]
        r_mtmp = [Res(f"mtmp{i}") for i in range(6)]
        rl = [sb(f"relu{i}", [128, TT], F32) for i in range(2)]
        r_rl = [Res(f"relu{i}") for i in range(2)]
        halo_cu = sb("halo_cu", [128, 8, 2], BF16); r_hcu = Res("halo_cu")
        halo_pin = sb("halo_pin", [128, 8, 16], F32); r_hpin = Res("halo_pin")
        U1N = 24 * 1024
        U2N = 28 * 1024
        U1 = Region(sb("U1", [128, U1N], BF16), U1N)
        U2 = Region(sb("U2", [128, U2N], BF16), U2N)
        U1.set_phase("kv")
        kvl, r_kvl = U1.take("kvl", [5, TT], F32)
        sq5, r_sq5 = U1.take("sq5", [5, TT], BF16)
        kvnT, r_kvnT = U1.take("kvnT", [4, TT], BF16)
        KTn_all, r_KTn = U1.take("KTn_all", [H, TT], BF16)
        KTr_all, r_KTr = U1.take("KTr_all", [H, TT], BF16)
        Vsb, r_Vsb = U1.take("Vsb", [3, D], BF16)
        krg, r_krg = U1.take("krg", [TT], F32)
        kx, r_kx = U1.take("kx", [2, TT], F32)
        kxb, r_kxb = U1.take("kxb", [TT], BF16)
        sqh, r_sqh = U1.take("sqh", [2, 2, TT], BF16)
        U1.set_phase("mix")
        usb, r_usb = U1.take("usb", [2, TT], F32)
        cu, r_cu = U1.take("cu", [8, TT + 2], BF16)
        bsb, r_bsb = U1.take("bsb", [8, TT], BF16)
        ctmp, r_ctmp = U1.take("ctmp", [2, TT], F32)
        yaT, r_yaT = U1.take("yaT", [8, TT], BF16)
        pin, r_pin = U1.take("pin", [8, TT + 16], F32)
        pw1, r_pw1 = U1.take("pw1", [2, TT + 16], F32)
        pw2, r_pw2 = U1.take("pw2", [2, TT + 16], F32)
        pooled, r_pooled = U1.take("pooled", [8, TT], BF16)
        ycT, r_ycT = U1.take("ycT", [8, TT], BF16)
        merged, r_merged = U1.take("merged", [NCD, TT], BF16)
        U2.set_phase("att")
        qlat, r_qlat = U2.take("qlat", [4, TT], F32)
        sq4, r_sq4 = U2.take("sq4", [4, TT], BF16)
        qnT, r_qnT = U2.take("qnT", [4, TT], BF16)
        QTn, r_QTn = U2.take("QTn", [H, TT], BF16)
        QTr, r_QTr = U2.take("QTr", [H, TT], BF16)
        qx, r_qx = U2.take("qx", [2, TT], F32)
        qxb, r_qxb = U2.take("qxb", [TT], BF16)
        sqq, r_sqq = U2.take("sqq", [2, 2, TT], BF16)
        Kn_s = []; Kr_s = []; V_s = []
        for i in range(2):
            a, ra = U2.take(f"Kn_s{i}", [SEGK], BF16)
            b_, rb = U2.take(f"Kr_s{i}", [SEGK], BF16)
            c_, rc = U2.take(f"V_s{i}", [17, 128], BF16)
            Kn_s.append((a, ra)); Kr_s.append((b_, rb)); V_s.append((c_, rc))
        U2.set_phase("mlp")
        sq16, r_sq16 = U2.take("sq16", [NCD, TT], BF16)
        act, r_act = U2.take("act", [64, TT], BF16)
        psall = es.enter_context(nc.psum_tensor("psall", [128, 8 * 512], F32))
        banks = [psall[:, 512 * i:512 * (i + 1)] for i in range(8)]
        r_bank = [Res(f"bank{i}", excl=True) for i in range(8)]
        ring = {"set": list(range(8)), "p": 0}

        def ps_next():
            i = ring["set"][ring["p"] % len(ring["set"])]
            ring["p"] += 1
            return banks[i], r_bank[i]

        WC_d = [dt(f"wcache{i}", [150, 128, SLOT], BF16, kind="Internal") for i in range(NL)]
        ws = WStream(k, wsl, r_wsl, lambda ci: WC_d[ci // 150][ci % 150])
        cc_sem = es.enter_context(nc.semaphore("cc_sem"))

        def dump(name, ap, shape, dtype, r):
            if not DEBUG or k.dry:
                return
            dram = nc.dram_tensor("dbg_" + name, list(shape), dtype, kind="ExternalOutput").ap()
            DBG_NAMES.append("dbg_" + name)
            k.dma("sp", dram, ap, reads=(r,), writes=(Res("dbg_" + name),))

        r_kv_d = [[Res(f"kvd{l}_{t}") for t in range(2 * NT)] for l in range(NL)]
        r_x1 = [[Res(f"x1_{l}_{t}") for t in range(2 * NT)] for l in range(NL)]
        r_out = [Res(f"out_{t}") for t in range(NT)]

        def V(fn, **kw):
            return lambda e: fn(e, **kw)

        def load_consts():
            k.dma("sp", params[:], I["params"].rearrange("l p c -> p l c"), writes=(r_par,))
            k.dma("sp", tri[:], I["tri"], writes=(r_const,))
            k.dma("sp", ones[:], I["ones"], writes=(r_const,))
            k.dma("sp", swp[:], I["swap"], writes=(r_const,))
            k.dma("sp", kbias[:], I["kbias"], writes=(r_const,))
            k.dma("sp", flag[:], I["flag"], writes=(r_const,))
            k.op("dve", lambda e: e.memset(epsb[:], EPS), writes=(r_const,))
            k.op("dve", lambda e: e.memset(zb[:], 0.0), writes=(r_const,))

        def pcol(li, c):
            return params[:, li, c:c + 1]

        def rstd_from_ps(ps, r_ps, n, dim, slot):
            lv, r_lv = lnv[slot % 2], r_lnv[slot % 2]
            rs, r_rs = rstd[slot], r_rstd[slot]
            k.op("act", lambda e: e.activation(out=lv[:, :n], in_=ps[:, :n], func=AF.Ln, bias=epsb[:], scale=1.0 / dim),
                 reads=(r_ps, r_const), writes=(r_lv,))
            k.op("act", lambda e: e.activation(out=rs[:, :n], in_=lv[:, :n], func=AF.Exp, scale=-0.5),
                 reads=(r_lv,), writes=(r_rs,))
            return rs, r_rs

        def norm_fm(src, r_src, nch, n, sq, r_sq, li, gcol, dst, r_dst, dim, slot):
            for c in range(nch):
                k.op("act", lambda e, c=c: e.activation(out=sq[:, c, :n], in_=src[:, c, :n], func=AF.Square),
                     reads=(r_src,), writes=(r_sq,))
            ps, r_ps = ps_next()
            k.mm([(ps[:, :n], ones[:, :], sq[:, c, :n], c == 0, c == nch - 1) for c in range(nch)],
                 reads=(r_sq, r_const), writes=(r_ps,))
            rs, r_rs = rstd_from_ps(ps, r_ps, n, dim, slot)
            for c in range(nch):
                k.op("dve", lambda e, c=c: e.scalar_tensor_tensor(out=dst[:, c, :n], in0=src[:, c, :n], scalar=pcol(li, gcol + c),
                                                                  in1=rs[:, :n], op0=ALU.mult, op1=ALU.mult),
                     reads=(r_src, r_rs, r_par), writes=(r_dst,))

        def w_in_loader(li, c0, ncols):
            def f(s):
                return [(slot_view(s, NCD, ncols), I["w_in"][li][:, c0:c0 + ncols].rearrange("(k p) n -> p k n", p=128))]
            f.key = ("w_in", li, c0, ncols)
            f.nel = NCD * ncols
            return f

        def gen_loader(ap2d, kc, ncols, key):
            def f(s):
                return [(slot_view(s, kc, ncols), ap2d.rearrange("(k p) n -> p k n", p=128))]
            f.key = key
            f.nel = kc * ncols
            return f

        def rope_apply(xps, r_xps, n, li, gcolr, x_f, r_x_f, xb, r_xb, dst_fn):
            k.op("act", lambda e: e.mul(out=xb[0:64, :n], in_=xps[0:64, :n], mul=pcol(li, gcolr)[0:64, :]),
                 reads=(r_xps, r_par), writes=(r_xb,))
            k.op("dve", lambda e: e.scalar_tensor_tensor(out=x_f[0:64, 0, :n], in0=xps[0:64, :n], scalar=pcol(li, gcolr)[0:64, :],
                                                         in1=ropeC[0:64, :n], op0=ALU.mult, op1=ALU.mult),
                 reads=(r_xps, r_par, r_rope), writes=(r_x_f,))
            ps2, r_ps2 = ps_next()
            k.mm([(ps2[0:64, :n], swp[:, :], xb[0:64, :n], True, True)], reads=(r_xb, r_const), writes=(r_ps2,))
            k.op("dve", lambda e: e.tensor_tensor(out=x_f[0:64, 1, :n], in0=ps2[0:64, :n], in1=ropeS[0:64, :n], op=ALU.mult),
                 reads=(r_ps2, r_rope, r_x_f), writes=(r_x_f,))
            k.op("pool", lambda e: e.tensor_tensor(out=x_f[0:64, 0, :n], in0=x_f[0:64, 0, :n], in1=x_f[0:64, 1, :n], op=ALU.add),
                 reads=(r_x_f,), writes=(r_x_f,))

        def load_rope(key0, n):
            k.dma("sp", ropeC[:, :n], I["ropeC"][:, key0:key0 + n], writes=(r_rope,))
            k.dma("sp", ropeS[:, :n], I["ropeS"][:, key0:key0 + n], writes=(r_rope,))

        def kv_path(li, n, key0, r_kvdst):
            for g in range(2):
                s, r_s = ws.get(w_in_loader(li, C_KVL + 256 * g, 256))
                sv = slot_view(s, NCD, 256)
                for m in range(2):
                    ps, r_ps = ps_next()
                    k.mm([(ps[:, :n], sv[:, kc, m * 128:(m + 1) * 128], hT[:, kc, :n], kc == 0, kc == NCD - 1) for kc in range(NCD)],
                         reads=(r_s, r_hT), writes=(r_ps,))
                    c = 2 * g + m
                    k.op("dve", lambda e, c=c, ps=ps: e.tensor_copy(out=kvl[:, c, :n], in_=ps[:, :n]), reads=(r_ps,), writes=(r_kvl,))
            s, r_s = ws.get(w_in_loader(li, C_KR, 64))
            sv = slot_view(s, NCD, 64)
            psr, r_psr = ps_next()
            k.mm([(psr[0:64, :n], sv[:, kc, :], hT[:, kc, :n], kc == 0, kc == NCD - 1) for kc in range(NCD)],
                 reads=(r_s, r_hT), writes=(r_psr,))
            k.op("act", lambda e: e.activation(out=sq5[0:64, 4, :n], in_=psr[0:64, :n], func=AF.Square), reads=(r_psr,), writes=(r_sq5,))
            rope_apply(psr, r_psr, n, li, P_KR, kx, r_kx, kxb, r_kxb, None)
            k.op("pool", lambda e: e.tensor_copy(out=krg[0:64, :n], in_=kx[0:64, 0, :n]), reads=(r_kx,), writes=(r_krg,))
            norm_fm(kvl, r_kvl, 4, n, sq5, r_sq5, li, P_KVLN, kvnT, r_kvnT, 512.0, 2)
            ntb = (n + 127) // 128
            for hg in range(4):
                s, r_s = ws.get(gen_loader(I["w_ukv"][li][:, 1024 * hg:1024 * (hg + 1)], 4, 1024, ("w_ukv", li, hg)))
                sv = slot_view(s, 4, 1024)
                for hh in range(4):
                    h = 4 * hg + hh
                    ps, r_ps = ps_next()
                    k.mm([(ps[:, :n], sv[:, kc, 256 * hh:256 * hh + 128], kvnT[:, kc, :n], kc == 0, kc == 3) for kc in range(4)],
                         reads=(r_s, r_kvnT), writes=(r_ps,))
                    sq, r_sq = sqh[:, h % 2, 0, :], r_sqh
                    k.op("act", lambda e, ps=ps, sq=sq: e.activation(out=sq[:, :n], in_=ps[:, :n], func=AF.Square),
                         reads=(r_ps,), writes=(r_sq,))
                    pss, r_pss = ps_next()
                    k.mm([(pss[:, :n], ones[:, :], sq[:, :n], True, False),
                          (pss[:, :n], ones[0:64, :], sq5[0:64, 4, :n], False, True)],
                         reads=(r_sq, r_sq5, r_const), writes=(r_pss,))
                    rs, r_rs = rstd_from_ps(pss, r_pss, n, 192.0, h % 2)
                    k.op("dve", lambda e, ps=ps, rs=rs, h=h: e.scalar_tensor_tensor(out=KTn_all[:, h, :n], in0=ps[:, :n], scalar=pcol(li, P_KN),
                                                                                   in1=rs[:, :n], op0=ALU.mult, op1=ALU.mult),
                         reads=(r_ps, r_rs, r_par), writes=(r_KTn,))
                    k.op("pool", lambda e, rs=rs, h=h: e.tensor_tensor(out=KTr_all[0:64, h, :n], in0=krg[0:64, :n], in1=rs[0:64, :n], op=ALU.mult),
                         reads=(r_krg, r_rs), writes=(r_KTr,))
                for tb in range(ntb):
                    m = min(128, n - 128 * tb)
                    ps, r_ps = ps_next()
                    mms = []
                    for hh in range(4):
                        for kc in range(4):
                            mms.append((ps[0:m, 128 * hh:128 * (hh + 1)], kvnT[:, kc, 128 * tb:128 * tb + m],
                                        sv[:, kc, 256 * hh + 128:256 * hh + 256], kc == 0, kc == 3))
                    k.mm(mms, reads=(r_s, r_kvnT), writes=(r_ps,))
                    k.op("act", lambda e, ps=ps, m=m, tb=tb, hg=hg: e.activation(out=Vsb[0:m, tb, 512 * hg:512 * (hg + 1)], in_=ps[0:m, 0:512], func=AF.Copy),
                         reads=(r_ps,), writes=(r_Vsb,))
            k.dma("sp", KTn_d[li][:, :, key0:key0 + n].rearrange("h p t -> p h t"), KTn_all[:, :, :n], reads=(r_KTn,), writes=(r_kvdst,))
            k.dma("sp", KTr_d[li][:, :, key0:key0 + n].rearrange("h p t -> p h t"), KTr_all[0:64, :, :n], reads=(r_KTr,), writes=(r_kvdst,))
            cuts = sorted(set([0, n] + [128 * i for i in range(1, ntb)] + [g * 128 - key0 for g in range(key0 // 128 + 1, (key0 + n) // 128 + 1) if 0 < g * 128 - key0 < n]))
            for a_, b_ in zip(cuts[:-1], cuts[1:]):
                tb, lp0 = divmod(a_, 128)
                gb, gp0 = divmod(key0 + a_, 128)
                np_ = b_ - a_
                k.dma("sp", V_d[li][:, gp0:gp0 + np_, gb, :].rearrange("h p d -> p h d"),
                      Vsb[lp0:lp0 + np_, tb, :].rearrange("p (h d) -> p h d", h=H), reads=(r_Vsb,), writes=(r_kvdst,))

        def mixer_inputs(li, n, c0, with_b):
            nn = n - c0
            for pr in range(4):
                su, r_su = ws.get(w_in_loader(li, C_U + 256 * pr, 256))
                sc, r_sc = ws.get(w_in_loader(li, C_C + 256 * pr, 256), keep=1)
                for m in range(2):
                    ch = 2 * pr + m
                    ps, r_ps = ps_next()
                    svu = slot_view(su, NCD, 256)
                    k.mm([(ps[:, :nn], svu[:, kc, m * 128:(m + 1) * 128], hT[:, kc, c0:n], kc == 0, kc == NCD - 1) for kc in range(NCD)],
                         reads=(r_su, r_hT), writes=(r_ps,))
                    ub = usb[:, ch % 2, :]
                    k.op("act", lambda e, ps=ps, ub=ub: e.activation(out=ub[:, :nn], in_=ps[:, :nn], func=AF.Copy), reads=(r_ps,), writes=(r_usb,))
                    ps2, r_ps2 = ps_next()
                    svc = slot_view(sc, NCD, 256)
                    k.mm([(ps2[:, :nn], svc[:, kc, m * 128:(m + 1) * 128], hT[:, kc, c0:n], kc == 0, kc == NCD - 1) for kc in range(NCD)],
                         reads=(r_sc, r_hT), writes=(r_ps2,))
                    k.op("dve", lambda e, ps2=ps2, ub=ub, ch=ch: e.tensor_tensor(out=cu[:, ch, 2 + c0:2 + n], in0=ps2[:, :nn], in1=ub[:, :nn], op=ALU.mult),
                         reads=(r_ps2, r_usb), writes=(r_cu,))
                if with_b:
                    sbb, r_sbb = ws.get(w_in_loader(li, C_B + 256 * pr, 256))
                    svb = slot_view(sbb, NCD, 256)
                    for m in range(2):
                        ch = 2 * pr + m
                        ps3, r_ps3 = ps_next()
                        k.mm([(ps3[:, :nn], svb[:, kc, m * 128:(m + 1) * 128], hT[:, kc, c0:n], kc == 0, kc == NCD - 1) for kc in range(NCD)],
                             reads=(r_sbb, r_hT), writes=(r_ps3,))
                        k.op("act", lambda e, ps3=ps3, ch=ch: e.activation(out=bsb[:, ch, c0:n], in_=ps3[:, :nn], func=AF.Copy),
                             reads=(r_ps3,), writes=(r_bsb,))
            for pr in range(4):
                sp_, r_sp = ws.get(w_in_loader(li, C_POOL + 256 * pr, 256))
                svp = slot_view(sp_, NCD, 256)
                for m in range(2):
                    ch = 2 * pr + m
                    ps, r_ps = ps_next()
                    k.mm([(ps[:, :nn], svp[:, kc, m * 128:(m + 1) * 128], hT[:, kc, c0:n], kc == 0, kc == NCD - 1) for kc in range(NCD)],
                         reads=(r_sp, r_hT), writes=(r_ps,))
                    k.op("act", lambda e, ps=ps, ch=ch: e.activation(out=pin[:, ch, 16 + c0:16 + n], in_=ps[:, :nn], func=AF.Copy),
                         reads=(r_ps,), writes=(r_pin,))

        def save_halo(n):
            k.op("pool", lambda e: e.tensor_copy(out=halo_cu[:, :, :], in_=cu[:, :, n:n + 2]), reads=(r_cu,), writes=(r_hcu,))
            k.op("pool", lambda e: e.tensor_copy(out=halo_pin[:, :, 1:16], in_=pin[:, :, n + 1:n + 16]), reads=(r_pin,), writes=(r_hpin,))

        def restore_halo(scale_flag):
            if scale_flag:
                k.op("pool", lambda e: e.tensor_scalar(out=cu[:, :, 0:2], in0=halo_cu[:, :, :], scalar1=flag[:, 0:1], scalar2=None, op0=ALU.mult),
                     reads=(r_hcu, r_const), writes=(r_cu,))
                k.op("pool", lambda e: e.tensor_scalar(out=pin[:, :, 1:16], in0=halo_pin[:, :, 1:16], scalar1=flag[:, 0:1], scalar2=None, op0=ALU.mult),
                     reads=(r_hpin, r_const), writes=(r_pin,))
            else:
                k.op("pool", lambda e: e.tensor_copy(out=cu[:, :, 0:2], in_=halo_cu[:, :, :]), reads=(r_hcu,), writes=(r_cu,))
                k.op("pool", lambda e: e.tensor_copy(out=pin[:, :, 1:16], in_=halo_pin[:, :, 1:16]), reads=(r_hpin,), writes=(r_pin,))

        def q_path(li, n):
            for g in range(2):
                s, r_s = ws.get(w_in_loader(li, C_QL + 256 * g, 256))
                sv = slot_view(s, NCD, 256)
                for m in range(2):
                    ps, r_ps = ps_next()
                    k.mm([(ps[:, :n], sv[:, kc, m * 128:(m + 1) * 128], hT[:, kc, :n], kc == 0, kc == NCD - 1) for kc in range(NCD)],
                         reads=(r_s, r_hT), writes=(r_ps,))
                    c = 2 * g + m
                    k.op("dve", lambda e, c=c, ps=ps: e.tensor_copy(out=qlat[:, c, :n], in_=ps[:, :n]), reads=(r_ps,), writes=(r_qlat,))
            norm_fm(qlat, r_qlat, 4, n, sq4, r_sq4, li, P_QLN, qnT, r_qnT, 512.0, 2)
            for hg in range(4):
                s, r_s = ws.get(gen_loader(I["w_uq"][li][:, 768 * hg:768 * (hg + 1)], 4, 768, ("w_uq", li, hg)))
                sv = slot_view(s, 4, 768)
                for hh in range(4):
                    h = 4 * hg + hh
                    psn, r_psn = ps_next()
                    k.mm([(psn[:, :n], sv[:, kc, 192 * hh:192 * hh + 128], qnT[:, kc, :n], kc == 0, kc == 3) for kc in range(4)],
                         reads=(r_s, r_qnT), writes=(r_psn,))
                    psr, r_psr = ps_next()
                    k.mm([(psr[0:64, :n], sv[:, kc, 192 * hh + 128:192 * hh + 192], qnT[:, kc, :n], kc == 0, kc == 3) for kc in range(4)],
                         reads=(r_s, r_qnT), writes=(r_psr,))
                    sqn = sqq[:, h % 2, 0, :]
                    sqr = sqq[:, h % 2, 1, :]
                    k.op("act", lambda e, psn=psn, sqn=sqn: e.activation(out=sqn[:, :n], in_=psn[:, :n], func=AF.Square), reads=(r_psn,), writes=(r_sqq,))
                    k.op("act", lambda e, psr=psr, sqr=sqr: e.activation(out=sqr[0:64, :n], in_=psr[0:64, :n], func=AF.Square), reads=(r_psr,), writes=(r_sqq,))
                    pss, r_pss = ps_next()
                    k.mm([(pss[:, :n], ones[:, :], sqn[:, :n], True, False), (pss[:, :n], ones[0:64, :], sqr[0:64, :n], False, True)],
                         reads=(r_sqq, r_const), writes=(r_pss,))
                    rs, r_rs = rstd_from_ps(pss, r_pss, n, 192.0, h % 2)
                    k.op("dve", lambda e, psn=psn, rs=rs, h=h: e.scalar_tensor_tensor(out=QTn[:, h, :n], in0=psn[:, :n], scalar=pcol(li, P_QN),
                                                                                     in1=rs[:, :n], op0=ALU.mult, op1=ALU.mult),
                         reads=(r_psn, r_rs, r_par), writes=(r_QTn,))
                    rope_apply(psr, r_psr, n, li, P_QR, qx, r_qx, qxb, r_qxb, None)
                    k.op("pool", lambda e, rs=rs, h=h: e.tensor_tensor(out=QTr[0:64, h, :n], in0=qx[0:64, 0, :n], in1=rs[0:64, :n], op=ALU.mult),
                         reads=(r_qx, r_rs), writes=(r_QTr,))

        def attention(li, t, n):
            qs = t * TT
            qe = qs + n
            nkb = (qe + 127) // 128
            segs = [(0, min(nkb, SEG0_KB))]
            if nkb > SEG0_KB:
                segs.append((SEG0_KB, nkb))
            kv_reads = tuple(r_kv_d[li][0:t + 1])
            ring["set"] = [0, 1, 2, 3]
            loads = [(h, si) for h in range(H) for si in range(len(segs))]

            def issue_load(idx):
                h, si = loads[idx]
                b0, b1 = segs[si]
                k0 = 128 * b0
                k1 = min(128 * b1, qe)
                nk = k1 - k0
                (kn, r_kn), (kr, r_kr), (vv, r_vv) = Kn_s[idx % 2], Kr_s[idx % 2], V_s[idx % 2]
                k.dma("sp", kn[:, 0:nk], KTn_d[li][h, :, k0:k1], reads=kv_reads, writes=(r_kn,))
                k.dma("sp", kr[0:64, 0:nk], KTr_d[li][h, :, k0:k1], reads=kv_reads, writes=(r_kr,))
                nb_ = (nk + 127) // 128
                k.dma("sp", vv[:, 0:nb_, :], V_d[li][h, :, b0:b0 + nb_, :], reads=kv_reads, writes=(r_vv,))

            items = []
            for idx, (h, si) in enumerate(loads):
                b0, b1 = segs[si]
                for j in range(b0, b1):
                    items.append((idx, h, si, j))
            last_item_of_load = {}
            for n_, it in enumerate(items):
                last_item_of_load[it[0]] = n_
            G = 3
            issue_load(0)
            if len(loads) > 1:
                issue_load(1)
            stash = {}
            O, r_O = banks[6], r_bank[6]
            Dn, r_Dn = banks[7], r_bank[7]
            ring["set"] = [0, 1, 2, 3, 4, 5]

            def item_geom(n_):
                idx, h, si, j = items[n_]
                kp = min(128, qe - 128 * j, NKEY - 128 * j)
                cst = max(0, 128 * j - qs)
                diag = (128 * j + 128 > qs)
                bcls = 0 if t < NT else (0 if j < 16 else (1 if j == 16 else 2))
                return kp, cst, diag, bcls

            def stage_s_mm(n_, st, r_st):
                idx, h, si, j = items[n_]
                b0, b1 = segs[si]
                (kn, r_kn), (kr, r_kr), (vv, r_vv) = Kn_s[idx % 2], Kr_s[idx % 2], V_s[idx % 2]
                kp, cst, diag, bcls = item_geom(n_)
                kl = 128 * (j - b0)
                k.mm([(st[0:kp, cst:n], kn[:, kl:kl + kp], QTn[:, h, cst:n], True, False),
                      (st[0:kp, cst:n], kr[0:64, kl:kl + kp], QTr[0:64, h, cst:n], False, True)],
                     reads=(r_kn, r_kr, r_QTn, r_QTr), writes=(r_st,))

            def stage_s_exp(n_, st, r_st, pslot):
                idx, h, si, j = items[n_]
                kp, cst, diag, bcls = item_geom(n_)
                pt_, r_pt = PT[pslot], r_PT[pslot]
                k.op("act", lambda e: e.activation(out=pt_[0:kp, cst:n], in_=st[0:kp, cst:n], func=AF.Exp,
                                                   bias=(kbias[0:kp, j:j + 1] if t >= NT else zb[0:kp, 0:1]), scale=SCALE),
                     reads=(r_st, r_const), writes=(r_pt,))
                dend = min(n, 128 * j + 128 - qs)
                if diag:
                    c0 = qs + cst - 128 * j
                    k.op("pool", lambda e: e.tensor_tensor(out=pt_[0:kp, cst:dend], in0=pt_[0:kp, cst:dend], in1=tri[0:kp, c0:c0 + dend - cst], op=ALU.mult),
                         reads=(r_pt, r_const), writes=(r_pt,))
                stash[n_] = (kp, cst, pt_, r_pt)

            def stage_pv(n_):
                idx, h, si, j = items[n_]
                b0, b1 = segs[si]
                (kn, r_kn), (kr, r_kr), (vv, r_vv) = Kn_s[idx % 2], Kr_s[idx % 2], V_s[idx % 2]
                kp, cst, pt_, r_pt = stash.pop(n_)
                first = (j == 0)
                last = (j == nkb - 1)
                k.mm([(O[:, cst:n], vv[0:kp, j - b0, :], pt_[0:kp, cst:n], first, last),
                      (Dn[:, cst:n], ones[0:kp, :], pt_[0:kp, cst:n], first, last)], reads=(r_vv, r_pt, r_const), writes=(r_O, r_Dn))
                if last:
                    k.op("dve", lambda e: e.reciprocal(out=rden[:, :n], in_=Dn[:, :n]), reads=(r_Dn,), writes=(r_rden,))
                    k.op("dve", lambda e: e.tensor_tensor(out=ybT[:, h, :n], in0=O[:, :n], in1=rden[:, :n], op=ALU.mult),
                         reads=(r_O, r_rden), writes=(r_ybT,))
                if last_item_of_load[idx] == n_ and idx + 2 < len(loads):
                    issue_load(idx + 2)

            groups = []
            cur = []
            for n_ in range(len(items)):
                kp, cst, diag, bcls = item_geom(n_)
                full = (kp == 128 and cst == 0 and not diag)
                if cur:
                    kp0, cst0, diag0, bcls0 = item_geom(cur[0])
                    full0 = (kp0 == 128 and cst0 == 0 and not diag0)
                    if len(cur) == G or not (full and full0 and bcls == bcls0):
                        groups.append(cur)
                        cur = []
                cur.append(n_)
            if cur:
                groups.append(cur)
            for g_ in range(len(groups) + 1):
                if g_ < len(groups):
                    grp = groups[g_]
                    base = 3 * (g_ % 2)
                    pbase = 3 * (g_ % 3)
                    bks = [(banks[base + i], r_bank[base + i]) for i in range(len(grp))]
                    k.prewait("pe", reads=(), writes=tuple(rb for (_, rb) in bks))
                    for n_, (st, r_st) in zip(grp, bks):
                        stage_s_mm(n_, st, r_st)
                    kp, cst, diag, bcls = item_geom(grp[0])
                    if len(grp) > 1:
                        ng = len(grp)
                        j0 = items[grp[0]][3]
                        src = psall[:, 512 * base:512 * (base + ng)].rearrange("p (g c) -> p g c", g=ng)[:, :, 0:n]
                        dstp = PTall[:, pbase:pbase + ng, 0:n]
                        k.op("act", lambda e: e.activation(out=dstp, in_=src, func=AF.Exp,
                                                           bias=(kbias[:, j0:j0 + 1] if t >= NT else zb[:, 0:1]), scale=SCALE),
                             reads=tuple(rb for (_, rb) in bks) + (r_const,), writes=tuple(r_PT[pbase + i] for i in range(ng)))
                        for i, n_ in enumerate(grp):
                            stash[n_] = (128, 0, PT[pbase + i], r_PT[pbase + i])
                    else:
                        stage_s_exp(grp[0], bks[0][0], bks[0][1], pbase)
                if g_ >= 1:
                    grp = groups[g_ - 1]
                    k.prewait("pe", reads=tuple(stash[n_][3] for n_ in grp), writes=())
                    for n_ in grp:
                        stage_pv(n_)
            ring["set"] = list(range(8))

        def conv_pool(li, t, n):
            k.dma("sp", invc[:, :, :n], I["invc"][:, :, t * TT:t * TT + n], writes=(r_invc,))
            for ch in range(8):
                tmp = ctmp[:, ch % 2, :]
                k.op("pool", lambda e, ch=ch, tmp=tmp: e.tensor_scalar(out=tmp[:, :n], in0=cu[:, ch, 0:n], scalar1=pcol(li, P_CW + 3 * ch), scalar2=None, op0=ALU.mult),
                     reads=(r_cu, r_par), writes=(r_ctmp,))
                k.op("dve", lambda e, ch=ch, tmp=tmp: e.scalar_tensor_tensor(out=tmp[:, :n], in0=cu[:, ch, 1:n + 1], scalar=pcol(li, P_CW + 3 * ch + 1),
                                                                             in1=tmp[:, :n], op0=ALU.mult, op1=ALU.add),
                     reads=(r_cu, r_par, r_ctmp), writes=(r_ctmp,))
                k.op("dve", lambda e, ch=ch, tmp=tmp: e.scalar_tensor_tensor(out=tmp[:, :n], in0=cu[:, ch, 2:n + 2], scalar=pcol(li, P_CW + 3 * ch + 2),
                                                                             in1=tmp[:, :n], op0=ALU.mult, op1=ALU.add),
                     reads=(r_cu, r_par, r_ctmp), writes=(r_ctmp,))
                k.op("dve", lambda e, ch=ch, tmp=tmp: e.tensor_tensor(out=yaT[:, ch, :n], in0=tmp[:, :n], in1=bsb[:, ch, :n], op=ALU.mult),
                     reads=(r_ctmp, r_bsb), writes=(r_yaT,))
            for g in range(4):
                src = pin[:, 2 * g:2 * g + 2, :]
                cur = src
                r_cur = r_pin
                lo = 1
                bufs = [(pw1, r_pw1), (pw2, r_pw2)]
                for step in range(g + 1):
                    sh = 1 << step
                    dstb, r_dst = bufs[step % 2]
                    lo2 = lo + sh
                    k.op("pool", lambda e, cur=cur, dstb=dstb, lo2=lo2, sh=sh: e.tensor_tensor(out=dstb[:, :, lo2:n + 16], in0=cur[:, :, lo2:n + 16],
                                                                                             in1=cur[:, :, lo2 - sh:n + 16 - sh], op=ALU.add),
                         reads=(r_cur,), writes=(r_dst,))
                    cur, r_cur, lo = dstb, r_dst, lo2
                for m in range(2):
                    ch = 2 * g + m
                    tmp = ctmp[:, m, :]
                    k.op("dve", lambda e, cur=cur, m=m, g=g, tmp=tmp: e.tensor_tensor(out=tmp[:, :n], in0=cur[:, m, 16:16 + n], in1=invc[:, g, :n], op=ALU.mult),
                         reads=(r_cur, r_invc), writes=(r_ctmp,))
                    k.op("dve", lambda e, ch=ch, tmp=tmp: e.tensor_tensor(out=pooled[:, ch, :n], in0=tmp[:, :n], in1=pin[:, ch, 16:16 + n], op=ALU.subtract),
                         reads=(r_ctmp, r_pin), writes=(r_pooled,))
            def pw_loader(s):
                return [(s[:, 0:2048].rearrange("p (g k n) -> p g k n", g=4, k=2), I["pool_w"][li].rearrange("g (k p) n -> p g k n", p=128))]
            pw_loader.key = ("pool_w", li)
            pw_loader.nel = 2048
            s, r_s = ws.get(pw_loader)
            sv = s[:, 0:2048].rearrange("p (g k n) -> p g k n", g=4, k=2)
            for g in range(4):
                for m in range(2):
                    ps, r_ps = ps_next()
                    k.mm([(ps[:, :n], sv[:, g, kc, m * 128:(m + 1) * 128], pooled[:, 2 * g + kc, :n], kc == 0, kc == 1) for kc in range(2)],
                         reads=(r_s, r_pooled), writes=(r_ps,))
                    ch = 2 * g + m
                    k.op("act", lambda e, ps=ps, ch=ch: e.mul(out=ycT[:, ch, :n], in_=ps[:, :n], mul=pcol(li, P_PS + ch)),
                         reads=(r_ps, r_par), writes=(r_ycT,))

        def merge_and_wo(li, n):
            for jj in range(8):
                for i in range(3):
                    sg, r_sg = ws.get(w_in_loader(li, C_GATE + 2048 * i + 256 * jj, 256))
                    svg = slot_view(sg, NCD, 256)
                    wsrc, nk, src, r_src = [(I["w_a"], 8, yaT, r_yaT), (I["w_b"], 16, ybT, r_ybT), (I["w_c"], 8, ycT, r_ycT)][i]
                    sw_, r_sw = ws.get(gen_loader(wsrc[li][:, 256 * jj:256 * (jj + 1)], nk, 256, ("w_br", li, i, jj)), keep=1)
                    svb = slot_view(sw_, nk, 256)
                    for m in range(2):
                        ps, r_ps = ps_next()
                        k.mm([(ps[:, :n], svg[:, kc, m * 128:(m + 1) * 128], hT[:, kc, :n], kc == 0, kc == NCD - 1) for kc in range(NCD)],
                             reads=(r_sg, r_hT), writes=(r_ps,))
                        k.op("act", lambda e, ps=ps, i=i: e.activation(out=gate[i][:, :n], in_=ps[:, :n], func=AF.Sigmoid), reads=(r_ps,), writes=(r_gate[i],))
                        ps2, r_ps2 = ps_next()
                        k.mm([(ps2[:, :n], svb[:, kc, m * 128:(m + 1) * 128], src[:, kc, :n], kc == 0, kc == nk - 1) for kc in range(nk)],
                             reads=(r_sw, r_src), writes=(r_ps2,))
                        mt, r_mt = mtmp[2 * i + m], r_mtmp[2 * i + m]
                        k.op("dve", lambda e, ps2=ps2, i=i, mt=mt: e.tensor_tensor(out=mt[:, :n], in0=ps2[:, :n], in1=gate[i][:, :n], op=ALU.mult),
                             reads=(r_ps2, r_gate[i]), writes=(r_mt,))
                for m in range(2):
                    oc = 2 * jj + m
                    k.op("pool", lambda e, m=m: e.tensor_tensor(out=mtmp[m][:, :n], in0=mtmp[m][:, :n], in1=mtmp[2 + m][:, :n], op=ALU.add),
                         reads=(r_mtmp[m], r_mtmp[2 + m]), writes=(r_mtmp[m],))
                    k.op("pool", lambda e, oc=oc, m=m: e.tensor_tensor(out=merged[:, oc, :n], in0=mtmp[m][:, :n], in1=mtmp[4 + m][:, :n], op=ALU.add),
                         reads=(r_mtmp[m], r_mtmp[4 + m]), writes=(r_merged,))
            for jj in range(8):
                s, r_s = ws.get(gen_loader(I["w_o"][li][:, 256 * jj:256 * (jj + 1)], 16, 256, ("w_o", li, jj)))
                sv = slot_view(s, 16, 256)
                for m in range(2):
                    oc = 2 * jj + m
                    ps, r_ps = ps_next()
                    k.mm([(ps[:, :n], sv[:, kc, m * 128:(m + 1) * 128], merged[:, kc, :n], kc == 0, kc == NCD - 1) for kc in range(NCD)],
                         reads=(r_s, r_merged), writes=(r_ps,))
                    k.op("dve", lambda e, ps=ps, oc=oc: e.tensor_tensor(out=xT[:, oc, :n], in0=ps[:, :n], in1=xT[:, oc, :n], op=ALU.add),
                         reads=(r_ps, r_xT), writes=(r_xT,))

        def mlp(li, n):
            norm_fm(xT, r_xT, NCD, n, sq16, r_sq16, li, P_MN, hT, r_hT, float(D), 2)
            for jj in range(32):
                s, r_s = ws.get(gen_loader(I["w_up"][li][:, 256 * jj:256 * (jj + 1)], 16, 256, ("w_up", li, jj)))
                sv = slot_view(s, 16, 256)
                for m in range(2):
                    oc = 2 * jj + m
                    ps, r_ps = ps_next()
                    k.mm([(ps[:, :n], sv[:, kc, m * 128:(m + 1) * 128], hT[:, kc, :n], kc == 0, kc == NCD - 1) for kc in range(NCD)],
                         reads=(r_s, r_hT), writes=(r_ps,))
                    rr, r_rr = rl[oc % 2], r_rl[oc % 2]
                    k.op("act", lambda e, ps=ps, rr=rr: e.activation(out=rr[:, :n], in_=ps[:, :n], func=AF.Relu), reads=(r_ps,), writes=(r_rr,))
                    k.op("pool", lambda e, rr=rr, oc=oc: e.tensor_tensor(out=act[:, oc, :n], in0=rr[:, :n], in1=rr[:, :n], op=ALU.mult),
                         reads=(r_rr,), writes=(r_act,))
            for oc in range(NCD):
                ps, r_ps = ps_next()
                for half in range(2):
                    s, r_s = ws.get(gen_loader(I["w_down"][li][4096 * half:4096 * (half + 1), 128 * oc:128 * (oc + 1)], 32, 128, ("w_down", li, oc, half)))
                    sv = slot_view(s, 32, 128)
                    k.mm([(ps[:, :n], sv[:, kc, :], act[:, 32 * half + kc, :n], (half == 0 and kc == 0), (half == 1 and kc == 31)) for kc in range(32)],
                         reads=(r_s, r_act), writes=(r_ps,))
                k.op("dve", lambda e, ps=ps, oc=oc: e.tensor_tensor(out=xT[:, oc, :n], in0=ps[:, :n], in1=xT[:, oc, :n], op=ALU.add),
                     reads=(r_ps, r_xT), writes=(r_xT,))

        def layer(li, src, r_src_list, kv_only_tiles, dst_fn):
            k.op("pool", lambda e: e.memset(halo_cu[:], 0.0), writes=(r_hcu,))
            k.op("pool", lambda e: e.memset(halo_pin[:], 0.0), writes=(r_hpin,))
            for t in range(2 * NT):
                n = TT
                k.dma("sp", xT[:, :, :n], src.rearrange("(c p) t -> p c t", p=128)[:, :, t * TT:t * TT + n],
                      reads=tuple(r_src_list[t:t + 1]), writes=(r_xT,))
                load_rope(t * TT, n)
                norm_fm(xT, r_xT, NCD, n, sq16, r_sq16, li, P_AN, hT, r_hT, float(D), 2)
                kv_path(li, n, t * TT, r_kv_d[li][t])
                if t < kv_only_tiles:
                    if t == kv_only_tiles - 1:
                        mixer_inputs(li, n, n - 16, False)
                        save_halo(n)
                    continue
                q_path(li, n)
                attention(li, t, n)
                restore_halo(t == NT)
                mixer_inputs(li, n, 0, True)
                save_halo(n)
                conv_pool(li, t, n)
                merge_and_wo(li, n)
                mlp(li, n)
                d_ap, r_d = dst_fn(t)
                k.dma("sp", d_ap, xT[:, :, :n], reads=(r_xT,), writes=(r_d,))

        def program():
            ring["p"] = 0
            ring["set"] = list(range(8))
            load_consts()
            for li in range(NL):
                last = (li == NL - 1)
                if li == 0:
                    src, r_src = I["xall"], []
                    kvo = KV_ONLY0
                else:
                    src, r_src = X1_d[li - 1], r_x1[li - 1]
                    kvo = NT
                if last:
                    dst_fn = lambda t: (OUT.rearrange("(c p) t -> p c t", p=128)[:, :, (t - NT) * TT:(t - NT + 1) * TT], r_out[t - NT])
                else:
                    dst_fn = lambda t, li=li: (X1_d[li].rearrange("(c p) t -> p c t", p=128)[:, :, t * TT:(t + 1) * TT], r_x1[li][t])
                layer(li, src, r_src, kvo, dst_fn)
            k.finish(r_out)

        k.dry = True
        program()
        ws.pos = 0
        k.dry = False
        program()
        print("[kernel] sbuf bytes remaining", nc.sbuf_bytes_remaining, flush=True)
        print(f"[kernel] instructions={k.n_ins} waits={k.n_wait} weight_slots={len(ws.reqs)}", flush=True)
    return nc


def _rope_tables():
    pos = np.arange(TALL, dtype=np.float32)
    inv = (np.float32(10000.0) ** (-np.arange(0, 64, 2, dtype=np.float32) / np.float32(64))).astype(np.float32)
    ang = (pos[:, None] * inv[None, :]).astype(np.float32)
    c, s = np.cos(ang).astype(np.float32), np.sin(ang).astype(np.float32)
    C = np.concatenate([c, c], 1).T
    S = np.concatenate([-s, s], 1).T
    return np.ascontiguousarray(C), np.ascontiguousarray(S)


def _params(inp, l):
    P = np.zeros((128, NPARAM), np.float32)
    P[:, P_AN:P_AN + 16] = inp["attn_norm"][l].reshape(16, 128).T
    P[:, P_MN:P_MN + 16] = inp["mlp_norm"][l].reshape(16, 128).T
    P[:, P_QLN:P_QLN + 4] = inp["q_lat_norm"][l].reshape(4, 128).T
    P[:, P_KVLN:P_KVLN + 4] = inp["kv_lat_norm"][l].reshape(4, 128).T
    P[:, P_QN] = inp["q_norm"][l][:128]
    P[:64, P_QR] = inp["q_norm"][l][128:]
    P[:, P_KN] = inp["k_norm"][l][:128]
    P[:64, P_KR] = inp["k_norm"][l][128:]
    cw = inp["conv_w"][l]
    for ch in range(8):
        for j in range(3):
            P[:, P_CW + 3 * ch + j] = cw[j, ch * 128:(ch + 1) * 128]
    P[:, P_PS:P_PS + 8] = inp["pool_scale"][l].reshape(8, 128).T
    return P


_NC_CACHE = {}


def _get_nc(layers, fused):
    key = (tuple(layers), fused)
    if key not in _NC_CACHE:
        _NC_CACHE[key] = build_program(list(layers), fused)
    return _NC_CACHE[key]


def _run(layers, inp, x_own, x_pre, fused):
    nc = _get_nc(layers, fused)
    C, S = _rope_tables()
    bf = ml_dtypes.bfloat16
    tri = (np.arange(128)[None, :] >= np.arange(128)[:, None]).astype(bf)
    ones = np.ones((128, 128), bf)
    swap = np.zeros((64, 64), bf)
    for i in range(32):
        swap[i + 32, i] = 1
        swap[i, i + 32] = 1
    ls = list(layers)
    common = {
        "params": np.stack([_params(inp, l) for l in ls]),
        "tri": tri, "ones": ones, "swap": swap,
    }
    for i, l in enumerate(ls):
        for nm in ["w_in", "w_uq", "w_ukv", "pool_w", "w_branch_a", "w_branch_b", "w_branch_c", "w_o", "w_up", "w_down"]:
            common[f"{nm}{i}"] = np.ascontiguousarray(inp[nm][l])
    in_maps = []
    for c in range(8):
        r = c % 2
        pos0 = r * NTOK
        ropeC = np.concatenate([C[:, 0:NPRE], C[:, pos0:pos0 + NTOK]], 1)
        ropeS = np.concatenate([S[:, 0:NPRE], S[:, pos0:pos0 + NTOK]], 1)
        tpos = np.concatenate([np.arange(0, NPRE), np.arange(pos0, pos0 + NTOK)]).astype(np.float32) + 1.0
        invc = np.stack([np.float32(1.0) / np.minimum(tpos, np.float32(w)) for w in (2, 4, 8, 16)]).astype(np.float32)
        invc = np.ascontiguousarray(np.broadcast_to(invc[None], (128, 4, NKEY)))
        kb = np.zeros((128, NKB), np.float32)
        if r == 0:
            keyidx = np.arange(NKB * 128).reshape(NKB, 128).T
            kb[keyidx < NPRE] = -30000.0
        m = dict(common)
        m.update({"xall": np.ascontiguousarray(np.concatenate([x_pre[c], x_own[c]], axis=1)),
                  "ropeC": np.ascontiguousarray(ropeC), "ropeS": np.ascontiguousarray(ropeS), "invc": invc,
                  "kbias": kb, "flag": np.full((128, 1), float(r), np.float32)})
        in_maps.append(m)
    res = run_bass_kernel_spmd(nc, in_maps, core_ids=list(range(8)))
    if DEBUG:
        global DBG_OUT
        DBG_OUT = [{nm: np.asarray(res.results[c][nm]) for nm in DBG_NAMES} for c in range(8)]
    return [np.asarray(res.results[c]["outT"]) for c in range(8)]


FUSED = True


def kernel(**inp):
    inp = {k_: np.asarray(v) for k_, v in inp.items()}
    x = inp["x"].astype(np.float32)
    B = x.shape[0]
    meta = np.broadcast_to(inp["meta_tokens"][None].astype(np.float32), (B, NMETA, D))
    hseq = np.concatenate([meta, x], axis=1)
    own = [np.ascontiguousarray(hseq[c // 2, (c % 2) * NTOK:(c % 2 + 1) * NTOK].T) for c in range(8)]
    zero = np.zeros((D, NPRE), np.float32)
    if FUSED:
        pre = [zero if c % 2 == 0 else own[c - 1] for c in range(8)]
        outs = _run([0, 1], inp, own, pre, True)
    else:
        cur = own
        for l in range(2):
            pre = [zero if c % 2 == 0 else cur[c - 1] for c in range(8)]
            cur = _run([l], inp, cur, pre, False)
        outs = cur
    full = np.stack([np.concatenate([outs[2 * b].T, outs[2 * b + 1].T], axis=0) for b in range(B)])
    return np.ascontiguousarray(full[:, NMETA:, :]).astype(np.float32)
```

```python
import contextlib
import numpy as np
import ml_dtypes
import concourse.bass as bass
import concourse.mybir as mybir
from concourse.bass_utils import run_bass_kernel_spmd

F32 = mybir.dt.float32
BF16 = mybir.dt.bfloat16
AF = mybir.ActivationFunctionType
ALU = mybir.AluOpType

D = 2048
NCD = 16
H = 16
SEQ = 4096
NMETA = 16
TALL = SEQ + NMETA
NTOK = TALL // 2
NPRE = NTOK
TT = 257
NT = NTOK // TT
NKEY = NPRE + NTOK
NKB = (NKEY + 127) // 128
SEG0_KB = 16
SEGK = 2064
D_IN = 11328
C_U, C_B, C_C, C_QL, C_KVL, C_KR, C_POOL, C_GATE = 0, 1024, 2048, 3072, 3584, 4096, 4160, 5184
DFF = 8192
EPS = 1e-6
SCALE = 192.0 ** -0.5
SLOT = 4096
NSLOT = 5
P_AN, P_MN, P_QLN, P_KVLN, P_QN, P_QR, P_KN, P_KR, P_CW, P_PS = 0, 16, 32, 36, 40, 41, 42, 43, 44, 68
NPARAM = 76


class Res:
    __slots__ = ("name", "excl", "w", "rs", "ov")

    def __init__(self, name, excl=False):
        self.name = name
        self.excl = excl
        self.w = None
        self.rs = {}
        self.ov = [self]


class Eng:
    def __init__(self, name, eng, sem):
        self.name, self.eng, self.sem = name, eng, sem
        self.cnt = 0
        self.seen = {}


class KB:
    def __init__(self, nc, es):
        self.nc = nc
        self.dry = False
        mk = lambda n: es.enter_context(nc.semaphore(n))
        self.E = {
            "pe": Eng("pe", nc.tensor, mk("s_pe")),
            "act": Eng("act", nc.scalar, mk("s_act")),
            "dve": Eng("dve", nc.vector, mk("s_dve")),
            "pool": Eng("pool", nc.gpsimd, mk("s_pool")),
            "sp": Eng("sp", nc.sync, mk("s_sp")),
        }
        self.dma_sems = {"sp": [mk(f"d_sp{i}") for i in range(12)], "pool": [mk(f"d_pl{i}") for i in range(8)]}
        self.dma_cnt = {}
        self.dma_rr = {"sp": 0, "pool": 0}
        self.n_wait = 0
        self.n_ins = 0

    def _wait(self, e, tk):
        sem, val = tk
        if e.seen.get(sem.num, 0) >= val:
            return
        e.eng.wait_ge(sem, val)
        e.seen[sem.num] = val
        self.n_wait += 1

    def _deps(self, reads, writes):
        tks = []
        for r in reads:
            for res in r.ov:
                if res.w is not None:
                    tks.append((res.w, False))
                if res.excl:
                    tks.extend((t, True) for t in res.rs.values())
        for w in writes:
            for res in w.ov:
                if res.w is not None:
                    tks.append((res.w, False))
                tks.extend((t, False) for t in res.rs.values())
        return tks

    def _mark(self, tk, reads, writes):
        for r in reads:
            r.rs[tk[0].num] = tk
        for w in writes:
            w.w = tk
            w.rs = {}

    def op(self, en, fn, reads=(), writes=()):
        if self.dry:
            return
        e = self.E[en]
        for tk, rr in self._deps(reads, writes):
            if tk[0] is e.sem and (rr or en == "pe"):
                continue
            self._wait(e, tk)
        ins = fn(e.eng)
        e.cnt += 1
        ins.then_inc(e.sem, 1)
        self.n_ins += 1
        self._mark((e.sem, e.cnt), reads, writes)

    def prewait(self, en, reads=(), writes=()):
        if self.dry:
            return
        e = self.E[en]
        best = {}
        for tk, rr in self._deps(reads, writes):
            if tk[0] is e.sem:
                continue
            if tk[0].num not in best or best[tk[0].num][1] < tk[1]:
                best[tk[0].num] = tk
        for tk in best.values():
            self._wait(e, tk)

    def mm(self, mms, reads, writes):
        if self.dry:
            return
        e = self.E["pe"]
        for tk, rr in self._deps(reads, writes):
            if tk[0] is e.sem:
                continue
            self._wait(e, tk)
        ins = None
        for (o, l, r, st, sp) in mms:
            ins = e.eng.matmul(o, lhsT=l, rhs=r, start=st, stop=sp)
            self.n_ins += 1
        e.cnt += 1
        ins.then_inc(e.sem, 1)
        self._mark((e.sem, e.cnt), reads, writes)

    def dma(self, q, out, in_, reads=(), writes=()):
        if self.dry:
            return
        e = self.E[q]
        for tk, rr in self._deps(reads, writes):
            self._wait(e, tk)
        pool = self.dma_sems[q]
        i = self.dma_rr[q]
        self.dma_rr[q] = (i + 1) % len(pool)
        sem = pool[i]
        prev = self.dma_cnt.get(sem.num, 0)
        if prev:
            self._wait(e, (sem, prev))
        ins = e.eng.dma_start(out=out, in_=in_)
        ins.then_inc(sem, 16)
        self.dma_cnt[sem.num] = prev + 16
        self.n_ins += 1
        self._mark((sem, prev + 16), reads, writes)

    def finish(self, res_list):
        e = self.E["sp"]
        for r in res_list:
            if r.w is not None:
                self._wait(e, r.w)


class Region:
    def __init__(self, tile, nelem):
        self.t = tile
        self.n = nelem
        self.off = 0
        self.phase = None
        self.bufs = []

    def set_phase(self, ph):
        self.phase = ph
        self.off = 0

    def take(self, name, free_shape, dtype):
        n = int(np.prod(free_shape))
        nb = n * (2 if dtype == F32 else 1)
        self.off = (self.off + 15) // 16 * 16
        off = self.off
        self.off += nb
        assert self.off <= self.n, (name, self.off, self.n)
        v = self.t[:, off:off + nb]
        if dtype == F32:
            v = v.bitcast(F32)
        if len(free_shape) == 2:
            v = v.rearrange("p (a b) -> p a b", a=free_shape[0])
        elif len(free_shape) == 3:
            v = v.rearrange("p (a b c) -> p a b c", a=free_shape[0], b=free_shape[1])
        r = Res(name)
        for (ph, r2) in self.bufs:
            if ph != self.phase:
                r.ov.append(r2)
                r2.ov.append(r)
        self.bufs.append((self.phase, r))
        return v, r


class WStream:
    def __init__(self, k, slots, slot_res, cache_fn):
        self.k = k
        self.slots = slots
        self.res = slot_res
        self.reqs = []
        self.issued = 0
        self.pos = 0
        self.cache_fn = cache_fn
        self.cache_idx = {}
        self.cache_res = {}
        self.loaded = set()

    def get(self, loader, keep=0):
        k = self.k
        i = self.pos
        self.pos += 1
        if k.dry:
            self.reqs.append(loader)
            if loader.key not in self.cache_idx:
                self.cache_idx[loader.key] = len(self.cache_idx)
                self.cache_res[loader.key] = Res("wc%d" % len(self.cache_idx))
            return self.slots[i % NSLOT], self.res[i % NSLOT]
        assert keep < NSLOT - 1
        lim = min(len(self.reqs), i - keep + NSLOT)
        while self.issued < lim:
            j = self.issued
            s, r = self.slots[j % NSLOT], self.res[j % NSLOT]
            ld = self.reqs[j]
            ci = self.cache_idx[ld.key]
            cr = self.cache_res[ld.key]
            if ld.key not in self.loaded:
                self.loaded.add(ld.key)
                for (o, src) in ld(s):
                    k.dma("pool", o, src, reads=(), writes=(r,))
                k.dma("sp", self.cache_fn(ci)[:, 0:ld.nel], s[:, 0:ld.nel], reads=(r,), writes=(cr,))
            else:
                k.dma("sp" if (j % 2) else "pool", s[:, 0:ld.nel], self.cache_fn(ci)[:, 0:ld.nel], reads=(cr,), writes=(r,))
            self.issued += 1
        return self.slots[i % NSLOT], self.res[i % NSLOT]


def slot_view(s, kc, ncols):
    return s[:, 0:kc * ncols].rearrange("p (k n) -> p k n", k=kc)


DEBUG = False
DBG_NAMES = []


def build_program(layers, fused):
    KV_ONLY0 = 0 if len(layers) > 1 else NT
    nc = bass.Bass("TRN2", target_bir_lowering=False)
    NL = len(layers)
    dt = lambda name, shape, dtype=F32, kind="ExternalInput": nc.dram_tensor(name, list(shape), dtype, kind=kind).ap()
    I = {}
    I["xall"] = dt("xall", [D, NKEY])
    I["w_in"] = [dt(f"w_in{l}", [D, D_IN]) for l in range(NL)]
    I["w_uq"] = [dt(f"w_uq{l}", [512, 3072]) for l in range(NL)]
    I["w_ukv"] = [dt(f"w_ukv{l}", [512, 4096]) for l in range(NL)]
    I["pool_w"] = [dt(f"pool_w{l}", [4, 256, 256]) for l in range(NL)]
    I["w_a"] = [dt(f"w_branch_a{l}", [1024, D]) for l in range(NL)]
    I["w_b"] = [dt(f"w_branch_b{l}", [D, D]) for l in range(NL)]
    I["w_c"] = [dt(f"w_branch_c{l}", [1024, D]) for l in range(NL)]
    I["w_o"] = [dt(f"w_o{l}", [D, D]) for l in range(NL)]
    I["w_up"] = [dt(f"w_up{l}", [D, DFF]) for l in range(NL)]
    I["w_down"] = [dt(f"w_down{l}", [DFF, D]) for l in range(NL)]
    I["params"] = dt("params", [NL, 128, NPARAM])
    I["ropeC"] = dt("ropeC", [64, NKEY])
    I["ropeS"] = dt("ropeS", [64, NKEY])
    I["invc"] = dt("invc", [128, 4, NKEY])
    I["kbias"] = dt("kbias", [128, NKB])
    I["flag"] = dt("flag", [128, 1])
    I["tri"] = dt("tri", [128, 128], BF16)
    I["ones"] = dt("ones", [128, 128], BF16)
    I["swap"] = dt("swap", [64, 64], BF16)
    OUT = dt("outT", [D, NTOK], F32, kind="ExternalOutput")
    KTn_d = [dt(f"ktn{l}", [H, 128, NKEY], BF16, kind="Internal") for l in range(NL)]
    KTr_d = [dt(f"ktr{l}", [H, 64, NKEY], BF16, kind="Internal") for l in range(NL)]
    V_d = [dt(f"v{l}", [H, 128, NKB, 128], BF16, kind="Internal") for l in range(NL)]
    X1_d = [dt(f"x1_{l}", [D, NKEY], F32, kind="Internal") for l in range(NL - 1)]

    with contextlib.ExitStack() as es:
        k = KB(nc, es)
        sb = lambda name, shape, dtype: es.enter_context(nc.sbuf_tensor(name, list(shape), dtype))
        xT = sb("xT_sb", [128, NCD, TT], F32); r_xT = Res("xT")
        hT = sb("hT_sb", [128, NCD, TT], BF16); r_hT = Res("hT")
        ybT = sb("ybT_sb", [128, H, TT], BF16); r_ybT = Res("ybT")
        wsl = [sb(f"wslot{i}", [128, SLOT], BF16) for i in range(NSLOT)]
        r_wsl = [Res(f"wslot{i}") for i in range(NSLOT)]
        params = sb("params_sb", [128, NL, NPARAM], F32); r_par = Res("params")
        tri = sb("tri_sb", [128, 128], BF16)
        ones = sb("ones_sb", [128, 128], BF16)
        swp = sb("swap_sb", [64, 64], BF16)
        kbias = sb("kbias_sb", [128, NKB], F32)
        flag = sb("flag_sb", [128, 1], F32)
        epsb = sb("eps_sb", [128, 1], F32)
        zb = sb("zb_sb", [128, 1], F32)
        r_const = Res("consts")
        rstd = [sb(f"rstd{i}", [128, TT], F32) for i in range(3)]
        r_rstd = [Res(f"rstd{i}") for i in range(3)]
        lnv = [sb(f"lnv{i}", [128, TT], F32) for i in range(2)]
        r_lnv = [Res(f"lnv{i}") for i in range(2)]
        ropeC = sb("ropeC_sb", [64, TT], F32)
        ropeS = sb("ropeS_sb", [64, TT], F32)
        r_rope = Res("rope")
        invc = sb("invc_sb", [128, 4, TT], F32); r_invc = Res("invc")
        PTall = sb("ptall", [128, 9, TT], BF16)
        PT = [PTall[:, i, :] for i in range(9)]
        r_PT = [Res(f"pt{i}") for i in range(9)]
        rden = sb("rden", [128, TT], F32); r_rden = Res("rden")
        gate = [sb(f"gate{i}", [128, TT], F32) for i in range(3)]
        r_gate = [Res(f"gate{i}") for i in range(3)]
        mtmp = [sb(f"mtmp{i}", [128, TT], F32) for i in range(6)]
        r_mtmp = [Res(f"mtmp{i}") for i in range(6)]
        rl = [sb(f"relu{i}", [128, TT], F32) for i in range(2)]
        r_rl = [Res(f"relu{i}") for i in range(2)]
        halo_cu = sb("halo_cu", [128, 8, 2], BF16); r_hcu = Res("halo_cu")
        halo_pin = sb("halo_pin", [128, 8, 16], F32); r_hpin = Res("halo_pin")
        U1N = 24 * 1024
        U2N = 28 * 1024
        U1 = Region(sb("U1", [128, U1N], BF16), U1N)
        U2 = Region(sb("U2", [128, U2N], BF16), U2N)
        U1.set_phase("kv")
        kvl, r_kvl = U1.take("kvl", [5, TT], F32)
        sq5, r_sq5 = U1.take("sq5", [5, TT], BF16)
        kvnT, r_kvnT = U1.take("kvnT", [4, TT], BF16)
        KTn_all, r_KTn = U1.take("KTn_all", [H, TT], BF16)
        KTr_all, r_KTr = U1.take("KTr_all", [H, TT], BF16)
        Vsb, r_Vsb = U1.take("Vsb", [3, D], BF16)
        krg, r_krg = U1.take("krg", [TT], F32)
        kx, r_kx = U1.take("kx", [2, TT], F32)
        kxb, r_kxb = U1.take("kxb", [TT], BF16)
        sqh, r_sqh = U1.take("sqh", [2, 2, TT], BF16)
        U1.set_phase("mix")
        usb, r_usb = U1.take("usb", [2, TT], F32)
        cu, r_cu = U1.take("cu", [8, TT + 2], BF16)
        bsb, r_bsb = U1.take("bsb", [8, TT], BF16)
        ctmp, r_ctmp = U1.take("ctmp", [2, TT], F32)
        yaT, r_yaT = U1.take("yaT", [8, TT], BF16)
        pin, r_pin = U1.take("pin", [8, TT + 16], F32)
        pw1, r_pw1 = U1.take("pw1", [2, TT + 16], F32)
        pw2, r_pw2 = U1.take("pw2", [2, TT + 16], F32)
        pooled, r_pooled = U1.take("pooled", [8, TT], BF16)
        ycT, r_ycT = U1.take("ycT", [8, TT], BF16)
        merged, r_merged = U1.take("merged", [NCD, TT], BF16)
        U2.set_phase("att")
        qlat, r_qlat = U2.take("qlat", [4, TT], F32)
        sq4, r_sq4 = U2.take("sq4", [4, TT], BF16)
        qnT, r_qnT = U2.take("qnT", [4, TT], BF16)
        QTn, r_QTn = U2.take("QTn", [H, TT], BF16)
        QTr, r_QTr = U2.take("QTr", [H, TT], BF16)
        qx, r_qx = U2.take("qx", [2, TT], F32)
        qxb, r_qxb = U2.take("qxb", [TT], BF16)
        sqq, r_sqq = U2.take("sqq", [2, 2, TT], BF16)
        Kn_s = []; Kr_s = []; V_s = []
        for i in range(2):
            a, ra = U2.take(f"Kn_s{i}", [SEGK], BF16)
            b_, rb = U2.take(f"Kr_s{i}", [SEGK], BF16)
            c_, rc = U2.take(f"V_s{i}", [17, 128], BF16)
            Kn_s.append((a, ra)); Kr_s.append((b_, rb)); V_s.append((c_, rc))
        U2.set_phase("mlp")
        sq16, r_sq16 = U2.take("sq16", [NCD, TT], BF16)
        act, r_act = U2.take("act", [64, TT], BF16)
        psall = es.enter_context(nc.psum_tensor("psall", [128, 8 * 512], F32))
        banks = [psall[:, 512 * i:512 * (i + 1)] for i in range(8)]
        r_bank = [Res(f"bank{i}", excl=True) for i in range(8)]
        ring = {"set": list(range(8)), "p": 0}

        def ps_next():
            i = ring["set"][ring["p"] % len(ring["set"])]
            ring["p"] += 1
            return banks[i], r_bank[i]

        WC_d = [dt(f"wcache{i}", [150, 128, SLOT], BF16, kind="Internal") for i in range(NL)]
        ws = WStream(k, wsl, r_wsl, lambda ci: WC_d[ci // 150][ci % 150])
        cc_sem = es.enter_context(nc.semaphore("cc_sem"))

        def dump(name, ap, shape, dtype, r):
            if not DEBUG or k.dry:
                return
            dram = nc.dram_tensor("dbg_" + name, list(shape), dtype, kind="ExternalOutput").ap()
            DBG_NAMES.append("dbg_" + name)
            k.dma("sp", dram, ap, reads=(r,), writes=(Res("dbg_" + name),))

        r_kv_d = [[Res(f"kvd{l}_{t}") for t in range(2 * NT)] for l in range(NL)]
        r_x1 = [[Res(f"x1_{l}_{t}") for t in range(2 * NT)] for l in range(NL)]
        r_out = [Res(f"out_{t}") for t in range(NT)]

        def V(fn, **kw):
            return lambda e: fn(e, **kw)

        def load_consts():
            k.dma("sp", params[:], I["params"].rearrange("l p c -> p l c"), writes=(r_par,))
            k.dma("sp", tri[:], I["tri"], writes=(r_const,))
            k.dma("sp", ones[:], I["ones"], writes=(r_const,))
            k.dma("sp", swp[:], I["swap"], writes=(r_const,))
            k.dma("sp", kbias[:], I["kbias"], writes=(r_const,))
            k.dma("sp", flag[:], I["flag"], writes=(r_const,))
            k.op("dve", lambda e: e.memset(epsb[:], EPS), writes=(r_const,))
            k.op("dve", lambda e: e.memset(zb[:], 0.0), writes=(r_const,))

        def pcol(li, c):
            return params[:, li, c:c + 1]

        def rstd_from_ps(ps, r_ps, n, dim, slot):
            lv, r_lv = lnv[slot % 2], r_lnv[slot % 2]
            rs, r_rs = rstd[slot], r_rstd[slot]
            k.op("act", lambda e: e.activation(out=lv[:, :n], in_=ps[:, :n], func=AF.Ln, bias=epsb[:], scale=1.0 / dim),
                 reads=(r_ps, r_const), writes=(r_lv,))
            k.op("act", lambda e: e.activation(out=rs[:, :n], in_=lv[:, :n], func=AF.Exp, scale=-0.5),
                 reads=(r_lv,), writes=(r_rs,))
            return rs, r_rs

        def norm_fm(src, r_src, nch, n, sq, r_sq, li, gcol, dst, r_dst, dim, slot):
            for c in range(nch):
                k.op("act", lambda e, c=c: e.activation(out=sq[:, c, :n], in_=src[:, c, :n], func=AF.Square),
                     reads=(r_src,), writes=(r_sq,))
            ps, r_ps = ps_next()
            k.mm([(ps[:, :n], ones[:, :], sq[:, c, :n], c == 0, c == nch - 1) for c in range(nch)],
                 reads=(r_sq, r_const), writes=(r_ps,))
            rs, r_rs = rstd_from_ps(ps, r_ps, n, dim, slot)
            for c in range(nch):
                k.op("dve", lambda e, c=c: e.scalar_tensor_tensor(out=dst[:, c, :n], in0=src[:, c, :n], scalar=pcol(li, gcol + c),
                                                                  in1=rs[:, :n], op0=ALU.mult, op1=ALU.mult),
                     reads=(r_src, r_rs, r_par), writes=(r_dst,))

        def w_in_loader(li, c0, ncols):
            def f(s):
                return [(slot_view(s, NCD, ncols), I["w_in"][li][:, c0:c0 + ncols].rearrange("(k p) n -> p k n", p=128))]
            f.key = ("w_in", li, c0, ncols)
            f.nel = NCD * ncols
            return f

        def gen_loader(ap2d, kc, ncols, key):
            def f(s):
                return [(slot_view(s, kc, ncols), ap2d.rearrange("(k p) n -> p k n", p=128))]
            f.key = key
            f.nel = kc * ncols
            return f

        def rope_apply(xps, r_xps, n, li, gcolr, x_f, r_x_f, xb, r_xb, dst_fn):
            k.op("act", lambda e: e.mul(out=xb[0:64, :n], in_=xps[0:64, :n], mul=pcol(li, gcolr)[0:64, :]),
                 reads=(r_xps, r_par), writes=(r_xb,))
            k.op("dve", lambda e: e.scalar_tensor_tensor(out=x_f[0:64, 0, :n], in0=xps[0:64, :n], scalar=pcol(li, gcolr)[0:64, :],
                                                         in1=ropeC[0:64, :n], op0=ALU.mult, op1=ALU.mult),
                 reads=(r_xps, r_par, r_rope), writes=(r_x_f,))
            ps2, r_ps2 = ps_next()
            k.mm([(ps2[0:64, :n], swp[:, :], xb[0:64, :n], True, True)], reads=(r_xb, r_const), writes=(r_ps2,))
            k.op("dve", lambda e: e.tensor_tensor(out=x_f[0:64, 1, :n], in0=ps2[0:64, :n], in1=ropeS[0:64, :n], op=ALU.mult),
                 reads=(r_ps2, r_rope, r_x_f), writes=(r_x_f,))
            k.op("pool", lambda e: e.tensor_tensor(out=x_f[0:64, 0, :n], in0=x_f[0:64, 0, :n], in1=x_f[0:64, 1, :n], op=ALU.add),
                 reads=(r_x_f,), writes=(r_x_f,))

        def load_rope(key0, n):
            k.dma("sp", ropeC[:, :n], I["ropeC"][:, key0:key0 + n], writes=(r_rope,))
            k.dma("sp", ropeS[:, :n], I["ropeS"][:, key0:key0 + n], writes=(r_rope,))

        def kv_path(li, n, key0, r_kvdst):
            for g in range(2):
                s, r_s = ws.get(w_in_loader(li, C_KVL + 256 * g, 256))
                sv = slot_view(s, NCD, 256)
                for m in range(2):
                    ps, r_ps = ps_next()
                    k.mm([(ps[:, :n], sv[:, kc, m * 128:(m + 1) * 128], hT[:, kc, :n], kc == 0, kc == NCD - 1) for kc in range(NCD)],
                         reads=(r_s, r_hT), writes=(r_ps,))
                    c = 2 * g + m
                    k.op("dve", lambda e, c=c, ps=ps: e.tensor_copy(out=kvl[:, c, :n], in_=ps[:, :n]), reads=(r_ps,), writes=(r_kvl,))
            s, r_s = ws.get(w_in_loader(li, C_KR, 64))
            sv = slot_view(s, NCD, 64)
            psr, r_psr = ps_next()
            k.mm([(psr[0:64, :n], sv[:, kc, :], hT[:, kc, :n], kc == 0, kc == NCD - 1) for kc in range(NCD)],
                 reads=(r_s, r_hT), writes=(r_psr,))
            k.op("act", lambda e: e.activation(out=sq5[0:64, 4, :n], in_=psr[0:64, :n], func=AF.Square), reads=(r_psr,), writes=(r_sq5,))
            rope_apply(psr, r_psr, n, li, P_KR, kx, r_kx, kxb, r_kxb, None)
            k.op("pool", lambda e: e.tensor_copy(out=krg[0:64, :n], in_=kx[0:64, 0, :n]), reads=(r_kx,), writes=(r_krg,))
            norm_fm(kvl, r_kvl, 4, n, sq5, r_sq5, li, P_KVLN, kvnT, r_kvnT, 512.0, 2)
            ntb = (n + 127) // 128
            for hg in range(4):
                s, r_s = ws.get(gen_loader(I["w_ukv"][li][:, 1024 * hg:1024 * (hg + 1)], 4, 1024, ("w_ukv", li, hg)))
                sv = slot_view(s, 4, 1024)
                for hh in range(4):
                    h = 4 * hg + hh
                    ps, r_ps = ps_next()
                    k.mm([(ps[:, :n], sv[:, kc, 256 * hh:256 * hh + 128], kvnT[:, kc, :n], kc == 0, kc == 3) for kc in range(4)],
                         reads=(r_s, r_kvnT), writes=(r_ps,))
                    sq, r_sq = sqh[:, h % 2, 0, :], r_sqh
                    k.op("act", lambda e, ps=ps, sq=sq: e.activation(out=sq[:, :n], in_=ps[:, :n], func=AF.Square),
                         reads=(r_ps,), writes=(r_sq,))
                    pss, r_pss = ps_next()
                    k.mm([(pss[:, :n], ones[:, :], sq[:, :n], True, False),
                          (pss[:, :n], ones[0:64, :], sq5[0:64, 4, :n], False, True)],
                         reads=(r_sq, r_sq5, r_const), writes=(r_pss,))
                    rs, r_rs = rstd_from_ps(pss, r_pss, n, 192.0, h % 2)
                    k.op("dve", lambda e, ps=ps, rs=rs, h=h: e.scalar_tensor_tensor(out=KTn_all[:, h, :n], in0=ps[:, :n], scalar=pcol(li, P_KN),
                                                                                   in1=rs[:, :n], op0=ALU.mult, op1=ALU.mult),
                         reads=(r_ps, r_rs, r_par), writes=(r_KTn,))
                    k.op("pool", lambda e, rs=rs, h=h: e.tensor_tensor(out=KTr_all[0:64, h, :n], in0=krg[0:64, :n], in1=rs[0:64, :n], op=ALU.mult),
                         reads=(r_krg, r_rs), writes=(r_KTr,))
                for tb in range(ntb):
                    m = min(128, n - 128 * tb)
                    ps, r_ps = ps_next()
                    mms = []
                    for hh in range(4):
                        for kc in range(4):
                            mms.append((ps[0:m, 128 * hh:128 * (hh + 1)], kvnT[:, kc, 128 * tb:128 * tb + m],
                                        sv[:, kc, 256 * hh + 128:256 * hh + 256], kc == 0, kc == 3))
                    k.mm(mms, reads=(r_s, r_kvnT), writes=(r_ps,))
                    k.op("act", lambda e, ps=ps, m=m, tb=tb, hg=hg: e.activation(out=Vsb[0:m, tb, 512 * hg:512 * (hg + 1)], in_=ps[0:m, 0:512], func=AF.Copy),
                         reads=(r_ps,), writes=(r_Vsb,))
            k.dma("sp", KTn_d[li][:, :, key0:key0 + n].rearrange("h p t -> p h t"), KTn_all[:, :, :n], reads=(r_KTn,), writes=(r_kvdst,))
            k.dma("sp", KTr_d[li][:, :, key0:key0 + n].rearrange("h p t -> p h t"), KTr_all[0:64, :, :n], reads=(r_KTr,), writes=(r_kvdst,))
            cuts = sorted(set([0, n] + [128 * i for i in range(1, ntb)] + [g * 128 - key0 for g in range(key0 // 128 + 1, (key0 + n) // 128 + 1) if 0 < g * 128 - key0 < n]))
            for a_, b_ in zip(cuts[:-1], cuts[1:]):
                tb, lp0 = divmod(a_, 128)
                gb, gp0 = divmod(key0 + a_, 128)
                np_ = b_ - a_
                k.dma("sp", V_d[li][:, gp0:gp0 + np_, gb, :].rearrange("h p d -> p h d"),
                      Vsb[lp0:lp0 + np_, tb, :].rearrange("p (h d) -> p h d", h=H), reads=(r_Vsb,), writes=(r_kvdst,))

        def mixer_inputs(li, n, c0, with_b):
            nn = n - c0
            for pr in range(4):
                su, r_su = ws.get(w_in_loader(li, C_U + 256 * pr, 256))
                sc, r_sc = ws.get(w_in_loader(li, C_C + 256 * pr, 256), keep=1)
                for m in range(2):
                    ch = 2 * pr + m
                    ps, r_ps = ps_next()
                    svu = slot_view(su, NCD, 256)
                    k.mm([(ps[:, :nn], svu[:, kc, m * 128:(m + 1) * 128], hT[:, kc, c0:n], kc == 0, kc == NCD - 1) for kc in range(NCD)],
                         reads=(r_su, r_hT), writes=(r_ps,))
                    ub = usb[:, ch % 2, :]
                    k.op("act", lambda e, ps=ps, ub=ub: e.activation(out=ub[:, :nn], in_=ps[:, :nn], func=AF.Copy), reads=(r_ps,), writes=(r_usb,))
                    ps2, r_ps2 = ps_next()
                    svc = slot_view(sc, NCD, 256)
                    k.mm([(ps2[:, :nn], svc[:, kc, m * 128:(m + 1) * 128], hT[:, kc, c0:n], kc == 0, kc == NCD - 1) for kc in range(NCD)],
                         reads=(r_sc, r_hT), writes=(r_ps2,))
                    k.op("dve", lambda e, ps2=ps2, ub=ub, ch=ch: e.tensor_tensor(out=cu[:, ch, 2 + c0:2 + n], in0=ps2[:, :nn], in1=ub[:, :nn], op=ALU.mult),
                         reads=(r_ps2, r_usb), writes=(r_cu,))
                if with_b:
                    sbb, r_sbb = ws.get(w_in_loader(li, C_B + 256 * pr, 256))
                    svb = slot_view(sbb, NCD, 256)
                    for m in range(2):
                        ch = 2 * pr + m
                        ps3, r_ps3 = ps_next()
                        k.mm([(ps3[:, :nn], svb[:, kc, m * 128:(m + 1) * 128], hT[:, kc, c0:n], kc == 0, kc == NCD - 1) for kc in range(NCD)],
                             reads=(r_sbb, r_hT), writes=(r_ps3,))
                        k.op("act", lambda e, ps3=ps3, ch=ch: e.activation(out=bsb[:, ch, c0:n], in_=ps3[:, :nn], func=AF.Copy),
                             reads=(r_ps3,), writes=(r_bsb,))
            for pr in range(4):
                sp_, r_sp = ws.get(w_in_loader(li, C_POOL + 256 * pr, 256))
                svp = slot_view(sp_, NCD, 256)
                for m in range(2):
                    ch = 2 * pr + m
                    ps, r_ps = ps_next()
                    k.mm([(ps[:, :nn], svp[:, kc, m * 128:(m + 1) * 128], hT[:, kc, c0:n], kc == 0, kc == NCD - 1) for kc in range(NCD)],
                         reads=(r_sp, r_hT), writes=(r_ps,))
                    k.op("act", lambda e, ps=ps, ch=ch: e.activation(out=pin[:, ch, 16 + c0:16 + n], in_=ps[:, :nn], func=AF.Copy),
                         reads=(r_ps,), writes=(r_pin,))

        def save_halo(n):
            k.op("pool", lambda e: e.tensor_copy(out=halo_cu[:, :, :], in_=cu[:, :, n:n + 2]), reads=(r_cu,), writes=(r_hcu,))
            k.op("pool", lambda e: e.tensor_copy(out=halo_pin[:, :, 1:16], in_=pin[:, :, n + 1:n + 16]), reads=(r_pin,), writes=(r_hpin,))

        def restore_halo(scale_flag):
            if scale_flag:
                k.op("pool", lambda e: e.tensor_scalar(out=cu[:, :, 0:2], in0=halo_cu[:, :, :], scalar1=flag[:, 0:1], scalar2=None, op0=ALU.mult),
                     reads=(r_hcu, r_const), writes=(r_cu,))
                k.op("pool", lambda e: e.tensor_scalar(out=pin[:, :, 1:16], in0=halo_pin[:, :, 1:16], scalar1=flag[:, 0:1], scalar2=None, op0=ALU.mult),
                     reads=(r_hpin, r_const), writes=(r_pin,))
            else:
                k.op("pool", lambda e: e.tensor_copy(out=cu[:, :, 0:2], in_=halo_cu[:, :, :]), reads=(r_hcu,), writes=(r_cu,))
                k.op("pool", lambda e: e.tensor_copy(out=pin[:, :, 1:16], in_=halo_pin[:, :, 1:16]), reads=(r_hpin,), writes=(r_pin,))

        def q_path(li, n):
            for g in range(2):
                s, r_s = ws.get(w_in_loader(li, C_QL + 256 * g, 256))
                sv = slot_view(s, NCD, 256)
                for m in range(2):
                    ps, r_ps = ps_next()
                    k.mm([(ps[:, :n], sv[:, kc, m * 128:(m + 1) * 128], hT[:, kc, :n], kc == 0, kc == NCD - 1) for kc in range(NCD)],
                         reads=(r_s, r_hT), writes=(r_ps,))
                    c = 2 * g + m
                    k.op("dve", lambda e, c=c, ps=ps: e.tensor_copy(out=qlat[:, c, :n], in_=ps[:, :n]), reads=(r_ps,), writes=(r_qlat,))
            norm_fm(qlat, r_qlat, 4, n, sq4, r_sq4, li, P_QLN, qnT, r_qnT, 512.0, 2)
            for hg in range(4):
                s, r_s = ws.get(gen_loader(I["w_uq"][li][:, 768 * hg:768 * (hg + 1)], 4, 768, ("w_uq", li, hg)))
                sv = slot_view(s, 4, 768)
                for hh in range(4):
                    h = 4 * hg + hh
                    psn, r_psn = ps_next()
                    k.mm([(psn[:, :n], sv[:, kc, 192 * hh:192 * hh + 128], qnT[:, kc, :n], kc == 0, kc == 3) for kc in range(4)],
                         reads=(r_s, r_qnT), writes=(r_psn,))
                    psr, r_psr = ps_next()
                    k.mm([(psr[0:64, :n], sv[:, kc, 192 * hh + 128:192 * hh + 192], qnT[:, kc, :n], kc == 0, kc == 3) for kc in range(4)],
                         reads=(r_s, r_qnT), writes=(r_psr,))
                    sqn = sqq[:, h % 2, 0, :]
                    sqr = sqq[:, h % 2, 1, :]
                    k.op("act", lambda e, psn=psn, sqn=sqn: e.activation(out=sqn[:, :n], in_=psn[:, :n], func=AF.Square), reads=(r_psn,), writes=(r_sqq,))
                    k.op("act", lambda e, psr=psr, sqr=sqr: e.activation(out=sqr[0:64, :n], in_=psr[0:64, :n], func=AF.Square), reads=(r_psr,), writes=(r_sqq,))
                    pss, r_pss = ps_next()
                    k.mm([(pss[:, :n], ones[:, :], sqn[:, :n], True, False), (pss[:, :n], ones[0:64, :], sqr[0:64, :n], False, True)],
                         reads=(r_sqq, r_const), writes=(r_pss,))
                    rs, r_rs = rstd_from_ps(pss, r_pss, n, 192.0, h % 2)
                    k.op("dve", lambda e, psn=psn, rs=rs, h=h: e.scalar_tensor_tensor(out=QTn[:, h, :n], in0=psn[:, :n], scalar=pcol(li, P_QN),
                                                                                     in1=rs[:, :n], op0=ALU.mult, op1=ALU.mult),
                         reads=(r_psn, r_rs, r_par), writes=(r_QTn,))
                    rope_apply(psr, r_psr, n, li, P_QR, qx, r_qx, qxb, r_qxb, None)
                    k.op("pool", lambda e, rs=rs, h=h: e.tensor_tensor(out=QTr[0:64, h, :n], in0=qx[0:64, 0, :n], in1=rs[0:64, :n], op=ALU.mult),
                         reads=(r_qx, r_rs), writes=(r_QTr,))

        def attention(li, t, n):
            qs = t * TT
            qe = qs + n
            nkb = (qe + 127) // 128
            segs = [(0, min(nkb, SEG0_KB))]
            if nkb > SEG0_KB:
                segs.append((SEG0_KB, nkb))
            kv_reads = tuple(r_kv_d[li][0:t + 1])
            ring["set"] = [0, 1, 2, 3]
            loads = [(h, si) for h in range(H) for si in range(len(segs))]

            def issue_load(idx):
                h, si = loads[idx]
                b0, b1 = segs[si]
                k0 = 128 * b0
                k1 = min(128 * b1, qe)
                nk = k1 - k0
                (kn, r_kn), (kr, r_kr), (vv, r_vv) = Kn_s[idx % 2], Kr_s[idx % 2], V_s[idx % 2]
                k.dma("sp", kn[:, 0:nk], KTn_d[li][h, :, k0:k1], reads=kv_reads, writes=(r_kn,))
                k.dma("sp", kr[0:64, 0:nk], KTr_d[li][h, :, k0:k1], reads=kv_reads, writes=(r_kr,))
                nb_ = (nk + 127) // 128
                k.dma("sp", vv[:, 0:nb_, :], V_d[li][h, :, b0:b0 + nb_, :], reads=kv_reads, writes=(r_vv,))

            items = []
            for idx, (h, si) in enumerate(loads):
                b0, b1 = segs[si]
                for j in range(b0, b1):
                    items.append((idx, h, si, j))
            last_item_of_load = {}
            for n_, it in enumerate(items):
                last_item_of_load[it[0]] = n_
            G = 3
            issue_load(0)
            if len(loads) > 1:
                issue_load(1)
            stash = {}
            O, r_O = banks[6], r_bank[6]
            Dn, r_Dn = banks[7], r_bank[7]
            ring["set"] = [0, 1, 2, 3, 4, 5]

            def item_geom(n_):
                idx, h, si, j = items[n_]
                kp = min(128, qe - 128 * j, NKEY - 128 * j)
                cst = max(0, 128 * j - qs)
                diag = (128 * j + 128 > qs)
                bcls = 0 if t < NT else (0 if j < 16 else (1 if j == 16 else 2))
                return kp, cst, diag, bcls

            def stage_s_mm(n_, st, r_st):
                idx, h, si, j = items[n_]
                b0, b1 = segs[si]
                (kn, r_kn), (kr, r_kr), (vv, r_vv) = Kn_s[idx % 2], Kr_s[idx % 2], V_s[idx % 2]
                kp, cst, diag, bcls = item_geom(n_)
                kl = 128 * (j - b0)
                k.mm([(st[0:kp, cst:n], kn[:, kl:kl + kp], QTn[:, h, cst:n], True, False),
                      (st[0:kp, cst:n], kr[0:64, kl:kl + kp], QTr[0:64, h, cst:n], False, True)],
                     reads=(r_kn, r_kr, r_QTn, r_QTr), writes=(r_st,))

            def stage_s_exp(n_, st, r_st, pslot):
                idx, h, si, j = items[n_]
                kp, cst, diag, bcls = item_geom(n_)
                pt_, r_pt = PT[pslot], r_PT[pslot]
                k.op("act", lambda e: e.activation(out=pt_[0:kp, cst:n], in_=st[0:kp, cst:n], func=AF.Exp,
                                                   bias=(kbias[0:kp, j:j + 1] if t >= NT else zb[0:kp, 0:1]), scale=SCALE),
                     reads=(r_st, r_const), writes=(r_pt,))
                dend = min(n, 128 * j + 128 - qs)
                if diag:
                    c0 = qs + cst - 128 * j
                    k.op("pool", lambda e: e.tensor_tensor(out=pt_[0:kp, cst:dend], in0=pt_[0:kp, cst:dend], in1=tri[0:kp, c0:c0 + dend - cst], op=ALU.mult),
                         reads=(r_pt, r_const), writes=(r_pt,))
                stash[n_] = (kp, cst, pt_, r_pt)

            def stage_pv(n_):
                idx, h, si, j = items[n_]
                b0, b1 = segs[si]
                (kn, r_kn), (kr, r_kr), (vv, r_vv) = Kn_s[idx % 2], Kr_s[idx % 2], V_s[idx % 2]
                kp, cst, pt_, r_pt = stash.pop(n_)
                first = (j == 0)
                last = (j == nkb - 1)
                k.mm([(O[:, cst:n], vv[0:kp, j - b0, :], pt_[0:kp, cst:n], first, last),
                      (Dn[:, cst:n], ones[0:kp, :], pt_[0:kp, cst:n], first, last)], reads=(r_vv, r_pt, r_const), writes=(r_O, r_Dn))
                if last:
                    k.op("dve", lambda e: e.reciprocal(out=rden[:, :n], in_=Dn[:, :n]), reads=(r_Dn,), writes=(r_rden,))
                    k.op("dve", lambda e: e.tensor_tensor(out=ybT[:, h, :n], in0=O[:, :n], in1=rden[:, :n], op=ALU.mult),
                         reads=(r_O, r_rden), writes=(r_ybT,))
                if last_item_of_load[idx] == n_ and idx + 2 < len(loads):
                    issue_load(idx + 2)

            groups = []
            cur = []
            for n_ in range(len(items)):
                kp, cst, diag, bcls = item_geom(n_)
                full = (kp == 128 and cst == 0 and not diag)
                if cur:
                    kp0, cst0, diag0, bcls0 = item_geom(cur[0])
                    full0 = (kp0 == 128 and cst0 == 0 and not diag0)
                    if len(cur) == G or not (full and full0 and bcls == bcls0):
                        groups.append(cur)
                        cur = []
                cur.append(n_)
            if cur:
                groups.append(cur)
            for g_ in range(len(groups) + 1):
                if g_ < len(groups):
                    grp = groups[g_]
                    base = 3 * (g_ % 2)
                    pbase = 3 * (g_ % 3)
                    bks = [(banks[base + i], r_bank[base + i]) for i in range(len(grp))]
                    k.prewait("pe", reads=(), writes=tuple(rb for (_, rb) in bks))
                    for n_, (st, r_st) in zip(grp, bks):
                        stage_s_mm(n_, st, r_st)
                    kp, cst, diag, bcls = item_geom(grp[0])
                    if len(grp) > 1:
                        ng = len(grp)
                        j0 = items[grp[0]][3]
                        src = psall[:, 512 * base:512 * (base + ng)].rearrange("p (g c) -> p g c", g=ng)[:, :, 0:n]
                        dstp = PTall[:, pbase:pbase + ng, 0:n]
                        k.op("act", lambda e: e.activation(out=dstp, in_=src, func=AF.Exp,
                                                           bias=(kbias[:, j0:j0 + 1] if t >= NT else zb[:, 0:1]), scale=SCALE),
                             reads=tuple(rb for (_, rb) in bks) + (r_const,), writes=tuple(r_PT[pbase + i] for i in range(ng)))
                        for i, n_ in enumerate(grp):
                            stash[n_] = (128, 0, PT[pbase + i], r_PT[pbase + i])
                    else:
                        stage_s_exp(grp[0], bks[0][0], bks[0][1], pbase)
                if g_ >= 1:
                    grp = groups[g_ - 1]
                    k.prewait("pe", reads=tuple(stash[n_][3] for n_ in grp), writes=())
                    for n_ in grp:
                        stage_pv(n_)
            ring["set"] = list(range(8))

        def conv_pool(li, t, n):
            k.dma("sp", invc[:, :, :n], I["invc"][:, :, t * TT:t * TT + n], writes=(r_invc,))
            for ch in range(8):
                tmp = ctmp[:, ch % 2, :]
                k.op("pool", lambda e, ch=ch, tmp=tmp: e.tensor_scalar(out=tmp[:, :n], in0=cu[:, ch, 0:n], scalar1=pcol(li, P_CW + 3 * ch), scalar2=None, op0=ALU.mult),
                     reads=(r_cu, r_par), writes=(r_ctmp,))
                k.op("dve", lambda e, ch=ch, tmp=tmp: e.scalar_tensor_tensor(out=tmp[:, :n], in0=cu[:, ch, 1:n + 1], scalar=pcol(li, P_CW + 3 * ch + 1),
                                                                             in1=tmp[:, :n], op0=ALU.mult, op1=ALU.add),
                     reads=(r_cu, r_par, r_ctmp), writes=(r_ctmp,))
                k.op("dve", lambda e, ch=ch, tmp=tmp: e.scalar_tensor_tensor(out=tmp[:, :n], in0=cu[:, ch, 2:n + 2], scalar=pcol(li, P_CW + 3 * ch + 2),
                                                                             in1=tmp[:, :n], op0=ALU.mult, op1=ALU.add),
                     reads=(r_cu, r_par, r_ctmp), writes=(r_ctmp,))
                k.op("dve", lambda e, ch=ch, tmp=tmp: e.tensor_tensor(out=yaT[:, ch, :n], in0=tmp[:, :n], in1=bsb[:, ch, :n], op=ALU.mult),
                     reads=(r_ctmp, r_bsb), writes=(r_yaT,))
            for g in range(4):
                src = pin[:, 2 * g:2 * g + 2, :]
                cur = src
                r_cur = r_pin
                lo = 1
                bufs = [(pw1, r_pw1), (pw2, r_pw2)]
                for step in range(g + 1):
                    sh = 1 << step
                    dstb, r_dst = bufs[step % 2]
                    lo2 = lo + sh
                    k.op("pool", lambda e, cur=cur, dstb=dstb, lo2=lo2, sh=sh: e.tensor_tensor(out=dstb[:, :, lo2:n + 16], in0=cur[:, :, lo2:n + 16],
                                                                                             in1=cur[:, :, lo2 - sh:n + 16 - sh], op=ALU.add),
                         reads=(r_cur,), writes=(r_dst,))
                    cur, r_cur, lo = dstb, r_dst, lo2
                for m in range(2):
                    ch = 2 * g + m
                    tmp = ctmp[:, m, :]
                    k.op("dve", lambda e, cur=cur, m=m, g=g, tmp=tmp: e.tensor_tensor(out=tmp[:, :n], in0=cur[:, m, 16:16 + n], in1=invc[:, g, :n], op=ALU.mult),
                         reads=(r_cur, r_invc), writes=(r_ctmp,))
                    k.op("dve", lambda e, ch=ch, tmp=tmp: e.tensor_tensor(out=pooled[:, ch, :n], in0=tmp[:, :n], in1=pin[:, ch, 16:16 + n], op=ALU.subtract),
                         reads=(r_ctmp, r_pin), writes=(r_pooled,))
            def pw_loader(s):
                return [(s[:, 0:2048].rearrange("p (g k n) -> p g k n", g=4, k=2), I["pool_w"][li].rearrange("g (k p) n -> p g k n", p=128))]
            pw_loader.key = ("pool_w", li)
            pw_loader.nel = 2048
            s, r_s = ws.get(pw_loader)
            sv = s[:, 0:2048].rearrange("p (g k n) -> p g k n", g=4, k=2)
            for g in range(4):
                for m in range(2):
                    ps, r_ps = ps_next()
                    k.mm([(ps[:, :n], sv[:, g, kc, m * 128:(m + 1) * 128], pooled[:, 2 * g + kc, :n], kc == 0, kc == 1) for kc in range(2)],
                         reads=(r_s, r_pooled), writes=(r_ps,))
                    ch = 2 * g + m
                    k.op("act", lambda e, ps=ps, ch=ch: e.mul(out=ycT[:, ch, :n], in_=ps[:, :n], mul=pcol(li, P_PS + ch)),
                         reads=(r_ps, r_par), writes=(r_ycT,))

        def merge_and_wo(li, n):
            for jj in range(8):
                for i in range(3):
                    sg, r_sg = ws.get(w_in_loader(li, C_GATE + 2048 * i + 256 * jj, 256))
                    svg = slot_view(sg, NCD, 256)
                    wsrc, nk, src, r_src = [(I["w_a"], 8, yaT, r_yaT), (I["w_b"], 16, ybT, r_ybT), (I["w_c"], 8, ycT, r_ycT)][i]
                    sw_, r_sw = ws.get(gen_loader(wsrc[li][:, 256 * jj:256 * (jj + 1)], nk, 256, ("w_br", li, i, jj)), keep=1)
                    svb = slot_view(sw_, nk, 256)
                    for m in range(2):
                        ps, r_ps = ps_next()
                        k.mm([(ps[:, :n], svg[:, kc, m * 128:(m + 1) * 128], hT[:, kc, :n], kc == 0, kc == NCD - 1) for kc in range(NCD)],
                             reads=(r_sg, r_hT), writes=(r_ps,))
                        k.op("act", lambda e, ps=ps, i=i: e.activation(out=gate[i][:, :n], in_=ps[:, :n], func=AF.Sigmoid), reads=(r_ps,), writes=(r_gate[i],))
                        ps2, r_ps2 = ps_next()
                        k.mm([(ps2[:, :n], svb[:, kc, m * 128:(m + 1) * 128], src[:, kc, :n], kc == 0, kc == nk - 1) for kc in range(nk)],
                             reads=(r_sw, r_src), writes=(r_ps2,))
                        mt, r_mt = mtmp[2 * i + m], r_mtmp[2 * i + m]
                        k.op("dve", lambda e, ps2=ps2, i=i, mt=mt: e.tensor_tensor(out=mt[:, :n], in0=ps2[:, :n], in1=gate[i][:, :n], op=ALU.mult),
                             reads=(r_ps2, r_gate[i]), writes=(r_mt,))
                for m in range(2):
                    oc = 2 * jj + m
                    k.op("pool", lambda e, m=m: e.tensor_tensor(out=mtmp[m][:, :n], in0=mtmp[m][:, :n], in1=mtmp[2 + m][:, :n], op=ALU.add),
                         reads=(r_mtmp[m], r_mtmp[2 + m]), writes=(r_mtmp[m],))
                    k.op("pool", lambda e, oc=oc, m=m: e.tensor_tensor(out=merged[:, oc, :n], in0=mtmp[m][:, :n], in1=mtmp[4 + m][:, :n], op=ALU.add),
                         reads=(r_mtmp[m], r_mtmp[4 + m]), writes=(r_merged,))
            for jj in range(8):
                s, r_s = ws.get(gen_loader(I["w_o"][li][:, 256 * jj:256 * (jj + 1)], 16, 256, ("w_o", li, jj)))
                sv = slot_view(s, 16, 256)
                for m in range(2):
                    oc = 2 * jj + m
                    ps, r_ps = ps_next()
                    k.mm([(ps[:, :n], sv[:, kc, m * 128:(m + 1) * 128], merged[:, kc, :n], kc == 0, kc == NCD - 1) for kc in range(NCD)],
                         reads=(r_s, r_merged), writes=(r_ps,))
                    k.op("dve", lambda e, ps=ps, oc=oc: e.tensor_tensor(out=xT[:, oc, :n], in0=ps[:, :n], in1=xT[:, oc, :n], op=ALU.add),
                         reads=(r_ps, r_xT), writes=(r_xT,))

        def mlp(li, n):
            norm_fm(xT, r_xT, NCD, n, sq16, r_sq16, li, P_MN, hT, r_hT, float(D), 2)
            for jj in range(32):
                s, r_s = ws.get(gen_loader(I["w_up"][li][:, 256 * jj:256 * (jj + 1)], 16, 256, ("w_up", li, jj)))
                sv = slot_view(s, 16, 256)
                for m in range(2):
                    oc = 2 * jj + m
                    ps, r_ps = ps_next()
                    k.mm([(ps[:, :n], sv[:, kc, m * 128:(m + 1) * 128], hT[:, kc, :n], kc == 0, kc == NCD - 1) for kc in range(NCD)],
                         reads=(r_s, r_hT), writes=(r_ps,))
                    rr, r_rr = rl[oc % 2], r_rl[oc % 2]
                    k.op("act", lambda e, ps=ps, rr=rr: e.activation(out=rr[:, :n], in_=ps[:, :n], func=AF.Relu), reads=(r_ps,), writes=(r_rr,))
                    k.op("pool", lambda e, rr=rr, oc=oc: e.tensor_tensor(out=act[:, oc, :n], in0=rr[:, :n], in1=rr[:, :n], op=ALU.mult),
                         reads=(r_rr,), writes=(r_act,))
            for oc in range(NCD):
                ps, r_ps = ps_next()
                for half in range(2):
                    s, r_s = ws.get(gen_loader(I["w_down"][li][4096 * half:4096 * (half + 1), 128 * oc:128 * (oc + 1)], 32, 128, ("w_down", li, oc, half)))
                    sv = slot_view(s, 32, 128)
                    k.mm([(ps[:, :n], sv[:, kc, :], act[:, 32 * half + kc, :n], (half == 0 and kc == 0), (half == 1 and kc == 31)) for kc in range(32)],
                         reads=(r_s, r_act), writes=(r_ps,))
                k.op("dve", lambda e, ps=ps, oc=oc: e.tensor_tensor(out=xT[:, oc, :n], in0=ps[:, :n], in1=xT[:, oc, :n], op=ALU.add),
                     reads=(r_ps, r_xT), writes=(r_xT,))

        def layer(li, src, r_src_list, kv_only_tiles, dst_fn):
            k.op("pool", lambda e: e.memset(halo_cu[:], 0.0), writes=(r_hcu,))
            k.op("pool", lambda e: e.memset(halo_pin[:], 0.0), writes=(r_hpin,))
            for t in range(2 * NT):
                n = TT
                k.dma("sp", xT[:, :, :n], src.rearrange("(c p) t -> p c t", p=128)[:, :, t * TT:t * TT + n],
                      reads=tuple(r_src_list[t:t + 1]), writes=(r_xT,))
                load_rope(t * TT, n)
                norm_fm(xT, r_xT, NCD, n, sq16, r_sq16, li, P_AN, hT, r_hT, float(D), 2)
                kv_path(li, n, t * TT, r_kv_d[li][t])
                if t < kv_only_tiles:
                    if t == kv_only_tiles - 1:
                        mixer_inputs(li, n, n - 16, False)
                        save_halo(n)
                    continue
                q_path(li, n)
                attention(li, t, n)
                restore_halo(t == NT)
                mixer_inputs(li, n, 0, True)
                save_halo(n)
                conv_pool(li, t, n)
                merge_and_wo(li, n)
                mlp(li, n)
                d_ap, r_d = dst_fn(t)
                k.dma("sp", d_ap, xT[:, :, :n], reads=(r_xT,), writes=(r_d,))

        def program():
            ring["p"] = 0
            ring["set"] = list(range(8))
            load_consts()
            for li in range(NL):
                last = (li == NL - 1)
                if li == 0:
                    src, r_src = I["xall"], []
                    kvo = KV_ONLY0
                else:
                    src, r_src = X1_d[li - 1], r_x1[li - 1]
                    kvo = NT
                if last:
                    dst_fn = lambda t: (OUT.rearrange("(c p) t -> p c t", p=128)[:, :, (t - NT) * TT:(t - NT + 1) * TT], r_out[t - NT])
                else:
                    dst_fn = lambda t, li=li: (X1_d[li].rearrange("(c p) t -> p c t", p=128)[:, :, t * TT:(t + 1) * TT], r_x1[li][t])
                layer(li, src, r_src, kvo, dst_fn)
            k.finish(r_out)

        k.dry = True
        program()
        ws.pos = 0
        k.dry = False
        program()
        print("[kernel] sbuf bytes remaining", nc.sbuf_bytes_remaining, flush=True)
        print(f"[kernel] instructions={k.n_ins} waits={k.n_wait} weight_slots={len(ws.reqs)}", flush=True)
    return nc


def _rope_tables():
    pos = np.arange(TALL, dtype=np.float32)
    inv = (np.float32(10000.0) ** (-np.arange(0, 64, 2, dtype=np.float32) / np.float32(64))).astype(np.float32)
    ang = (pos[:, None] * inv[None, :]).astype(np.float32)
    c, s = np.cos(ang).astype(np.float32), np.sin(ang).astype(np.float32)
    C = np.concatenate([c, c], 1).T
    S = np.concatenate([-s, s], 1).T
    return np.ascontiguousarray(C), np.ascontiguousarray(S)


def _params(inp, l):
    P = np.zeros((128, NPARAM), np.float32)
    P[:, P_AN:P_AN + 16] = inp["attn_norm"][l].reshape(16, 128).T
    P[:, P_MN:P_MN + 16] = inp["mlp_norm"][l].reshape(16, 128).T
    P[:, P_QLN:P_QLN + 4] = inp["q_lat_norm"][l].reshape(4, 128).T
    P[:, P_KVLN:P_KVLN + 4] = inp["kv_lat_norm"][l].reshape(4, 128).T
    P[:, P_QN] = inp["q_norm"][l][:128]
    P[:64, P_QR] = inp["q_norm"][l][128:]
    P[:, P_KN] = inp["k_norm"][l][:128]
    P[:64, P_KR] = inp["k_norm"][l][128:]
    cw = inp["conv_w"][l]
    for ch in range(8):
        for j in range(3):
            P[:, P_CW + 3 * ch + j] = cw[j, ch * 128:(ch + 1) * 128]
    P[:, P_PS:P_PS + 8] = inp["pool_scale"][l].reshape(8, 128).T
    return P


_NC_CACHE = {}


def _get_nc(layers, fused):
    key = (tuple(layers), fused)
    if key not in _NC_CACHE:
        _NC_CACHE[key] = build_program(list(layers), fused)
    return _NC_CACHE[key]


def _run(layers, inp, x_own, x_pre, fused):
    nc = _get_nc(layers, fused)
    C, S = _rope_tables()
    bf = ml_dtypes.bfloat16
    tri = (np.arange(128)[None, :] >= np.arange(128)[:, None]).astype(bf)
    ones = np.ones((128, 128), bf)
    swap = np.zeros((64, 64), bf)
    for i in range(32):
        swap[i + 32, i] = 1
        swap[i, i + 32] = 1
    ls = list(layers)
    common = {
        "params": np.stack([_params(inp, l) for l in ls]),
        "tri": tri, "ones": ones, "swap": swap,
    }
    for i, l in enumerate(ls):
        for nm in ["w_in", "w_uq", "w_ukv", "pool_w", "w_branch_a", "w_branch_b", "w_branch_c", "w_o", "w_up", "w_down"]:
            common[f"{nm}{i}"] = np.ascontiguousarray(inp[nm][l])
    in_maps = []
    for c in range(8):
        r = c % 2
        pos0 = r * NTOK
        ropeC = np.concatenate([C[:, 0:NPRE], C[:, pos0:pos0 + NTOK]], 1)
        ropeS = np.concatenate([S[:, 0:NPRE], S[:, pos0:pos0 + NTOK]], 1)
        tpos = np.concatenate([np.arange(0, NPRE), np.arange(pos0, pos0 + NTOK)]).astype(np.float32) + 1.0
        invc = np.stack([np.float32(1.0) / np.minimum(tpos, np.float32(w)) for w in (2, 4, 8, 16)]).astype(np.float32)
        invc = np.ascontiguousarray(np.broadcast_to(invc[None], (128, 4, NKEY)))
        kb = np.zeros((128, NKB), np.float32)
        if r == 0:
            keyidx = np.arange(NKB * 128).reshape(NKB, 128).T
            kb[keyidx < NPRE] = -30000.0
        m = dict(common)
        m.update({"xall": np.ascontiguousarray(np.concatenate([x_pre[c], x_own[c]], axis=1)),
                  "ropeC": np.ascontiguousarray(ropeC), "ropeS": np.ascontiguousarray(ropeS), "invc": invc,
                  "kbias": kb, "flag": np.full((128, 1), float(r), np.float32)})
        in_maps.append(m)
    res = run_bass_kernel_spmd(nc, in_maps, core_ids=list(range(8)))
    if DEBUG:
        global DBG_OUT
        DBG_OUT = [{nm: np.asarray(res.results[c][nm]) for nm in DBG_NAMES} for c in range(8)]
    return [np.asarray(res.results[c]["outT"]) for c in range(8)]


FUSED = True


def kernel(**inp):
    inp = {k_: np.asarray(v) for k_, v in inp.items()}
    x = inp["x"].astype(np.float32)
    B = x.shape[0]
    meta = np.broadcast_to(inp["meta_tokens"][None].astype(np.float32), (B, NMETA, D))
    hseq = np.concatenate([meta, x], axis=1)
    own = [np.ascontiguousarray(hseq[c // 2, (c % 2) * NTOK:(c % 2 + 1) * NTOK].T) for c in range(8)]
    zero = np.zeros((D, NPRE), np.float32)
    if FUSED:
        pre = [zero if c % 2 == 0 else own[c - 1] for c in range(8)]
        outs = _run([0, 1], inp, own, pre, True)
    else:
        cur = own
        for l in range(2):
            pre = [zero if c % 2 == 0 else cur[c - 1] for c in range(8)]
            cur = _run([l], inp, cur, pre, False)
        outs = cur
    full = np.stack([np.concatenate([outs[2 * b].T, outs[2 * b + 1].T], axis=0) for b in range(B)])
    return np.ascontiguousarray(full[:, NMETA:, :]).astype(np.float32)
```

```python
import contextlib
import numpy as np
import ml_dtypes
import concourse.bass as bass
import concourse.mybir as mybir
from concourse.bass_utils import run_bass_kernel_spmd

F32 = mybir.dt.float32
BF16 = mybir.dt.bfloat16
AF = mybir.ActivationFunctionType
ALU = mybir.AluOpType

D = 2048
NCD = 16
H = 16
SEQ = 4096
NMETA = 16
TALL = SEQ + NMETA
NTOK = TALL // 2
NPRE = NTOK
TT = 257
NT = NTOK // TT
NKEY = NPRE + NTOK
NKB = (NKEY + 127) // 128
SEG0_KB = 16
SEGK = 2064
D_IN = 11328
C_U, C_B, C_C, C_QL, C_KVL, C_KR, C_POOL, C_GATE = 0, 1024, 2048, 3072, 3584, 4096, 4160, 5184
DFF = 8192
EPS = 1e-6
SCALE = 192.0 ** -0.5
SLOT = 4096
NSLOT = 6
P_AN, P_MN, P_QLN, P_KVLN, P_QN, P_QR, P_KN, P_KR, P_CW, P_PS = 0, 16, 32, 36, 40, 41, 42, 43, 44, 68
NPARAM = 76


class Res:
    __slots__ = ("name", "excl", "w", "rs", "ov")

    def __init__(self, name, excl=False):
        self.name = name
        self.excl = excl
        self.w = None
        self.rs = {}
        self.ov = [self]


class Eng:
    def __init__(self, name, eng, sem):
        self.name, self.eng, self.sem = name, eng, sem
        self.cnt = 0
        self.seen = {}


class KB:
    def __init__(self, nc, es):
        self.nc = nc
        self.dry = False
        mk = lambda n: es.enter_context(nc.semaphore(n))
        self.E = {
            "pe": Eng("pe", nc.tensor, mk("s_pe")),
            "act": Eng("act", nc.scalar, mk("s_act")),
            "dve": Eng("dve", nc.vector, mk("s_dve")),
            "pool": Eng("pool", nc.gpsimd, mk("s_pool")),
            "sp": Eng("sp", nc.sync, mk("s_sp")),
        }
        self.dma_sems = {"sp": [mk(f"d_sp{i}") for i in range(12)], "pool": [mk(f"d_pl{i}") for i in range(8)]}
        self.dma_cnt = {}
        self.dma_rr = {"sp": 0, "pool": 0}
        self.n_wait = 0
        self.n_ins = 0

    def _wait(self, e, tk):
        sem, val = tk
        if e.seen.get(sem.num, 0) >= val:
            return
        e.eng.wait_ge(sem, val)
        e.seen[sem.num] = val
        self.n_wait += 1

    def _deps(self, reads, writes):
        tks = []
        for r in reads:
            for res in r.ov:
                if res.w is not None:
                    tks.append((res.w, False))
                if res.excl:
                    tks.extend((t, True) for t in res.rs.values())
        for w in writes:
            for res in w.ov:
                if res.w is not None:
                    tks.append((res.w, False))
                tks.extend((t, False) for t in res.rs.values())
        return tks

    def _mark(self, tk, reads, writes):
        for r in reads:
            r.rs[tk[0].num] = tk
        for w in writes:
            w.w = tk
            w.rs = {}

    def op(self, en, fn, reads=(), writes=()):
        if self.dry:
            return
        e = self.E[en]
        for tk, rr in self._deps(reads, writes):
            if tk[0] is e.sem and (rr or en == "pe"):
                continue
            self._wait(e, tk)
        ins = fn(e.eng)
        e.cnt += 1
        ins.then_inc(e.sem, 1)
        self.n_ins += 1
        self._mark((e.sem, e.cnt), reads, writes)

    def prewait(self, en, reads=(), writes=()):
        if self.dry:
            return
        e = self.E[en]
        best = {}
        for tk, rr in self._deps(reads, writes):
            if tk[0] is e.sem:
                continue
            if tk[0].num not in best or best[tk[0].num][1] < tk[1]:
                best[tk[0].num] = tk
        for tk in best.values():
            self._wait(e, tk)

    def mm(self, mms, reads, writes):
        if self.dry:
            return
        e = self.E["pe"]
        for tk, rr in self._deps(reads, writes):
            if tk[0] is e.sem:
                continue
            self._wait(e, tk)
        ins = None
        for (o, l, r, st, sp) in mms:
            ins = e.eng.matmul(o, lhsT=l, rhs=r, start=st, stop=sp)
            self.n_ins += 1
        e.cnt += 1
        ins.then_inc(e.sem, 1)
        self._mark((e.sem, e.cnt), reads, writes)

    def dma(self, q, out, in_, reads=(), writes=()):
        if self.dry:
            return
        e = self.E[q]
        for tk, rr in self._deps(reads, writes):
            self._wait(e, tk)
        pool = self.dma_sems[q]
        i = self.dma_rr[q]
        self.dma_rr[q] = (i + 1) % len(pool)
        sem = pool[i]
        prev = self.dma_cnt.get(sem.num, 0)
        if prev:
            self._wait(e, (sem, prev))
        ins = e.eng.dma_start(out=out, in_=in_)
        ins.then_inc(sem, 16)
        self.dma_cnt[sem.num] = prev + 16
        self.n_ins += 1
        self._mark((sem, prev + 16), reads, writes)

    def finish(self, res_list):
        e = self.E["sp"]
        for r in res_list:
            if r.w is not None:
                self._wait(e, r.w)


class Region:
    def __init__(self, tile, nelem):
        self.t = tile
        self.n = nelem
        self.off = 0
        self.phase = None
        self.bufs = []

    def set_phase(self, ph):
        self.phase = ph
        self.off = 0

    def take(self, name, free_shape, dtype):
        n = int(np.prod(free_shape))
        nb = n * (2 if dtype == F32 else 1)
        self.off = (self.off + 15) // 16 * 16
        off = self.off
        self.off += nb
        assert self.off <= self.n, (name, self.off, self.n)
        v = self.t[:, off:off + nb]
        if dtype == F32:
            v = v.bitcast(F32)
        if len(free_shape) == 2:
            v = v.rearrange("p (a b) -> p a b", a=free_shape[0])
        elif len(free_shape) == 3:
            v = v.rearrange("p (a b c) -> p a b c", a=free_shape[0], b=free_shape[1])
        r = Res(name)
        for (ph, r2) in self.bufs:
            if ph != self.phase:
                r.ov.append(r2)
                r2.ov.append(r)
        self.bufs.append((self.phase, r))
        return v, r


class WStream:
    def __init__(self, k, slots, slot_res, cache_fn):
        self.k = k
        self.slots = slots
        self.res = slot_res
        self.reqs = []
        self.issued = 0
        self.pos = 0
        self.cache_fn = cache_fn
        self.cache_idx = {}
        self.cache_res = {}
        self.loaded = set()

    def get(self, loader, keep=0):
        k = self.k
        i = self.pos
        self.pos += 1
        if k.dry:
            self.reqs.append(loader)
            if loader.key not in self.cache_idx:
                self.cache_idx[loader.key] = len(self.cache_idx)
                self.cache_res[loader.key] = Res("wc%d" % len(self.cache_idx))
            return self.slots[i % NSLOT], self.res[i % NSLOT]
        assert keep < NSLOT - 1
        lim = min(len(self.reqs), i - keep + NSLOT)
        while self.issued < lim:
            j = self.issued
            s, r = self.slots[j % NSLOT], self.res[j % NSLOT]
            ld = self.reqs[j]
            ci = self.cache_idx[ld.key]
            cr = self.cache_res[ld.key]
            if ld.key not in self.loaded:
                self.loaded.add(ld.key)
                for (o, src) in ld(s):
                    k.dma("pool", o, src, reads=(), writes=(r,))
                k.dma("sp", self.cache_fn(ci)[:, 0:ld.nel], s[:, 0:ld.nel], reads=(r,), writes=(cr,))
            else:
                k.dma("sp" if (j % 2) else "pool", s[:, 0:ld.nel], self.cache_fn(ci)[:, 0:ld.nel], reads=(cr,), writes=(r,))
            self.issued += 1
        return self.slots[i % NSLOT], self.res[i % NSLOT]


def slot_view(s, kc, ncols):
    return s[:, 0:kc * ncols].rearrange("p (k n) -> p k n", k=kc)


DEBUG = False
DBG_NAMES = []


def build_program(layers, fused):
    KV_ONLY0 = 0 if len(layers) > 1 else NT
    nc = bass.Bass("TRN2", target_bir_lowering=False)
    NL = len(layers)
    dt = lambda name, shape, dtype=F32, kind="ExternalInput": nc.dram_tensor(name, list(shape), dtype, kind=kind).ap()
    I = {}
    I["xall"] = dt("xall", [D, NKEY])
    I["w_in"] = [dt(f"w_in{l}", [D, D_IN]) for l in range(NL)]
    I["w_uq"] = [dt(f"w_uq{l}", [512, 3072]) for l in range(NL)]
    I["w_ukv"] = [dt(f"w_ukv{l}", [512, 4096]) for l in range(NL)]
    I["pool_w"] = [dt(f"pool_w{l}", [4, 256, 256]) for l in range(NL)]
    I["w_a"] = [dt(f"w_branch_a{l}", [1024, D]) for l in range(NL)]
    I["w_b"] = [dt(f"w_branch_b{l}", [D, D]) for l in range(NL)]
    I["w_c"] = [dt(f"w_branch_c{l}", [1024, D]) for l in range(NL)]
    I["w_o"] = [dt(f"w_o{l}", [D, D]) for l in range(NL)]
    I["w_up"] = [dt(f"w_up{l}", [D, DFF]) for l in range(NL)]
    I["w_down"] = [dt(f"w_down{l}", [DFF, D]) for l in range(NL)]
    I["params"] = dt("params", [NL, 128, NPARAM])
    I["ropeC"] = dt("ropeC", [64, NKEY])
    I["ropeS"] = dt("ropeS", [64, NKEY])
    I["invc"] = dt("invc", [128, 4, NKEY])
    I["kbias"] = dt("kbias", [128, NKB])
    I["flag"] = dt("flag", [128, 1])
    I["tri"] = dt("tri", [128, 128], BF16)
    I["ones"] = dt("ones", [128, 128], BF16)
    I["swap"] = dt("swap", [64, 64], BF16)
    OUT = dt("outT", [D, NTOK], F32, kind="ExternalOutput")
    KTn_d = [dt(f"ktn{l}", [H, 128, NKEY], BF16, kind="Internal") for l in range(NL)]
    KTr_d = [dt(f"ktr{l}", [H, 64, NKEY], BF16, kind="Internal") for l in range(NL)]
    V_d = [dt(f"v{l}", [H, 128, NKB, 128], BF16, kind="Internal") for l in range(NL)]
    X1_d = [dt(f"x1_{l}", [D, NKEY], F32, kind="Internal") for l in range(NL - 1)]

    with contextlib.ExitStack() as es:
        k = KB(nc, es)
        sb = lambda name, shape, dtype: es.enter_context(nc.sbuf_tensor(name, list(shape), dtype))
        xT = sb("xT_sb", [128, NCD, TT], F32); r_xT = Res("xT")
        hT = sb("hT_sb", [128, NCD, TT], BF16); r_hT = Res("hT")
        ybT = sb("ybT_sb", [128, H, TT], BF16); r_ybT = Res("ybT")
        wsl = [sb(f"wslot{i}", [128, SLOT], BF16) for i in range(NSLOT)]
        r_wsl = [Res(f"wslot{i}") for i in range(NSLOT)]
        params = sb("params_sb", [128, NL, NPARAM], F32); r_par = Res("params")
        tri = sb("tri_sb", [128, 128], BF16)
        ones = sb("ones_sb", [128, 128], BF16)
        swp = sb("swap_sb", [64, 64], BF16)
        kbias = sb("kbias_sb", [128, NKB], F32)
        flag = sb("flag_sb", [128, 1], F32)
        epsb = sb("eps_sb", [128, 1], F32)
        zb = sb("zb_sb", [128, 1], F32)
        r_const = Res("consts")
        rstd = [sb(f"rstd{i}", [128, TT], F32) for i in range(3)]
        r_rstd = [Res(f"rstd{i}") for i in range(3)]
        lnv = [sb(f"lnv{i}", [128, TT], F32) for i in range(2)]
        r_lnv = [Res(f"lnv{i}") for i in range(2)]
        ropeC = sb("ropeC_sb", [64, TT], F32)
        ropeS = sb("ropeS_sb", [64, TT], F32)
        r_rope = Res("rope")
        invc = sb("invc_sb", [128, 4, TT], F32); r_invc = Res("invc")
        PTall = sb("ptall", [128, 9, TT], BF16)
        PT = [PTall[:, i, :] for i in range(9)]
        r_PT = [Res(f"pt{i}") for i in range(9)]
        rden = sb("rden", [128, TT], F32); r_rden = Res("rden")
        rl = [sb(f"relu{i}", [128, TT], F32) for i in range(2)]
        r_rl = [Res(f"relu{i}") for i in range(2)]
        halo_cu = sb("halo_cu", [128, 8, 2], BF16); r_hcu = Res("halo_cu")
        halo_pin = sb("halo_pin", [128, 8, 16], F32); r_hpin = Res("halo_pin")
        U1N = 24 * 1024
        U2N = 28 * 1024
        U1 = Region(sb("U1", [128, U1N], BF16), U1N)
        U2 = Region(sb("U2", [128, U2N], BF16), U2N)
        U1.set_phase("kv")
        kvl, r_kvl = U1.take("kvl", [5, TT], F32)
        sq5, r_sq5 = U1.take("sq5", [5, TT], BF16)
        kvnT, r_kvnT = U1.take("kvnT", [4, TT], BF16)
        KTn_all, r_KTn = U1.take("KTn_all", [H, TT], BF16)
        KTr_all, r_KTr = U1.take("KTr_all", [H, TT], BF16)
        Vsb, r_Vsb = U1.take("Vsb", [3, D], BF16)
        krg, r_krg = U1.take("krg", [TT], F32)
        kx, r_kx = U1.take("kx", [2, TT], F32)
        kxb, r_kxb = U1.take("kxb", [TT], BF16)
        sqh, r_sqh = U1.take("sqh", [2, 2, TT], BF16)
        U1.set_phase("mix")
        usb, r_usb = U1.take("usb", [2, TT], F32)
        cu, r_cu = U1.take("cu", [8, TT + 2], BF16)
        bsb, r_bsb = U1.take("bsb", [8, TT], BF16)
        ctmp, r_ctmp = U1.take("ctmp", [2, TT], F32)
        yaT, r_yaT = U1.take("yaT", [8, TT], BF16)
        pin, r_pin = U1.take("pin", [8, TT + 16], F32)
        pw1, r_pw1 = U1.take("pw1", [2, TT + 16], F32)
        pw2, r_pw2 = U1.take("pw2", [2, TT + 16], F32)
        pooled, r_pooled = U1.take("pooled", [8, TT], BF16)
        ycT, r_ycT = U1.take("ycT", [8, TT], BF16)
        merged, r_merged = U1.take("merged", [NCD, TT], BF16)
        U2.set_phase("att")
        qlat, r_qlat = U2.take("qlat", [4, TT], F32)
        sq4, r_sq4 = U2.take("sq4", [4, TT], BF16)
        qnT, r_qnT = U2.take("qnT", [4, TT], BF16)
        QTn, r_QTn = U2.take("QTn", [H, TT], BF16)
        QTr, r_QTr = U2.take("QTr", [H, TT], BF16)
        qx, r_qx = U2.take("qx", [2, TT], F32)
        qxb, r_qxb = U2.take("qxb", [TT], BF16)
        sqq, r_sqq = U2.take("sqq", [2, 2, TT], BF16)
        Kn_s = []; Kr_s = []; V_s = []
        for i in range(2):
            a, ra = U2.take(f"Kn_s{i}", [SEGK], BF16)
            b_, rb = U2.take(f"Kr_s{i}", [SEGK], BF16)
            c_, rc = U2.take(f"V_s{i}", [17, 128], BF16)
            Kn_s.append((a, ra)); Kr_s.append((b_, rb)); V_s.append((c_, rc))
        U2.set_phase("mrg")
        gate = []; r_gate = []; mtmp = []; r_mtmp = []
        for i in range(3):
            a, ra = U2.take(f"gate{i}", [TT], F32)
            gate.append(a); r_gate.append(ra)
        for i in range(6):
            a, ra = U2.take(f"mtmp{i}", [TT], F32)
            mtmp.append(a); r_mtmp.append(ra)
        U2.set_phase("mlp")
        sq16, r_sq16 = U2.take("sq16", [NCD, TT], BF16)
        act, r_act = U2.take("act", [64, TT], BF16)
        psall = es.enter_context(nc.psum_tensor("psall", [128, 8 * 512], F32))
        banks = [psall[:, 512 * i:512 * (i + 1)] for i in range(8)]
        r_bank = [Res(f"bank{i}", excl=True) for i in range(8)]
        ring = {"set": list(range(8)), "p": 0}

        def ps_next():
            i = ring["set"][ring["p"] % len(ring["set"])]
            ring["p"] += 1
            return banks[i], r_bank[i]

        WC_d = [dt(f"wcache{i}", [150, 128, SLOT], BF16, kind="Internal") for i in range(NL)]
        ws = WStream(k, wsl, r_wsl, lambda ci: WC_d[ci // 150][ci % 150])
        cc_sem = es.enter_context(nc.semaphore("cc_sem"))

        def dump(name, ap, shape, dtype, r):
            if not DEBUG or k.dry:
                return
            dram = nc.dram_tensor("dbg_" + name, list(shape), dtype, kind="ExternalOutput").ap()
            DBG_NAMES.append("dbg_" + name)
            k.dma("sp", dram, ap, reads=(r,), writes=(Res("dbg_" + name),))

        r_kv_d = [[Res(f"kvd{l}_{t}") for t in range(2 * NT)] for l in range(NL)]
        r_x1 = [[Res(f"x1_{l}_{t}") for t in range(2 * NT)] for l in range(NL)]
        r_out = [Res(f"out_{t}") for t in range(NT)]

        def V(fn, **kw):
            return lambda e: fn(e, **kw)

        def load_consts():
            k.dma("sp", params[:], I["params"].rearrange("l p c -> p l c"), writes=(r_par,))
            k.dma("sp", tri[:], I["tri"], writes=(r_const,))
            k.dma("sp", ones[:], I["ones"], writes=(r_const,))
            k.dma("sp", swp[:], I["swap"], writes=(r_const,))
            k.dma("sp", kbias[:], I["kbias"], writes=(r_const,))
            k.dma("sp", flag[:], I["flag"], writes=(r_const,))
            k.op("dve", lambda e: e.memset(epsb[:], EPS), writes=(r_const,))
            k.op("dve", lambda e: e.memset(zb[:], 0.0), writes=(r_const,))

        def pcol(li, c):
            return params[:, li, c:c + 1]

        def rstd_from_ps(ps, r_ps, n, dim, slot):
            lv, r_lv = lnv[slot % 2], r_lnv[slot % 2]
            rs, r_rs = rstd[slot], r_rstd[slot]
            k.op("act", lambda e: e.activation(out=lv[:, :n], in_=ps[:, :n], func=AF.Ln, bias=epsb[:], scale=1.0 / dim),
                 reads=(r_ps, r_const), writes=(r_lv,))
            k.op("act", lambda e: e.activation(out=rs[:, :n], in_=lv[:, :n], func=AF.Exp, scale=-0.5),
                 reads=(r_lv,), writes=(r_rs,))
            return rs, r_rs

        def norm_fm(src, r_src, nch, n, sq, r_sq, li, gcol, dst, r_dst, dim, slot):
            for c in range(nch):
                if nch == NCD and c % 2 == 1:
                    k.op("pool", lambda e, c=c: e.tensor_tensor(out=sq[:, c, :n], in0=src[:, c, :n], in1=src[:, c, :n], op=ALU.mult),
                         reads=(r_src,), writes=(r_sq,))
                else:
                    k.op("act", lambda e, c=c: e.activation(out=sq[:, c, :n], in_=src[:, c, :n], func=AF.Square),
                         reads=(r_src,), writes=(r_sq,))
            ps, r_ps = ps_next()
            k.mm([(ps[:, :n], ones[:, :], sq[:, c, :n], c == 0, c == nch - 1) for c in range(nch)],
                 reads=(r_sq, r_const), writes=(r_ps,))
            rs, r_rs = rstd_from_ps(ps, r_ps, n, dim, slot)
            for c in range(nch):
                k.op("dve", lambda e, c=c: e.scalar_tensor_tensor(out=dst[:, c, :n], in0=src[:, c, :n], scalar=pcol(li, gcol + c),
                                                                  in1=rs[:, :n], op0=ALU.mult, op1=ALU.mult),
                     reads=(r_src, r_rs, r_par), writes=(r_dst,))

        def w_in_loader(li, c0, ncols):
            def f(s):
                return [(slot_view(s, NCD, ncols), I["w_in"][li][:, c0:c0 + ncols].rearrange("(k p) n -> p k n", p=128))]
            f.key = ("w_in", li, c0, ncols)
            f.nel = NCD * ncols
            return f

        def gen_loader(ap2d, kc, ncols, key):
            def f(s):
                return [(slot_view(s, kc, ncols), ap2d.rearrange("(k p) n -> p k n", p=128))]
            f.key = key
            f.nel = kc * ncols
            return f

        def rope_apply(xps, r_xps, n, li, gcolr, x_f, r_x_f, xb, r_xb, dst_fn):
            k.op("act", lambda e: e.mul(out=xb[0:64, :n], in_=xps[0:64, :n], mul=pcol(li, gcolr)[0:64, :]),
                 reads=(r_xps, r_par), writes=(r_xb,))
            k.op("dve", lambda e: e.scalar_tensor_tensor(out=x_f[0:64, 0, :n], in0=xps[0:64, :n], scalar=pcol(li, gcolr)[0:64, :],
                                                         in1=ropeC[0:64, :n], op0=ALU.mult, op1=ALU.mult),
                 reads=(r_xps, r_par, r_rope), writes=(r_x_f,))
            ps2, r_ps2 = ps_next()
            k.mm([(ps2[0:64, :n], swp[:, :], xb[0:64, :n], True, True)], reads=(r_xb, r_const), writes=(r_ps2,))
            k.op("dve", lambda e: e.tensor_tensor(out=x_f[0:64, 1, :n], in0=ps2[0:64, :n], in1=ropeS[0:64, :n], op=ALU.mult),
                 reads=(r_ps2, r_rope, r_x_f), writes=(r_x_f,))
            k.op("pool", lambda e: e.tensor_tensor(out=x_f[0:64, 0, :n], in0=x_f[0:64, 0, :n], in1=x_f[0:64, 1, :n], op=ALU.add),
                 reads=(r_x_f,), writes=(r_x_f,))

        def load_rope(key0, n):
            k.dma("sp", ropeC[:, :n], I["ropeC"][:, key0:key0 + n], writes=(r_rope,))
            k.dma("sp", ropeS[:, :n], I["ropeS"][:, key0:key0 + n], writes=(r_rope,))

        def kv_path(li, n, key0, r_kvdst):
            for g in range(2):
                s, r_s = ws.get(w_in_loader(li, C_KVL + 256 * g, 256))
                sv = slot_view(s, NCD, 256)
                for m in range(2):
                    ps, r_ps = ps_next()
                    k.mm([(ps[:, :n], sv[:, kc, m * 128:(m + 1) * 128], hT[:, kc, :n], kc == 0, kc == NCD - 1) for kc in range(NCD)],
                         reads=(r_s, r_hT), writes=(r_ps,))
                    c = 2 * g + m
                    k.op("dve", lambda e, c=c, ps=ps: e.tensor_copy(out=kvl[:, c, :n], in_=ps[:, :n]), reads=(r_ps,), writes=(r_kvl,))
            s, r_s = ws.get(w_in_loader(li, C_KR, 64))
            sv = slot_view(s, NCD, 64)
            psr, r_psr = ps_next()
            k.mm([(psr[0:64, :n], sv[:, kc, :], hT[:, kc, :n], kc == 0, kc == NCD - 1) for kc in range(NCD)],
                 reads=(r_s, r_hT), writes=(r_psr,))
            k.op("act", lambda e: e.activation(out=sq5[0:64, 4, :n], in_=psr[0:64, :n], func=AF.Square), reads=(r_psr,), writes=(r_sq5,))
            rope_apply(psr, r_psr, n, li, P_KR, kx, r_kx, kxb, r_kxb, None)
            k.op("pool", lambda e: e.tensor_copy(out=krg[0:64, :n], in_=kx[0:64, 0, :n]), reads=(r_kx,), writes=(r_krg,))
            norm_fm(kvl, r_kvl, 4, n, sq5, r_sq5, li, P_KVLN, kvnT, r_kvnT, 512.0, 2)
            ntb = (n + 127) // 128
            for hg in range(4):
                s, r_s = ws.get(gen_loader(I["w_ukv"][li][:, 1024 * hg:1024 * (hg + 1)], 4, 1024, ("w_ukv", li, hg)))
                sv = slot_view(s, 4, 1024)
                for hh in range(4):
                    h = 4 * hg + hh
                    ps, r_ps = ps_next()
                    k.mm([(ps[:, :n], sv[:, kc, 256 * hh:256 * hh + 128], kvnT[:, kc, :n], kc == 0, kc == 3) for kc in range(4)],
                         reads=(r_s, r_kvnT), writes=(r_ps,))
                    sq, r_sq = sqh[:, h % 2, 0, :], r_sqh
                    k.op("act", lambda e, ps=ps, sq=sq: e.activation(out=sq[:, :n], in_=ps[:, :n], func=AF.Square),
                         reads=(r_ps,), writes=(r_sq,))
                    pss, r_pss = ps_next()
                    k.mm([(pss[:, :n], ones[:, :], sq[:, :n], True, False),
                          (pss[:, :n], ones[0:64, :], sq5[0:64, 4, :n], False, True)],
                         reads=(r_sq, r_sq5, r_const), writes=(r_pss,))
                    rs, r_rs = rstd_from_ps(pss, r_pss, n, 192.0, h % 2)
                    k.op("dve", lambda e, ps=ps, rs=rs, h=h: e.scalar_tensor_tensor(out=KTn_all[:, h, :n], in0=ps[:, :n], scalar=pcol(li, P_KN),
                                                                                   in1=rs[:, :n], op0=ALU.mult, op1=ALU.mult),
                         reads=(r_ps, r_rs, r_par), writes=(r_KTn,))
                    k.op("pool", lambda e, rs=rs, h=h: e.tensor_tensor(out=KTr_all[0:64, h, :n], in0=krg[0:64, :n], in1=rs[0:64, :n], op=ALU.mult),
                         reads=(r_krg, r_rs), writes=(r_KTr,))
                for tb in range(ntb):
                    m = min(128, n - 128 * tb)
                    ps, r_ps = ps_next()
                    mms = []
                    for hh in range(4):
                        for kc in range(4):
                            mms.append((ps[0:m, 128 * hh:128 * (hh + 1)], kvnT[:, kc, 128 * tb:128 * tb + m],
                                        sv[:, kc, 256 * hh + 128:256 * hh + 256], kc == 0, kc == 3))
                    k.mm(mms, reads=(r_s, r_kvnT), writes=(r_ps,))
                    k.op("act", lambda e, ps=ps, m=m, tb=tb, hg=hg: e.activation(out=Vsb[0:m, tb, 512 * hg:512 * (hg + 1)], in_=ps[0:m, 0:512], func=AF.Copy),
                         reads=(r_ps,), writes=(r_Vsb,))
            k.dma("sp", KTn_d[li][:, :, key0:key0 + n].rearrange("h p t -> p h t"), KTn_all[:, :, :n], reads=(r_KTn,), writes=(r_kvdst,))
            k.dma("sp", KTr_d[li][:, :, key0:key0 + n].rearrange("h p t -> p h t"), KTr_all[0:64, :, :n], reads=(r_KTr,), writes=(r_kvdst,))
            cuts = sorted(set([0, n] + [128 * i for i in range(1, ntb)] + [g * 128 - key0 for g in range(key0 // 128 + 1, (key0 + n) // 128 + 1) if 0 < g * 128 - key0 < n]))
            for a_, b_ in zip(cuts[:-1], cuts[1:]):
                tb, lp0 = divmod(a_, 128)
                gb, gp0 = divmod(key0 + a_, 128)
                np_ = b_ - a_
                k.dma("sp", V_d[li][:, gp0:gp0 + np_, gb, :].rearrange("h p d -> p h d"),
                      Vsb[lp0:lp0 + np_, tb, :].rearrange("p (h d) -> p h d", h=H), reads=(r_Vsb,), writes=(r_kvdst,))

        def mixer_inputs(li, n, c0, with_b):
            nn = n - c0
            for pr in range(4):
                su, r_su = ws.get(w_in_loader(li, C_U + 256 * pr, 256))
                sc, r_sc = ws.get(w_in_loader(li, C_C + 256 * pr, 256), keep=1)
                for m in range(2):
                    ch = 2 * pr + m
                    ps, r_ps = ps_next()
                    svu = slot_view(su, NCD, 256)
                    k.mm([(ps[:, :nn], svu[:, kc, m * 128:(m + 1) * 128], hT[:, kc, c0:n], kc == 0, kc == NCD - 1) for kc in range(NCD)],
                         reads=(r_su, r_hT), writes=(r_ps,))
                    ub = usb[:, ch % 2, :]
                    k.op("act", lambda e, ps=ps, ub=ub: e.activation(out=ub[:, :nn], in_=ps[:, :nn], func=AF.Copy), reads=(r_ps,), writes=(r_usb,))
                    ps2, r_ps2 = ps_next()
                    svc = slot_view(sc, NCD, 256)
                    k.mm([(ps2[:, :nn], svc[:, kc, m * 128:(m + 1) * 128], hT[:, kc, c0:n], kc == 0, kc == NCD - 1) for kc in range(NCD)],
                         reads=(r_sc, r_hT), writes=(r_ps2,))
                    k.op("dve", lambda e, ps2=ps2, ub=ub, ch=ch: e.tensor_tensor(out=cu[:, ch, 2 + c0:2 + n], in0=ps2[:, :nn], in1=ub[:, :nn], op=ALU.mult),
                         reads=(r_ps2, r_usb), writes=(r_cu,))
                if with_b:
                    sbb, r_sbb = ws.get(w_in_loader(li, C_B + 256 * pr, 256))
                    svb = slot_view(sbb, NCD, 256)
                    for m in range(2):
                        ch = 2 * pr + m
                        ps3, r_ps3 = ps_next()
                        k.mm([(ps3[:, :nn], svb[:, kc, m * 128:(m + 1) * 128], hT[:, kc, c0:n], kc == 0, kc == NCD - 1) for kc in range(NCD)],
                             reads=(r_sbb, r_hT), writes=(r_ps3,))
                        k.op("act", lambda e, ps3=ps3, ch=ch: e.activation(out=bsb[:, ch, c0:n], in_=ps3[:, :nn], func=AF.Copy),
                             reads=(r_ps3,), writes=(r_bsb,))
            for pr in range(4):
                sp_, r_sp = ws.get(w_in_loader(li, C_POOL + 256 * pr, 256))
                svp = slot_view(sp_, NCD, 256)
                for m in range(2):
                    ch = 2 * pr + m
                    ps, r_ps = ps_next()
                    k.mm([(ps[:, :nn], svp[:, kc, m * 128:(m + 1) * 128], hT[:, kc, c0:n], kc == 0, kc == NCD - 1) for kc in range(NCD)],
                         reads=(r_sp, r_hT), writes=(r_ps,))
                    k.op("act", lambda e, ps=ps, ch=ch: e.activation(out=pin[:, ch, 16 + c0:16 + n], in_=ps[:, :nn], func=AF.Copy),
                         reads=(r_ps,), writes=(r_pin,))

        def save_halo(n):
            k.op("pool", lambda e: e.tensor_copy(out=halo_cu[:, :, :], in_=cu[:, :, n:n + 2]), reads=(r_cu,), writes=(r_hcu,))
            k.op("pool", lambda e: e.tensor_copy(out=halo_pin[:, :, 1:16], in_=pin[:, :, n + 1:n + 16]), reads=(r_pin,), writes=(r_hpin,))

        def restore_halo(scale_flag):
            if scale_flag:
                k.op("pool", lambda e: e.tensor_scalar(out=cu[:, :, 0:2], in0=halo_cu[:, :, :], scalar1=flag[:, 0:1], scalar2=None, op0=ALU.mult),
                     reads=(r_hcu, r_const), writes=(r_cu,))
                k.op("pool", lambda e: e.tensor_scalar(out=pin[:, :, 1:16], in0=halo_pin[:, :, 1:16], scalar1=flag[:, 0:1], scalar2=None, op0=ALU.mult),
                     reads=(r_hpin, r_const), writes=(r_pin,))
            else:
                k.op("pool", lambda e: e.tensor_copy(out=cu[:, :, 0:2], in_=halo_cu[:, :, :]), reads=(r_hcu,), writes=(r_cu,))
                k.op("pool", lambda e: e.tensor_copy(out=pin[:, :, 1:16], in_=halo_pin[:, :, 1:16]), reads=(r_hpin,), writes=(r_pin,))

        def q_path(li, n):
            for g in range(2):
                s, r_s = ws.get(w_in_loader(li, C_QL + 256 * g, 256))
                sv = slot_view(s, NCD, 256)
                for m in range(2):
                    ps, r_ps = ps_next()
                    k.mm([(ps[:, :n], sv[:, kc, m * 128:(m + 1) * 128], hT[:, kc, :n], kc == 0, kc == NCD - 1) for kc in range(NCD)],
                         reads=(r_s, r_hT), writes=(r_ps,))
                    c = 2 * g + m
                    k.op("dve", lambda e, c=c, ps=ps: e.tensor_copy(out=qlat[:, c, :n], in_=ps[:, :n]), reads=(r_ps,), writes=(r_qlat,))
            norm_fm(qlat, r_qlat, 4, n, sq4, r_sq4, li, P_QLN, qnT, r_qnT, 512.0, 2)
            for hg in range(4):
                s, r_s = ws.get(gen_loader(I["w_uq"][li][:, 768 * hg:768 * (hg + 1)], 4, 768, ("w_uq", li, hg)))
                sv = slot_view(s, 4, 768)
                for hh in range(4):
                    h = 4 * hg + hh
                    psn, r_psn = ps_next()
                    k.mm([(psn[:, :n], sv[:, kc, 192 * hh:192 * hh + 128], qnT[:, kc, :n], kc == 0, kc == 3) for kc in range(4)],
                         reads=(r_s, r_qnT), writes=(r_psn,))
                    psr, r_psr = ps_next()
                    k.mm([(psr[0:64, :n], sv[:, kc, 192 * hh + 128:192 * hh + 192], qnT[:, kc, :n], kc == 0, kc == 3) for kc in range(4)],
                         reads=(r_s, r_qnT), writes=(r_psr,))
                    sqn = sqq[:, h % 2, 0, :]
                    sqr = sqq[:, h % 2, 1, :]
                    k.op("act", lambda e, psn=psn, sqn=sqn: e.activation(out=sqn[:, :n], in_=psn[:, :n], func=AF.Square), reads=(r_psn,), writes=(r_sqq,))
                    k.op("act", lambda e, psr=psr, sqr=sqr: e.activation(out=sqr[0:64, :n], in_=psr[0:64, :n], func=AF.Square), reads=(r_psr,), writes=(r_sqq,))
                    pss, r_pss = ps_next()
                    k.mm([(pss[:, :n], ones[:, :], sqn[:, :n], True, False), (pss[:, :n], ones[0:64, :], sqr[0:64, :n], False, True)],
                         reads=(r_sqq, r_const), writes=(r_pss,))
                    rs, r_rs = rstd_from_ps(pss, r_pss, n, 192.0, h % 2)
                    k.op("dve", lambda e, psn=psn, rs=rs, h=h: e.scalar_tensor_tensor(out=QTn[:, h, :n], in0=psn[:, :n], scalar=pcol(li, P_QN),
                                                                                     in1=rs[:, :n], op0=ALU.mult, op1=ALU.mult),
                         reads=(r_psn, r_rs, r_par), writes=(r_QTn,))
                    rope_apply(psr, r_psr, n, li, P_QR, qx, r_qx, qxb, r_qxb, None)
                    k.op("pool", lambda e, rs=rs, h=h: e.tensor_tensor(out=QTr[0:64, h, :n], in0=qx[0:64, 0, :n], in1=rs[0:64, :n], op=ALU.mult),
                         reads=(r_qx, r_rs), writes=(r_QTr,))

        def attention(li, t, n):
            qs = t * TT
            qe = qs + n
            nkb = (qe + 127) // 128
            segs = [(0, min(nkb, SEG0_KB))]
            if nkb > SEG0_KB:
                segs.append((SEG0_KB, nkb))
            kv_reads = tuple(r_kv_d[li][0:t + 1])
            ring["set"] = [0, 1, 2, 3]
            loads = [(h, si) for h in range(H) for si in range(len(segs))]

            def issue_load(idx):
                h, si = loads[idx]
                b0, b1 = segs[si]
                k0 = 128 * b0
                k1 = min(128 * b1, qe)
                nk = k1 - k0
                (kn, r_kn), (kr, r_kr), (vv, r_vv) = Kn_s[idx % 2], Kr_s[idx % 2], V_s[idx % 2]
                k.dma("sp", kn[:, 0:nk], KTn_d[li][h, :, k0:k1], reads=kv_reads, writes=(r_kn,))
                k.dma("sp", kr[0:64, 0:nk], KTr_d[li][h, :, k0:k1], reads=kv_reads, writes=(r_kr,))
                nb_ = (nk + 127) // 128
                k.dma("sp", vv[:, 0:nb_, :], V_d[li][h, :, b0:b0 + nb_, :], reads=kv_reads, writes=(r_vv,))

            items = []
            for idx, (h, si) in enumerate(loads):
                b0, b1 = segs[si]
                for j in range(b0, b1):
                    items.append((idx, h, si, j))
            last_item_of_load = {}
            for n_, it in enumerate(items):
                last_item_of_load[it[0]] = n_
            G = 3
            issue_load(0)
            if len(loads) > 1:
                issue_load(1)
            stash = {}
            O, r_O = banks[6], r_bank[6]
            Dn, r_Dn = banks[7], r_bank[7]
            ring["set"] = [0, 1, 2, 3, 4, 5]

            def item_geom(n_):
                idx, h, si, j = items[n_]
                kp = min(128, qe - 128 * j, NKEY - 128 * j)
                cst = max(0, 128 * j - qs)
                diag = (128 * j + 128 > qs)
                bcls = 0 if t < NT else (0 if j < 16 else (1 if j == 16 else 2))
                return kp, cst, diag, bcls

            def stage_s_mm(n_, st, r_st):
                idx, h, si, j = items[n_]
                b0, b1 = segs[si]
                (kn, r_kn), (kr, r_kr), (vv, r_vv) = Kn_s[idx % 2], Kr_s[idx % 2], V_s[idx % 2]
                kp, cst, diag, bcls = item_geom(n_)
                kl = 128 * (j - b0)
                k.mm([(st[0:kp, cst:n], kn[:, kl:kl + kp], QTn[:, h, cst:n], True, False),
                      (st[0:kp, cst:n], kr[0:64, kl:kl + kp], QTr[0:64, h, cst:n], False, True)],
                     reads=(r_kn, r_kr, r_QTn, r_QTr), writes=(r_st,))

            def stage_s_exp(n_, st, r_st, pslot):
                idx, h, si, j = items[n_]
                kp, cst, diag, bcls = item_geom(n_)
                pt_, r_pt = PT[pslot], r_PT[pslot]
                k.op("act", lambda e: e.activation(out=pt_[0:kp, cst:n], in_=st[0:kp, cst:n], func=AF.Exp,
                                                   bias=(kbias[0:kp, j:j + 1] if t >= NT else zb[0:kp, 0:1]), scale=SCALE),
                     reads=(r_st, r_const), writes=(r_pt,))
                dend = min(n, 128 * j + 128 - qs)
                if diag:
                    c0 = qs + cst - 128 * j
                    k.op("pool", lambda e: e.tensor_tensor(out=pt_[0:kp, cst:dend], in0=pt_[0:kp, cst:dend], in1=tri[0:kp, c0:c0 + dend - cst], op=ALU.mult),
                         reads=(r_pt, r_const), writes=(r_pt,))
                stash[n_] = (kp, cst, pt_, r_pt)

            def stage_pv(n_):
                idx, h, si, j = items[n_]
                b0, b1 = segs[si]
                (kn, r_kn), (kr, r_kr), (vv, r_vv) = Kn_s[idx % 2], Kr_s[idx % 2], V_s[idx % 2]
                kp, cst, pt_, r_pt = stash.pop(n_)
                first = (j == 0)
                last = (j == nkb - 1)
                k.mm([(O[:, cst:n], vv[0:kp, j - b0, :], pt_[0:kp, cst:n], first, last),
                      (Dn[:, cst:n], ones[0:kp, :], pt_[0:kp, cst:n], first, last)], reads=(r_vv, r_pt, r_const), writes=(r_O, r_Dn))
                if last:
                    k.op("dve", lambda e: e.reciprocal(out=rden[:, :n], in_=Dn[:, :n]), reads=(r_Dn,), writes=(r_rden,))
                    k.op("dve", lambda e: e.tensor_tensor(out=ybT[:, h, :n], in0=O[:, :n], in1=rden[:, :n], op=ALU.mult),
                         reads=(r_O, r_rden), writes=(r_ybT,))
                if last_item_of_load[idx] == n_ and idx + 2 < len(loads):
                    issue_load(idx + 2)

            groups = []
            cur = []
            for n_ in range(len(items)):
                kp, cst, diag, bcls = item_geom(n_)
                full = (kp == 128 and cst == 0 and not diag)
                if cur:
                    kp0, cst0, diag0, bcls0 = item_geom(cur[0])
                    full0 = (kp0 == 128 and cst0 == 0 and not diag0)
                    if len(cur) == G or not (full and full0 and bcls == bcls0):
                        groups.append(cur)
                        cur = []
                cur.append(n_)
            if cur:
                groups.append(cur)
            for g_ in range(len(groups) + 1):
                if g_ < len(groups):
                    grp = groups[g_]
                    base = 3 * (g_ % 2)
                    pbase = 3 * (g_ % 3)
                    bks = [(banks[base + i], r_bank[base + i]) for i in range(len(grp))]
                    k.prewait("pe", reads=(), writes=tuple(rb for (_, rb) in bks))
                    for n_, (st, r_st) in zip(grp, bks):
                        stage_s_mm(n_, st, r_st)
                    kp, cst, diag, bcls = item_geom(grp[0])
                    if len(grp) > 1:
                        ng = len(grp)
                        j0 = items[grp[0]][3]
                        src = psall[:, 512 * base:512 * (base + ng)].rearrange("p (g c) -> p g c", g=ng)[:, :, 0:n]
                        dstp = PTall[:, pbase:pbase + ng, 0:n]
                        k.op("act", lambda e: e.activation(out=dstp, in_=src, func=AF.Exp,
                                                           bias=(kbias[:, j0:j0 + 1] if t >= NT else zb[:, 0:1]), scale=SCALE),
                             reads=tuple(rb for (_, rb) in bks) + (r_const,), writes=tuple(r_PT[pbase + i] for i in range(ng)))
                        for i, n_ in enumerate(grp):
                            stash[n_] = (128, 0, PT[pbase + i], r_PT[pbase + i])
                    else:
                        stage_s_exp(grp[0], bks[0][0], bks[0][1], pbase)
                if g_ >= 1:
                    grp = groups[g_ - 1]
                    k.prewait("pe", reads=tuple(stash[n_][3] for n_ in grp), writes=())
                    for n_ in grp:
                        stage_pv(n_)
            ring["set"] = list(range(8))

        def conv_pool(li, t, n):
            k.dma("sp", invc[:, :, :n], I["invc"][:, :, t * TT:t * TT + n], writes=(r_invc,))
            for ch in range(8):
                tmp = ctmp[:, ch % 2, :]
                k.op("pool", lambda e, ch=ch, tmp=tmp: e.tensor_scalar(out=tmp[:, :n], in0=cu[:, ch, 0:n], scalar1=pcol(li, P_CW + 3 * ch), scalar2=None, op0=ALU.mult),
                     reads=(r_cu, r_par), writes=(r_ctmp,))
                k.op("dve", lambda e, ch=ch, tmp=tmp: e.scalar_tensor_tensor(out=tmp[:, :n], in0=cu[:, ch, 1:n + 1], scalar=pcol(li, P_CW + 3 * ch + 1),
                                                                             in1=tmp[:, :n], op0=ALU.mult, op1=ALU.add),
                     reads=(r_cu, r_par, r_ctmp), writes=(r_ctmp,))
                k.op("dve", lambda e, ch=ch, tmp=tmp: e.scalar_tensor_tensor(out=tmp[:, :n], in0=cu[:, ch, 2:n + 2], scalar=pcol(li, P_CW + 3 * ch + 2),
                                                                             in1=tmp[:, :n], op0=ALU.mult, op1=ALU.add),
                     reads=(r_cu, r_par, r_ctmp), writes=(r_ctmp,))
                k.op("dve", lambda e, ch=ch, tmp=tmp: e.tensor_tensor(out=yaT[:, ch, :n], in0=tmp[:, :n], in1=bsb[:, ch, :n], op=ALU.mult),
                     reads=(r_ctmp, r_bsb), writes=(r_yaT,))
            for g in range(4):
                src = pin[:, 2 * g:2 * g + 2, :]
                cur = src
                r_cur = r_pin
                lo = 1
                bufs = [(pw1, r_pw1), (pw2, r_pw2)]
                for step in range(g + 1):
                    sh = 1 << step
                    dstb, r_dst = bufs[step % 2]
                    lo2 = lo + sh
                    k.op("pool", lambda e, cur=cur, dstb=dstb, lo2=lo2, sh=sh: e.tensor_tensor(out=dstb[:, :, lo2:n + 16], in0=cur[:, :, lo2:n + 16],
                                                                                             in1=cur[:, :, lo2 - sh:n + 16 - sh], op=ALU.add),
                         reads=(r_cur,), writes=(r_dst,))
                    cur, r_cur, lo = dstb, r_dst, lo2
                for m in range(2):
                    ch = 2 * g + m
                    tmp = ctmp[:, m, :]
                    k.op("dve", lambda e, cur=cur, m=m, g=g, tmp=tmp: e.tensor_tensor(out=tmp[:, :n], in0=cur[:, m, 16:16 + n], in1=invc[:, g, :n], op=ALU.mult),
                         reads=(r_cur, r_invc), writes=(r_ctmp,))
                    k.op("dve", lambda e, ch=ch, tmp=tmp: e.tensor_tensor(out=pooled[:, ch, :n], in0=tmp[:, :n], in1=pin[:, ch, 16:16 + n], op=ALU.subtract),
                         reads=(r_ctmp, r_pin), writes=(r_pooled,))
            def pw_loader(s):
                return [(s[:, 0:2048].rearrange("p (g k n) -> p g k n", g=4, k=2), I["pool_w"][li].rearrange("g (k p) n -> p g k n", p=128))]
            pw_loader.key = ("pool_w", li)
            pw_loader.nel = 2048
            s, r_s = ws.get(pw_loader)
            sv = s[:, 0:2048].rearrange("p (g k n) -> p g k n", g=4, k=2)
            for g in range(4):
                for m in range(2):
                    ps, r_ps = ps_next()
                    k.mm([(ps[:, :n], sv[:, g, kc, m * 128:(m + 1) * 128], pooled[:, 2 * g + kc, :n], kc == 0, kc == 1) for kc in range(2)],
                         reads=(r_s, r_pooled), writes=(r_ps,))
                    ch = 2 * g + m
                    k.op("act", lambda e, ps=ps, ch=ch: e.mul(out=ycT[:, ch, :n], in_=ps[:, :n], mul=pcol(li, P_PS + ch)),
                         reads=(r_ps, r_par), writes=(r_ycT,))

        def merge_and_wo(li, n):
            for jj in range(8):
                for i in range(3):
                    sg, r_sg = ws.get(w_in_loader(li, C_GATE + 2048 * i + 256 * jj, 256))
                    svg = slot_view(sg, NCD, 256)
                    wsrc, nk, src, r_src = [(I["w_a"], 8, yaT, r_yaT), (I["w_b"], 16, ybT, r_ybT), (I["w_c"], 8, ycT, r_ycT)][i]
                    sw_, r_sw = ws.get(gen_loader(wsrc[li][:, 256 * jj:256 * (jj + 1)], nk, 256, ("w_br", li, i, jj)), keep=1)
                    svb = slot_view(sw_, nk, 256)
                    for m in range(2):
                        ps, r_ps = ps_next()
                        k.mm([(ps[:, :n], svg[:, kc, m * 128:(m + 1) * 128], hT[:, kc, :n], kc == 0, kc == NCD - 1) for kc in range(NCD)],
                             reads=(r_sg, r_hT), writes=(r_ps,))
                        k.op("act", lambda e, ps=ps, i=i: e.activation(out=gate[i][:, :n], in_=ps[:, :n], func=AF.Sigmoid), reads=(r_ps,), writes=(r_gate[i],))
                        ps2, r_ps2 = ps_next()
                        k.mm([(ps2[:, :n], svb[:, kc, m * 128:(m + 1) * 128], src[:, kc, :n], kc == 0, kc == nk - 1) for kc in range(nk)],
                             reads=(r_sw, r_src), writes=(r_ps2,))
                        mt, r_mt = mtmp[2 * i + m], r_mtmp[2 * i + m]
                        k.op("dve", lambda e, ps2=ps2, i=i, mt=mt: e.tensor_tensor(out=mt[:, :n], in0=ps2[:, :n], in1=gate[i][:, :n], op=ALU.mult),
                             reads=(r_ps2, r_gate[i]), writes=(r_mt,))
                for m in range(2):
                    oc = 2 * jj + m
                    k.op("pool", lambda e, m=m: e.tensor_tensor(out=mtmp[m][:, :n], in0=mtmp[m][:, :n], in1=mtmp[2 + m][:, :n], op=ALU.add),
                         reads=(r_mtmp[m], r_mtmp[2 + m]), writes=(r_mtmp[m],))
                    k.op("pool", lambda e, oc=oc, m=m: e.tensor_tensor(out=merged[:, oc, :n], in0=mtmp[m][:, :n], in1=mtmp[4 + m][:, :n], op=ALU.add),
                         reads=(r_mtmp[m], r_mtmp[4 + m]), writes=(r_merged,))
            for jj in range(8):
                s, r_s = ws.get(gen_loader(I["w_o"][li][:, 256 * jj:256 * (jj + 1)], 16, 256, ("w_o", li, jj)))
                sv = slot_view(s, 16, 256)
                for m in range(2):
                    oc = 2 * jj + m
                    ps, r_ps = ps_next()
                    k.mm([(ps[:, :n], sv[:, kc, m * 128:(m + 1) * 128], merged[:, kc, :n], kc == 0, kc == NCD - 1) for kc in range(NCD)],
                         reads=(r_s, r_merged), writes=(r_ps,))
                    k.op("dve", lambda e, ps=ps, oc=oc: e.tensor_tensor(out=xT[:, oc, :n], in0=ps[:, :n], in1=xT[:, oc, :n], op=ALU.add),
                         reads=(r_ps, r_xT), writes=(r_xT,))

        def mlp(li, n):
            norm_fm(xT, r_xT, NCD, n, sq16, r_sq16, li, P_MN, hT, r_hT, float(D), 2)
            for jj in range(32):
                s, r_s = ws.get(gen_loader(I["w_up"][li][:, 256 * jj:256 * (jj + 1)], 16, 256, ("w_up", li, jj)))
                sv = slot_view(s, 16, 256)
                for m in range(2):
                    oc = 2 * jj + m
                    ps, r_ps = ps_next()
                    k.mm([(ps[:, :n], sv[:, kc, m * 128:(m + 1) * 128], hT[:, kc, :n], kc == 0, kc == NCD - 1) for kc in range(NCD)],
                         reads=(r_s, r_hT), writes=(r_ps,))
                    rr, r_rr = rl[oc % 2], r_rl[oc % 2]
                    k.op("act", lambda e, ps=ps, rr=rr: e.activation(out=rr[:, :n], in_=ps[:, :n], func=AF.Relu), reads=(r_ps,), writes=(r_rr,))
                    k.op("pool", lambda e, rr=rr, oc=oc: e.tensor_tensor(out=act[:, oc, :n], in0=rr[:, :n], in1=rr[:, :n], op=ALU.mult),
                         reads=(r_rr,), writes=(r_act,))
            for oc in range(NCD):
                ps, r_ps = ps_next()
                for half in range(2):
                    s, r_s = ws.get(gen_loader(I["w_down"][li][4096 * half:4096 * (half + 1), 128 * oc:128 * (oc + 1)], 32, 128, ("w_down", li, oc, half)))
                    sv = slot_view(s, 32, 128)
                    k.mm([(ps[:, :n], sv[:, kc, :], act[:, 32 * half + kc, :n], (half == 0 and kc == 0), (half == 1 and kc == 31)) for kc in range(32)],
                         reads=(r_s, r_act), writes=(r_ps,))
                k.op("dve", lambda e, ps=ps, oc=oc: e.tensor_tensor(out=xT[:, oc, :n], in0=ps[:, :n], in1=xT[:, oc, :n], op=ALU.add),
                     reads=(r_ps, r_xT), writes=(r_xT,))

        def layer(li, src, r_src_list, kv_only_tiles, dst_fn):
            k.op("pool", lambda e: e.memset(halo_cu[:], 0.0), writes=(r_hcu,))
            k.op("pool", lambda e: e.memset(halo_pin[:], 0.0), writes=(r_hpin,))
            for t in range(2 * NT):
                n = TT
                k.dma("sp", xT[:, :, :n], src.rearrange("(c p) t -> p c t", p=128)[:, :, t * TT:t * TT + n],
                      reads=tuple(r_src_list[t:t + 1]), writes=(r_xT,))
                load_rope(t * TT, n)
                norm_fm(xT, r_xT, NCD, n, sq16, r_sq16, li, P_AN, hT, r_hT, float(D), 2)
                kv_path(li, n, t * TT, r_kv_d[li][t])
                if t < kv_only_tiles:
                    if t == kv_only_tiles - 1:
                        mixer_inputs(li, n, n - 16, False)
                        save_halo(n)
                    continue
                q_path(li, n)
                attention(li, t, n)
                restore_halo(t == NT)
                mixer_inputs(li, n, 0, True)
                save_halo(n)
                conv_pool(li, t, n)
                merge_and_wo(li, n)
                mlp(li, n)
                d_ap, r_d = dst_fn(t)
                k.dma("sp", d_ap, xT[:, :, :n], reads=(r_xT,), writes=(r_d,))

        def program():
            ring["p"] = 0
            ring["set"] = list(range(8))
            load_consts()
            for li in range(NL):
                last = (li == NL - 1)
                if li == 0:
                    src, r_src = I["xall"], []
                    kvo = KV_ONLY0
                else:
                    src, r_src = X1_d[li - 1], r_x1[li - 1]
                    kvo = NT
                if last:
                    dst_fn = lambda t: (OUT.rearrange("(c p) t -> p c t", p=128)[:, :, (t - NT) * TT:(t - NT + 1) * TT], r_out[t - NT])
                else:
                    dst_fn = lambda t, li=li: (X1_d[li].rearrange("(c p) t -> p c t", p=128)[:, :, t * TT:(t + 1) * TT], r_x1[li][t])
                layer(li, src, r_src, kvo, dst_fn)
            k.finish(r_out)

        k.dry = True
        program()
        ws.pos = 0
        k.dry = False
        program()
        print("[kernel] sbuf bytes remaining", nc.sbuf_bytes_remaining, flush=True)
        print(f"[kernel] instructions={k.n_ins} waits={k.n_wait} weight_slots={len(ws.reqs)}", flush=True)
    return nc


def _rope_tables():
    pos = np.arange(TALL, dtype=np.float32)
    inv = (np.float32(10000.0) ** (-np.arange(0, 64, 2, dtype=np.float32) / np.float32(64))).astype(np.float32)
    ang = (pos[:, None] * inv[None, :]).astype(np.float32)
    c, s = np.cos(ang).astype(np.float32), np.sin(ang).astype(np.float32)
    C = np.concatenate([c, c], 1).T
    S = np.concatenate([-s, s], 1).T
    return np.ascontiguousarray(C), np.ascontiguousarray(S)


def _params(inp, l):
    P = np.zeros((128, NPARAM), np.float32)
    P[:, P_AN:P_AN + 16] = inp["attn_norm"][l].reshape(16, 128).T
    P[:, P_MN:P_MN + 16] = inp["mlp_norm"][l].reshape(16, 128).T
    P[:, P_QLN:P_QLN + 4] = inp["q_lat_norm"][l].reshape(4, 128).T
    P[:, P_KVLN:P_KVLN + 4] = inp["kv_lat_norm"][l].reshape(4, 128).T
    P[:, P_QN] = inp["q_norm"][l][:128]
    P[:64, P_QR] = inp["q_norm"][l][128:]
    P[:, P_KN] = inp["k_norm"][l][:128]
    P[:64, P_KR] = inp["k_norm"][l][128:]
    cw = inp["conv_w"][l]
    for ch in range(8):
        for j in range(3):
            P[:, P_CW + 3 * ch + j] = cw[j, ch * 128:(ch + 1) * 128]
    P[:, P_PS:P_PS + 8] = inp["pool_scale"][l].reshape(8, 128).T
    return P


_NC_CACHE = {}


def _get_nc(layers, fused):
    key = (tuple(layers), fused)
    if key not in _NC_CACHE:
        _NC_CACHE[key] = build_program(list(layers), fused)
    return _NC_CACHE[key]


def _run(layers, inp, x_own, x_pre, fused):
    nc = _get_nc(layers, fused)
    C, S = _rope_tables()
    bf = ml_dtypes.bfloat16
    tri = (np.arange(128)[None, :] >= np.arange(128)[:, None]).astype(bf)
    ones = np.ones((128, 128), bf)
    swap = np.zeros((64, 64), bf)
    for i in range(32):
        swap[i + 32, i] = 1
        swap[i, i + 32] = 1
    ls = list(layers)
    common = {
        "params": np.stack([_params(inp, l) for l in ls]),
        "tri": tri, "ones": ones, "swap": swap,
    }
    for i, l in enumerate(ls):
        for nm in ["w_in", "w_uq", "w_ukv", "pool_w", "w_branch_a", "w_branch_b", "w_branch_c", "w_o", "w_up", "w_down"]:
            common[f"{nm}{i}"] = np.ascontiguousarray(inp[nm][l])
    in_maps = []
    for c in range(8):
        r = c % 2
        pos0 = r * NTOK
        ropeC = np.concatenate([C[:, 0:NPRE], C[:, pos0:pos0 + NTOK]], 1)
        ropeS = np.concatenate([S[:, 0:NPRE], S[:, pos0:pos0 + NTOK]], 1)
        tpos = np.concatenate([np.arange(0, NPRE), np.arange(pos0, pos0 + NTOK)]).astype(np.float32) + 1.0
        invc = np.stack([np.float32(1.0) / np.minimum(tpos, np.float32(w)) for w in (2, 4, 8, 16)]).astype(np.float32)
        invc = np.ascontiguousarray(np.broadcast_to(invc[None], (128, 4, NKEY)))
        kb = np.zeros((128, NKB), np.float32)
        if r == 0:
            keyidx = np.arange(NKB * 128).reshape(NKB, 128).T
            kb[keyidx < NPRE] = -30000.0
        m = dict(common)
        m.update({"xall": np.ascontiguousarray(np.concatenate([x_pre[c], x_own[c]], axis=1)),
                  "ropeC": np.ascontiguousarray(ropeC), "ropeS": np.ascontiguousarray(ropeS), "invc": invc,
                  "kbias": kb, "flag": np.full((128, 1), float(r), np.float32)})
        in_maps.append(m)
    res = run_bass_kernel_spmd(nc, in_maps, core_ids=list(range(8)))
    if DEBUG:
        global DBG_OUT
        DBG_OUT = [{nm: np.asarray(res.results[c][nm]) for nm in DBG_NAMES} for c in range(8)]
    return [np.asarray(res.results[c]["outT"]) for c in range(8)]


FUSED = True


def kernel(**inp):
    inp = {k_: np.asarray(v) for k_, v in inp.items()}
    x = inp["x"].astype(np.float32)
    B = x.shape[0]
    meta = np.broadcast_to(inp["meta_tokens"][None].astype(np.float32), (B, NMETA, D))
    hseq = np.concatenate([meta, x], axis=1)
    own = [np.ascontiguousarray(hseq[c // 2, (c % 2) * NTOK:(c % 2 + 1) * NTOK].T) for c in range(8)]
    zero = np.zeros((D, NPRE), np.float32)
    if FUSED:
        pre = [zero if c % 2 == 0 else own[c - 1] for c in range(8)]
        outs = _run([0, 1], inp, own, pre, True)
    else:
        cur = own
        for l in range(2):
            pre = [zero if c % 2 == 0 else cur[c - 1] for c in range(8)]
            cur = _run([l], inp, cur, pre, False)
        outs = cur
    full = np.stack([np.concatenate([outs[2 * b].T, outs[2 * b + 1].T], axis=0) for b in range(B)])
    return np.ascontiguousarray(full[:, NMETA:, :]).astype(np.float32)
```

```python
import contextlib
import numpy as np
import ml_dtypes
import concourse.bass as bass
import concourse.mybir as mybir
from concourse.bass_utils import run_bass_kernel_spmd

F32 = mybir.dt.float32
BF16 = mybir.dt.bfloat16
AF = mybir.ActivationFunctionType
ALU = mybir.AluOpType

D = 2048
NCD = 16
H = 16
SEQ = 4096
NMETA = 16
TALL = SEQ + NMETA
NTOK = TALL // 2
NPRE = NTOK
TT = 257
NT = NTOK // TT
NKEY = NPRE + NTOK
NKB = (NKEY + 127) // 128
SEG0_KB = 16
SEGK = 2064
D_IN = 11328
C_U, C_B, C_C, C_QL, C_KVL, C_KR, C_POOL, C_GATE = 0, 1024, 2048, 3072, 3584, 4096, 4160, 5184
DFF = 8192
EPS = 1e-6
SCALE = 192.0 ** -0.5
SLOT = 4096
NSLOT = 6
P_AN, P_MN, P_QLN, P_KVLN, P_QN, P_QR, P_KN, P_KR, P_CW, P_PS = 0, 16, 32, 36, 40, 41, 42, 43, 44, 68
NPARAM = 76


class Res:
    __slots__ = ("name", "excl", "w", "rs", "ov")

    def __init__(self, name, excl=False):
        self.name = name
        self.excl = excl
        self.w = None
        self.rs = {}
        self.ov = [self]


class Eng:
    def __init__(self, name, eng, sem):
        self.name, self.eng, self.sem = name, eng, sem
        self.cnt = 0
        self.seen = {}


class KB:
    def __init__(self, nc, es):
        self.nc = nc
        self.dry = False
        mk = lambda n: es.enter_context(nc.semaphore(n))
        self.E = {
            "pe": Eng("pe", nc.tensor, mk("s_pe")),
            "act": Eng("act", nc.scalar, mk("s_act")),
            "dve": Eng("dve", nc.vector, mk("s_dve")),
            "pool": Eng("pool", nc.gpsimd, mk("s_pool")),
            "sp": Eng("sp", nc.sync, mk("s_sp")),
        }
        self.dma_sems = {"sp": [mk(f"d_sp{i}") for i in range(12)], "pool": [mk(f"d_pl{i}") for i in range(8)]}
        self.dma_cnt = {}
        self.dma_rr = {"sp": 0, "pool": 0}
        self.n_wait = 0
        self.n_ins = 0

    def _wait(self, e, tk):
        sem, val = tk
        if e.seen.get(sem.num, 0) >= val:
            return
        e.eng.wait_ge(sem, val)
        e.seen[sem.num] = val
        self.n_wait += 1

    def _deps(self, reads, writes):
        tks = []
        for r in reads:
            for res in r.ov:
                if res.w is not None:
                    tks.append((res.w, False))
                if res.excl:
                    tks.extend((t, True) for t in res.rs.values())
        for w in writes:
            for res in w.ov:
                if res.w is not None:
                    tks.append((res.w, False))
                tks.extend((t, False) for t in res.rs.values())
        return tks

    def _mark(self, tk, reads, writes):
        for r in reads:
            r.rs[tk[0].num] = tk
        for w in writes:
            w.w = tk
            w.rs = {}

    def op(self, en, fn, reads=(), writes=()):
        if self.dry:
            return
        e = self.E[en]
        for tk, rr in self._deps(reads, writes):
            if tk[0] is e.sem and (rr or en == "pe"):
                continue
            self._wait(e, tk)
        ins = fn(e.eng)
        e.cnt += 1
        ins.then_inc(e.sem, 1)
        self.n_ins += 1
        self._mark((e.sem, e.cnt), reads, writes)

    def prewait(self, en, reads=(), writes=()):
        if self.dry:
            return
        e = self.E[en]
        best = {}
        for tk, rr in self._deps(reads, writes):
            if tk[0] is e.sem:
                continue
            if tk[0].num not in best or best[tk[0].num][1] < tk[1]:
                best[tk[0].num] = tk
        for tk in best.values():
            self._wait(e, tk)

    def mm(self, mms, reads, writes):
        if self.dry:
            return
        e = self.E["pe"]
        for tk, rr in self._deps(reads, writes):
            if tk[0] is e.sem:
                continue
            self._wait(e, tk)
        ins = None
        for (o, l, r, st, sp) in mms:
            ins = e.eng.matmul(o, lhsT=l, rhs=r, start=st, stop=sp)
            self.n_ins += 1
        e.cnt += 1
        ins.then_inc(e.sem, 1)
        self._mark((e.sem, e.cnt), reads, writes)

    def dma(self, q, out, in_, reads=(), writes=()):
        if self.dry:
            return
        e = self.E[q]
        for tk, rr in self._deps(reads, writes):
            self._wait(e, tk)
        pool = self.dma_sems[q]
        i = self.dma_rr[q]
        self.dma_rr[q] = (i + 1) % len(pool)
        sem = pool[i]
        prev = self.dma_cnt.get(sem.num, 0)
        if prev:
            self._wait(e, (sem, prev))
        ins = e.eng.dma_start(out=out, in_=in_)
        ins.then_inc(sem, 16)
        self.dma_cnt[sem.num] = prev + 16
        self.n_ins += 1
        self._mark((sem, prev + 16), reads, writes)

    def finish(self, res_list):
        e = self.E["sp"]
        for r in res_list:
            if r.w is not None:
                self._wait(e, r.w)


class Region:
    def __init__(self, tile, nelem):
        self.t = tile
        self.n = nelem
        self.off = 0
        self.phase = None
        self.bufs = []

    def set_phase(self, ph):
        self.phase = ph
        self.off = 0

    def take(self, name, free_shape, dtype):
        n = int(np.prod(free_shape))
        nb = n * (2 if dtype == F32 else 1)
        self.off = (self.off + 15) // 16 * 16
        off = self.off
        self.off += nb
        assert self.off <= self.n, (name, self.off, self.n)
        v = self.t[:, off:off + nb]
        if dtype == F32:
            v = v.bitcast(F32)
        if len(free_shape) == 2:
            v = v.rearrange("p (a b) -> p a b", a=free_shape[0])
        elif len(free_shape) == 3:
            v = v.rearrange("p (a b c) -> p a b c", a=free_shape[0], b=free_shape[1])
        r = Res(name)
        for (ph, r2) in self.bufs:
            if ph != self.phase:
                r.ov.append(r2)
                r2.ov.append(r)
        self.bufs.append((self.phase, r))
        return v, r


class WStream:
    def __init__(self, k, slots, slot_res, cache_fn):
        self.k = k
        self.slots = slots
        self.res = slot_res
        self.reqs = []
        self.issued = 0
        self.pos = 0
        self.cache_fn = cache_fn
        self.cache_idx = {}
        self.cache_res = {}
        self.loaded = set()

    def get(self, loader, keep=0):
        k = self.k
        i = self.pos
        self.pos += 1
        if k.dry:
            self.reqs.append(loader)
            if loader.key not in self.cache_idx:
                self.cache_idx[loader.key] = len(self.cache_idx)
                self.cache_res[loader.key] = Res("wc%d" % len(self.cache_idx))
            return self.slots[i % NSLOT], self.res[i % NSLOT]
        assert keep < NSLOT - 1
        lim = min(len(self.reqs), i - keep + NSLOT)
        while self.issued < lim:
            j = self.issued
            s, r = self.slots[j % NSLOT], self.res[j % NSLOT]
            ld = self.reqs[j]
            ci = self.cache_idx[ld.key]
            cr = self.cache_res[ld.key]
            if ld.key not in self.loaded:
                self.loaded.add(ld.key)
                for (o, src) in ld(s):
                    k.dma("pool", o, src, reads=(), writes=(r,))
                k.dma("sp", self.cache_fn(ci)[:, 0:ld.nel], s[:, 0:ld.nel], reads=(r,), writes=(cr,))
            else:
                k.dma("sp" if (j % 2) else "pool", s[:, 0:ld.nel], self.cache_fn(ci)[:, 0:ld.nel], reads=(cr,), writes=(r,))
            self.issued += 1
        return self.slots[i % NSLOT], self.res[i % NSLOT]


def slot_view(s, kc, ncols):
    return s[:, 0:kc * ncols].rearrange("p (k n) -> p k n", k=kc)


DEBUG = False
DBG_NAMES = []


def build_program(layers, fused):
    KV_ONLY0 = 0 if len(layers) > 1 else NT
    nc = bass.Bass("TRN2", target_bir_lowering=False)
    NL = len(layers)
    dt = lambda name, shape, dtype=F32, kind="ExternalInput": nc.dram_tensor(name, list(shape), dtype, kind=kind).ap()
    I = {}
    I["xall"] = dt("xall", [D, NKEY])
    I["w_in"] = [dt(f"w_in{l}", [D, D_IN]) for l in range(NL)]
    I["w_uq"] = [dt(f"w_uq{l}", [512, 3072]) for l in range(NL)]
    I["w_ukv"] = [dt(f"w_ukv{l}", [512, 4096]) for l in range(NL)]
    I["pool_w"] = [dt(f"pool_w{l}", [4, 256, 256]) for l in range(NL)]
    I["w_a"] = [dt(f"w_branch_a{l}", [1024, D]) for l in range(NL)]
    I["w_b"] = [dt(f"w_branch_b{l}", [D, D]) for l in range(NL)]
    I["w_c"] = [dt(f"w_branch_c{l}", [1024, D]) for l in range(NL)]
    I["w_o"] = [dt(f"w_o{l}", [D, D]) for l in range(NL)]
    I["w_up"] = [dt(f"w_up{l}", [D, DFF]) for l in range(NL)]
    I["w_down"] = [dt(f"w_down{l}", [DFF, D]) for l in range(NL)]
    I["params"] = dt("params", [NL, 128, NPARAM])
    I["ropeC"] = dt("ropeC", [64, NKEY])
    I["ropeS"] = dt("ropeS", [64, NKEY])
    I["invc"] = dt("invc", [128, 4, NKEY])
    I["kbias"] = dt("kbias", [128, NKB])
    I["flag"] = dt("flag", [128, 1])
    I["tri"] = dt("tri", [128, 128], BF16)
    I["ones"] = dt("ones", [128, 128], BF16)
    I["swap"] = dt("swap", [64, 64], BF16)
    OUT = dt("outT", [D, NTOK], F32, kind="ExternalOutput")
    KTn_d = [dt(f"ktn{l}", [H, 128, NKEY], BF16, kind="Internal") for l in range(NL)]
    KTr_d = [dt(f"ktr{l}", [H, 64, NKEY], BF16, kind="Internal") for l in range(NL)]
    V_d = [dt(f"v{l}", [H, 128, NKB, 128], BF16, kind="Internal") for l in range(NL)]
    X1_d = [dt(f"x1_{l}", [D, NKEY], F32, kind="Internal") for l in range(NL - 1)]

    with contextlib.ExitStack() as es:
        k = KB(nc, es)
        sb = lambda name, shape, dtype: es.enter_context(nc.sbuf_tensor(name, list(shape), dtype))
        xT = sb("xT_sb", [128, NCD, TT], F32); r_xT = Res("xT")
        hT = sb("hT_sb", [128, NCD, TT], BF16); r_hT = Res("hT")
        ybT = sb("ybT_sb", [128, H, TT], BF16); r_ybT = Res("ybT")
        wsl = [sb(f"wslot{i}", [128, SLOT], BF16) for i in range(NSLOT)]
        r_wsl = [Res(f"wslot{i}") for i in range(NSLOT)]
        params = sb("params_sb", [128, NL, NPARAM], F32); r_par = Res("params")
        tri = sb("tri_sb", [128, 128], BF16)
        ones = sb("ones_sb", [128, 128], BF16)
        swp = sb("swap_sb", [64, 64], BF16)
        kbias = sb("kbias_sb", [128, NKB], F32)
        flag = sb("flag_sb", [128, 1], F32)
        epsb = sb("eps_sb", [128, 1], F32)
        zb = sb("zb_sb", [128, 1], F32)
        r_const = Res("consts")
        rstd = [sb(f"rstd{i}", [128, TT], F32) for i in range(3)]
        r_rstd = [Res(f"rstd{i}") for i in range(3)]
        lnv = [sb(f"lnv{i}", [128, TT], F32) for i in range(2)]
        r_lnv = [Res(f"lnv{i}") for i in range(2)]
        ropeC = sb("ropeC_sb", [64, TT], F32)
        ropeS = sb("ropeS_sb", [64, TT], F32)
        r_rope = Res("rope")
        invc = sb("invc_sb", [128, 4, TT], F32); r_invc = Res("invc")
        PTall = sb("ptall", [128, 9, TT], BF16)
        PT = [PTall[:, i, :] for i in range(9)]
        r_PT = [Res(f"pt{i}") for i in range(9)]
        rden = sb("rden", [128, TT], F32); r_rden = Res("rden")
        rl = [sb(f"relu{i}", [128, TT], F32) for i in range(2)]
        r_rl = [Res(f"relu{i}") for i in range(2)]
        halo_cu = sb("halo_cu", [128, 8, 2], BF16); r_hcu = Res("halo_cu")
        halo_pin = sb("halo_pin", [128, 8, 16], F32); r_hpin = Res("halo_pin")
        U1N = 24 * 1024
        U2N = 28 * 1024
        U1 = Region(sb("U1", [128, U1N], BF16), U1N)
        U2 = Region(sb("U2", [128, U2N], BF16), U2N)
        U1.set_phase("kv")
        kvl, r_kvl = U1.take("kvl", [5, TT], F32)
        sq5, r_sq5 = U1.take("sq5", [5, TT], BF16)
        kvnT, r_kvnT = U1.take("kvnT", [4, TT], BF16)
        KTn_all, r_KTn = U1.take("KTn_all", [H, TT], BF16)
        KTr_all, r_KTr = U1.take("KTr_all", [H, TT], BF16)
        Vsb, r_Vsb = U1.take("Vsb", [3, D], BF16)
        krg, r_krg = U1.take("krg", [TT], F32)
        kx, r_kx = U1.take("kx", [2, TT], F32)
        kxb, r_kxb = U1.take("kxb", [TT], BF16)
        sqh, r_sqh = U1.take("sqh", [2, 2, TT], BF16)
        U1.set_phase("mix")
        usb, r_usb = U1.take("usb", [2, TT], F32)
        cu, r_cu = U1.take("cu", [8, TT + 2], BF16)
        bsb, r_bsb = U1.take("bsb", [8, TT], BF16)
        ctmp, r_ctmp = U1.take("ctmp", [2, TT], F32)
        yaT, r_yaT = U1.take("yaT", [8, TT], BF16)
        pin, r_pin = U1.take("pin", [8, TT + 16], F32)
        pw1, r_pw1 = U1.take("pw1", [2, TT + 16], F32)
        pw2, r_pw2 = U1.take("pw2", [2, TT + 16], F32)
        pooled, r_pooled = U1.take("pooled", [8, TT], BF16)
        ycT, r_ycT = U1.take("ycT", [8, TT], BF16)
        merged, r_merged = U1.take("merged", [NCD, TT], BF16)
        U2.set_phase("att")
        qlat, r_qlat = U2.take("qlat", [4, TT], F32)
        sq4, r_sq4 = U2.take("sq4", [4, TT], BF16)
        qnT, r_qnT = U2.take("qnT", [4, TT], BF16)
        QTn, r_QTn = U2.take("QTn", [H, TT], BF16)
        QTr, r_QTr = U2.take("QTr", [H, TT], BF16)
        qx, r_qx = U2.take("qx", [2, TT], F32)
        qxb, r_qxb = U2.take("qxb", [TT], BF16)
        sqq, r_sqq = U2.take("sqq", [2, 2, TT], BF16)
        Kn_s = []; Kr_s = []; V_s = []
        for i in range(2):
            a, ra = U2.take(f"Kn_s{i}", [SEGK], BF16)
            b_, rb = U2.take(f"Kr_s{i}", [SEGK], BF16)
            c_, rc = U2.take(f"V_s{i}", [17, 128], BF16)
            Kn_s.append((a, ra)); Kr_s.append((b_, rb)); V_s.append((c_, rc))
        U2.set_phase("mrg")
        gate = []; r_gate = []; mtmp = []; r_mtmp = []
        for i in range(3):
            a, ra = U2.take(f"gate{i}", [TT], F32)
            gate.append(a); r_gate.append(ra)
        for i in range(6):
            a, ra = U2.take(f"mtmp{i}", [TT], F32)
            mtmp.append(a); r_mtmp.append(ra)
        U2.set_phase("mlp")
        sq16, r_sq16 = U2.take("sq16", [NCD // 2, TT], BF16)
        sq16b, r_sq16b = U2.take("sq16b", [NCD // 2, TT], BF16)
        act, r_act = U2.take("act", [64, TT], BF16)
        psall = es.enter_context(nc.psum_tensor("psall", [128, 8 * 512], F32))
        banks = [psall[:, 512 * i:512 * (i + 1)] for i in range(8)]
        r_bank = [Res(f"bank{i}", excl=True) for i in range(8)]
        ring = {"set": list(range(8)), "p": 0}

        def ps_next():
            i = ring["set"][ring["p"] % len(ring["set"])]
            ring["p"] += 1
            return banks[i], r_bank[i]

        WC_d = [dt(f"wcache{i}", [150, 128, SLOT], BF16, kind="Internal") for i in range(NL)]
        ws = WStream(k, wsl, r_wsl, lambda ci: WC_d[ci // 150][ci % 150])
        cc_sem = es.enter_context(nc.semaphore("cc_sem"))

        def dump(name, ap, shape, dtype, r):
            if not DEBUG or k.dry:
                return
            dram = nc.dram_tensor("dbg_" + name, list(shape), dtype, kind="ExternalOutput").ap()
            DBG_NAMES.append("dbg_" + name)
            k.dma("sp", dram, ap, reads=(r,), writes=(Res("dbg_" + name),))

        r_kv_d = [[Res(f"kvd{l}_{t}") for t in range(2 * NT)] for l in range(NL)]
        r_x1 = [[Res(f"x1_{l}_{t}") for t in range(2 * NT)] for l in range(NL)]
        r_out = [Res(f"out_{t}") for t in range(NT)]

        def V(fn, **kw):
            return lambda e: fn(e, **kw)

        def load_consts():
            k.dma("sp", params[:], I["params"].rearrange("l p c -> p l c"), writes=(r_par,))
            k.dma("sp", tri[:], I["tri"], writes=(r_const,))
            k.dma("sp", ones[:], I["ones"], writes=(r_const,))
            k.dma("sp", swp[:], I["swap"], writes=(r_const,))
            k.dma("sp", kbias[:], I["kbias"], writes=(r_const,))
            k.dma("sp", flag[:], I["flag"], writes=(r_const,))
            k.op("dve", lambda e: e.memset(epsb[:], EPS), writes=(r_const,))
            k.op("dve", lambda e: e.memset(zb[:], 0.0), writes=(r_const,))

        def pcol(li, c):
            return params[:, li, c:c + 1]

        def rstd_from_ps(ps, r_ps, n, dim, slot):
            lv, r_lv = lnv[slot % 2], r_lnv[slot % 2]
            rs, r_rs = rstd[slot], r_rstd[slot]
            k.op("act", lambda e: e.activation(out=lv[:, :n], in_=ps[:, :n], func=AF.Ln, bias=epsb[:], scale=1.0 / dim),
                 reads=(r_ps, r_const), writes=(r_lv,))
            k.op("act", lambda e: e.activation(out=rs[:, :n], in_=lv[:, :n], func=AF.Exp, scale=-0.5),
                 reads=(r_lv,), writes=(r_rs,))
            return rs, r_rs

        def norm_fm(src, r_src, nch, n, sq, r_sq, li, gcol, dst, r_dst, dim, slot):
            split = (nch == NCD)

            def sqc(c):
                if split:
                    return (sq16b[:, c // 2, :n], r_sq16b) if c % 2 else (sq16[:, c // 2, :n], r_sq16)
                return sq[:, c, :n], r_sq
            for c in range(nch):
                o_, r_o = sqc(c)
                if split and c % 2 == 1:
                    k.op("pool", lambda e, c=c, o_=o_: e.tensor_tensor(out=o_, in0=src[:, c, :n], in1=src[:, c, :n], op=ALU.mult),
                         reads=(r_src,), writes=(r_o,))
                else:
                    k.op("act", lambda e, c=c, o_=o_: e.activation(out=o_, in_=src[:, c, :n], func=AF.Square),
                         reads=(r_src,), writes=(r_o,))
            ps, r_ps = ps_next()
            rd = (r_sq16, r_sq16b, r_const) if split else (r_sq, r_const)
            k.mm([(ps[:, :n], ones[:, :], sqc(c)[0], c == 0, c == nch - 1) for c in range(nch)],
                 reads=rd, writes=(r_ps,))
            rs, r_rs = rstd_from_ps(ps, r_ps, n, dim, slot)
            for c in range(nch):
                k.op("dve", lambda e, c=c: e.scalar_tensor_tensor(out=dst[:, c, :n], in0=src[:, c, :n], scalar=pcol(li, gcol + c),
                                                                  in1=rs[:, :n], op0=ALU.mult, op1=ALU.mult),
                     reads=(r_src, r_rs, r_par), writes=(r_dst,))

        def w_in_loader(li, c0, ncols):
            def f(s):
                return [(slot_view(s, NCD, ncols), I["w_in"][li][:, c0:c0 + ncols].rearrange("(k p) n -> p k n", p=128))]
            f.key = ("w_in", li, c0, ncols)
            f.nel = NCD * ncols
            return f

        def gen_loader(ap2d, kc, ncols, key):
            def f(s):
                return [(slot_view(s, kc, ncols), ap2d.rearrange("(k p) n -> p k n", p=128))]
            f.key = key
            f.nel = kc * ncols
            return f

        def rope_apply(xps, r_xps, n, li, gcolr, x_f, r_x_f, xb, r_xb, dst_fn):
            k.op("act", lambda e: e.mul(out=xb[0:64, :n], in_=xps[0:64, :n], mul=pcol(li, gcolr)[0:64, :]),
                 reads=(r_xps, r_par), writes=(r_xb,))
            k.op("dve", lambda e: e.scalar_tensor_tensor(out=x_f[0:64, 0, :n], in0=xps[0:64, :n], scalar=pcol(li, gcolr)[0:64, :],
                                                         in1=ropeC[0:64, :n], op0=ALU.mult, op1=ALU.mult),
                 reads=(r_xps, r_par, r_rope), writes=(r_x_f,))
            ps2, r_ps2 = ps_next()
            k.mm([(ps2[0:64, :n], swp[:, :], xb[0:64, :n], True, True)], reads=(r_xb, r_const), writes=(r_ps2,))
            k.op("dve", lambda e: e.tensor_tensor(out=x_f[0:64, 1, :n], in0=ps2[0:64, :n], in1=ropeS[0:64, :n], op=ALU.mult),
                 reads=(r_ps2, r_rope, r_x_f), writes=(r_x_f,))
            k.op("pool", lambda e: e.tensor_tensor(out=x_f[0:64, 0, :n], in0=x_f[0:64, 0, :n], in1=x_f[0:64, 1, :n], op=ALU.add),
                 reads=(r_x_f,), writes=(r_x_f,))

        def load_rope(key0, n):
            k.dma("sp", ropeC[:, :n], I["ropeC"][:, key0:key0 + n], writes=(r_rope,))
            k.dma("sp", ropeS[:, :n], I["ropeS"][:, key0:key0 + n], writes=(r_rope,))

        def kv_path(li, n, key0, r_kvdst):
            for g in range(2):
                s, r_s = ws.get(w_in_loader(li, C_KVL + 256 * g, 256))
                sv = slot_view(s, NCD, 256)
                for m in range(2):
                    ps, r_ps = ps_next()
                    k.mm([(ps[:, :n], sv[:, kc, m * 128:(m + 1) * 128], hT[:, kc, :n], kc == 0, kc == NCD - 1) for kc in range(NCD)],
                         reads=(r_s, r_hT), writes=(r_ps,))
                    c = 2 * g + m
                    k.op("dve", lambda e, c=c, ps=ps: e.tensor_copy(out=kvl[:, c, :n], in_=ps[:, :n]), reads=(r_ps,), writes=(r_kvl,))
            s, r_s = ws.get(w_in_loader(li, C_KR, 64))
            sv = slot_view(s, NCD, 64)
            psr, r_psr = ps_next()
            k.mm([(psr[0:64, :n], sv[:, kc, :], hT[:, kc, :n], kc == 0, kc == NCD - 1) for kc in range(NCD)],
                 reads=(r_s, r_hT), writes=(r_psr,))
            k.op("act", lambda e: e.activation(out=sq5[0:64, 4, :n], in_=psr[0:64, :n], func=AF.Square), reads=(r_psr,), writes=(r_sq5,))
            rope_apply(psr, r_psr, n, li, P_KR, kx, r_kx, kxb, r_kxb, None)
            k.op("pool", lambda e: e.tensor_copy(out=krg[0:64, :n], in_=kx[0:64, 0, :n]), reads=(r_kx,), writes=(r_krg,))
            norm_fm(kvl, r_kvl, 4, n, sq5, r_sq5, li, P_KVLN, kvnT, r_kvnT, 512.0, 2)
            ntb = (n + 127) // 128
            for hg in range(4):
                s, r_s = ws.get(gen_loader(I["w_ukv"][li][:, 1024 * hg:1024 * (hg + 1)], 4, 1024, ("w_ukv", li, hg)))
                sv = slot_view(s, 4, 1024)
                for hh in range(4):
                    h = 4 * hg + hh
                    ps, r_ps = ps_next()
                    k.mm([(ps[:, :n], sv[:, kc, 256 * hh:256 * hh + 128], kvnT[:, kc, :n], kc == 0, kc == 3) for kc in range(4)],
                         reads=(r_s, r_kvnT), writes=(r_ps,))
                    sq, r_sq = sqh[:, h % 2, 0, :], r_sqh
                    k.op("act", lambda e, ps=ps, sq=sq: e.activation(out=sq[:, :n], in_=ps[:, :n], func=AF.Square),
                         reads=(r_ps,), writes=(r_sq,))
                    pss, r_pss = ps_next()
                    k.mm([(pss[:, :n], ones[:, :], sq[:, :n], True, False),
                          (pss[:, :n], ones[0:64, :], sq5[0:64, 4, :n], False, True)],
                         reads=(r_sq, r_sq5, r_const), writes=(r_pss,))
                    rs, r_rs = rstd_from_ps(pss, r_pss, n, 192.0, h % 2)
                    k.op("dve", lambda e, ps=ps, rs=rs, h=h: e.scalar_tensor_tensor(out=KTn_all[:, h, :n], in0=ps[:, :n], scalar=pcol(li, P_KN),
                                                                                   in1=rs[:, :n], op0=ALU.mult, op1=ALU.mult),
                         reads=(r_ps, r_rs, r_par), writes=(r_KTn,))
                    k.op("pool", lambda e, rs=rs, h=h: e.tensor_tensor(out=KTr_all[0:64, h, :n], in0=krg[0:64, :n], in1=rs[0:64, :n], op=ALU.mult),
                         reads=(r_krg, r_rs), writes=(r_KTr,))
                for tb in range(ntb):
                    m = min(128, n - 128 * tb)
                    ps, r_ps = ps_next()
                    mms = []
                    for hh in range(4):
                        for kc in range(4):
                            mms.append((ps[0:m, 128 * hh:128 * (hh + 1)], kvnT[:, kc, 128 * tb:128 * tb + m],
                                        sv[:, kc, 256 * hh + 128:256 * hh + 256], kc == 0, kc == 3))
                    k.mm(mms, reads=(r_s, r_kvnT), writes=(r_ps,))
                    k.op("act", lambda e, ps=ps, m=m, tb=tb, hg=hg: e.activation(out=Vsb[0:m, tb, 512 * hg:512 * (hg + 1)], in_=ps[0:m, 0:512], func=AF.Copy),
                         reads=(r_ps,), writes=(r_Vsb,))
            k.dma("sp", KTn_d[li][:, :, key0:key0 + n].rearrange("h p t -> p h t"), KTn_all[:, :, :n], reads=(r_KTn,), writes=(r_kvdst,))
            k.dma("sp", KTr_d[li][:, :, key0:key0 + n].rearrange("h p t -> p h t"), KTr_all[0:64, :, :n], reads=(r_KTr,), writes=(r_kvdst,))
            cuts = sorted(set([0, n] + [128 * i for i in range(1, ntb)] + [g * 128 - key0 for g in range(key0 // 128 + 1, (key0 + n) // 128 + 1) if 0 < g * 128 - key0 < n]))
            for a_, b_ in zip(cuts[:-1], cuts[1:]):
                tb, lp0 = divmod(a_, 128)
                gb, gp0 = divmod(key0 + a_, 128)
                np_ = b_ - a_
                k.dma("sp", V_d[li][:, gp0:gp0 + np_, gb, :].rearrange("h p d -> p h d"),
                      Vsb[lp0:lp0 + np_, tb, :].rearrange("p (h d) -> p h d", h=H), reads=(r_Vsb,), writes=(r_kvdst,))

        def mixer_inputs(li, n, c0, with_b):
            nn = n - c0
            for pr in range(4):
                su, r_su = ws.get(w_in_loader(li, C_U + 256 * pr, 256))
                sc, r_sc = ws.get(w_in_loader(li, C_C + 256 * pr, 256), keep=1)
                for m in range(2):
                    ch = 2 * pr + m
                    ps, r_ps = ps_next()
                    svu = slot_view(su, NCD, 256)
                    k.mm([(ps[:, :nn], svu[:, kc, m * 128:(m + 1) * 128], hT[:, kc, c0:n], kc == 0, kc == NCD - 1) for kc in range(NCD)],
                         reads=(r_su, r_hT), writes=(r_ps,))
                    ub = usb[:, ch % 2, :]
                    k.op("act", lambda e, ps=ps, ub=ub: e.activation(out=ub[:, :nn], in_=ps[:, :nn], func=AF.Copy), reads=(r_ps,), writes=(r_usb,))
                    ps2, r_ps2 = ps_next()
                    svc = slot_view(sc, NCD, 256)
                    k.mm([(ps2[:, :nn], svc[:, kc, m * 128:(m + 1) * 128], hT[:, kc, c0:n], kc == 0, kc == NCD - 1) for kc in range(NCD)],
                         reads=(r_sc, r_hT), writes=(r_ps2,))
                    k.op("dve", lambda e, ps2=ps2, ub=ub, ch=ch: e.tensor_tensor(out=cu[:, ch, 2 + c0:2 + n], in0=ps2[:, :nn], in1=ub[:, :nn], op=ALU.mult),
                         reads=(r_ps2, r_usb), writes=(r_cu,))
                if with_b:
                    sbb, r_sbb = ws.get(w_in_loader(li, C_B + 256 * pr, 256))
                    svb = slot_view(sbb, NCD, 256)
                    for m in range(2):
                        ch = 2 * pr + m
                        ps3, r_ps3 = ps_next()
                        k.mm([(ps3[:, :nn], svb[:, kc, m * 128:(m + 1) * 128], hT[:, kc, c0:n], kc == 0, kc == NCD - 1) for kc in range(NCD)],
                             reads=(r_sbb, r_hT), writes=(r_ps3,))
                        k.op("act", lambda e, ps3=ps3, ch=ch: e.activation(out=bsb[:, ch, c0:n], in_=ps3[:, :nn], func=AF.Copy),
                             reads=(r_ps3,), writes=(r_bsb,))
            for pr in range(4):
                sp_, r_sp = ws.get(w_in_loader(li, C_POOL + 256 * pr, 256))
                svp = slot_view(sp_, NCD, 256)
                for m in range(2):
                    ch = 2 * pr + m
                    ps, r_ps = ps_next()
                    k.mm([(ps[:, :nn], svp[:, kc, m * 128:(m + 1) * 128], hT[:, kc, c0:n], kc == 0, kc == NCD - 1) for kc in range(NCD)],
                         reads=(r_sp, r_hT), writes=(r_ps,))
                    k.op("act", lambda e, ps=ps, ch=ch: e.activation(out=pin[:, ch, 16 + c0:16 + n], in_=ps[:, :nn], func=AF.Copy),
                         reads=(r_ps,), writes=(r_pin,))

        def save_halo(n):
            k.op("pool", lambda e: e.tensor_copy(out=halo_cu[:, :, :], in_=cu[:, :, n:n + 2]), reads=(r_cu,), writes=(r_hcu,))
            k.op("pool", lambda e: e.tensor_copy(out=halo_pin[:, :, 1:16], in_=pin[:, :, n + 1:n + 16]), reads=(r_pin,), writes=(r_hpin,))

        def restore_halo(scale_flag):
            if scale_flag:
                k.op("pool", lambda e: e.tensor_scalar(out=cu[:, :, 0:2], in0=halo_cu[:, :, :], scalar1=flag[:, 0:1], scalar2=None, op0=ALU.mult),
                     reads=(r_hcu, r_const), writes=(r_cu,))
                k.op("pool", lambda e: e.tensor_scalar(out=pin[:, :, 1:16], in0=halo_pin[:, :, 1:16], scalar1=flag[:, 0:1], scalar2=None, op0=ALU.mult),
                     reads=(r_hpin, r_const), writes=(r_pin,))
            else:
                k.op("pool", lambda e: e.tensor_copy(out=cu[:, :, 0:2], in_=halo_cu[:, :, :]), reads=(r_hcu,), writes=(r_cu,))
                k.op("pool", lambda e: e.tensor_copy(out=pin[:, :, 1:16], in_=halo_pin[:, :, 1:16]), reads=(r_hpin,), writes=(r_pin,))

        def q_path(li, n):
            for g in range(2):
                s, r_s = ws.get(w_in_loader(li, C_QL + 256 * g, 256))
                sv = slot_view(s, NCD, 256)
                for m in range(2):
                    ps, r_ps = ps_next()
                    k.mm([(ps[:, :n], sv[:, kc, m * 128:(m + 1) * 128], hT[:, kc, :n], kc == 0, kc == NCD - 1) for kc in range(NCD)],
                         reads=(r_s, r_hT), writes=(r_ps,))
                    c = 2 * g + m
                    k.op("dve", lambda e, c=c, ps=ps: e.tensor_copy(out=qlat[:, c, :n], in_=ps[:, :n]), reads=(r_ps,), writes=(r_qlat,))
            norm_fm(qlat, r_qlat, 4, n, sq4, r_sq4, li, P_QLN, qnT, r_qnT, 512.0, 2)
            for hg in range(4):
                s, r_s = ws.get(gen_loader(I["w_uq"][li][:, 768 * hg:768 * (hg + 1)], 4, 768, ("w_uq", li, hg)))
                sv = slot_view(s, 4, 768)
                for hh in range(4):
                    h = 4 * hg + hh
                    psn, r_psn = ps_next()
                    k.mm([(psn[:, :n], sv[:, kc, 192 * hh:192 * hh + 128], qnT[:, kc, :n], kc == 0, kc == 3) for kc in range(4)],
                         reads=(r_s, r_qnT), writes=(r_psn,))
                    psr, r_psr = ps_next()
                    k.mm([(psr[0:64, :n], sv[:, kc, 192 * hh + 128:192 * hh + 192], qnT[:, kc, :n], kc == 0, kc == 3) for kc in range(4)],
                         reads=(r_s, r_qnT), writes=(r_psr,))
                    sqn = sqq[:, h % 2, 0, :]
                    sqr = sqq[:, h % 2, 1, :]
                    k.op("act", lambda e, psn=psn, sqn=sqn: e.activation(out=sqn[:, :n], in_=psn[:, :n], func=AF.Square), reads=(r_psn,), writes=(r_sqq,))
                    k.op("act", lambda e, psr=psr, sqr=sqr: e.activation(out=sqr[0:64, :n], in_=psr[0:64, :n], func=AF.Square), reads=(r_psr,), writes=(r_sqq,))
                    pss, r_pss = ps_next()
                    k.mm([(pss[:, :n], ones[:, :], sqn[:, :n], True, False), (pss[:, :n], ones[0:64, :], sqr[0:64, :n], False, True)],
                         reads=(r_sqq, r_const), writes=(r_pss,))
                    rs, r_rs = rstd_from_ps(pss, r_pss, n, 192.0, h % 2)
                    k.op("dve", lambda e, psn=psn, rs=rs, h=h: e.scalar_tensor_tensor(out=QTn[:, h, :n], in0=psn[:, :n], scalar=pcol(li, P_QN),
                                                                                     in1=rs[:, :n], op0=ALU.mult, op1=ALU.mult),
                         reads=(r_psn, r_rs, r_par), writes=(r_QTn,))
                    rope_apply(psr, r_psr, n, li, P_QR, qx, r_qx, qxb, r_qxb, None)
                    k.op("pool", lambda e, rs=rs, h=h: e.tensor_tensor(out=QTr[0:64, h, :n], in0=qx[0:64, 0, :n], in1=rs[0:64, :n], op=ALU.mult),
                         reads=(r_qx, r_rs), writes=(r_QTr,))

        def attention(li, t, n):
            qs = t * TT
            qe = qs + n
            nkb = (qe + 127) // 128
            segs = [(0, min(nkb, SEG0_KB))]
            if nkb > SEG0_KB:
                segs.append((SEG0_KB, nkb))
            kv_reads = tuple(r_kv_d[li][0:t + 1])
            ring["set"] = [0, 1, 2, 3]
            loads = [(h, si) for h in range(H) for si in range(len(segs))]

            def issue_load(idx):
                h, si = loads[idx]
                b0, b1 = segs[si]
                k0 = 128 * b0
                k1 = min(128 * b1, qe)
                nk = k1 - k0
                (kn, r_kn), (kr, r_kr), (vv, r_vv) = Kn_s[idx % 2], Kr_s[idx % 2], V_s[idx % 2]
                k.dma("sp", kn[:, 0:nk], KTn_d[li][h, :, k0:k1], reads=kv_reads, writes=(r_kn,))
                k.dma("sp", kr[0:64, 0:nk], KTr_d[li][h, :, k0:k1], reads=kv_reads, writes=(r_kr,))
                nb_ = (nk + 127) // 128
                k.dma("sp", vv[:, 0:nb_, :], V_d[li][h, :, b0:b0 + nb_, :], reads=kv_reads, writes=(r_vv,))

            items = []
            for idx, (h, si) in enumerate(loads):
                b0, b1 = segs[si]
                for j in range(b0, b1):
                    items.append((idx, h, si, j))
            last_item_of_load = {}
            for n_, it in enumerate(items):
                last_item_of_load[it[0]] = n_
            G = 3
            issue_load(0)
            if len(loads) > 1:
                issue_load(1)
            stash = {}
            O, r_O = banks[6], r_bank[6]
            Dn, r_Dn = banks[7], r_bank[7]
            ring["set"] = [0, 1, 2, 3, 4, 5]

            def item_geom(n_):
                idx, h, si, j = items[n_]
                kp = min(128, qe - 128 * j, NKEY - 128 * j)
                cst = max(0, 128 * j - qs)
                diag = (128 * j + 128 > qs)
                bcls = 0 if t < NT else (0 if j < 16 else (1 if j == 16 else 2))
                return kp, cst, diag, bcls

            def stage_s_mm(n_, st, r_st):
                idx, h, si, j = items[n_]
                b0, b1 = segs[si]
                (kn, r_kn), (kr, r_kr), (vv, r_vv) = Kn_s[idx % 2], Kr_s[idx % 2], V_s[idx % 2]
                kp, cst, diag, bcls = item_geom(n_)
                kl = 128 * (j - b0)
                k.mm([(st[0:kp, cst:n], kn[:, kl:kl + kp], QTn[:, h, cst:n], True, False),
                      (st[0:kp, cst:n], kr[0:64, kl:kl + kp], QTr[0:64, h, cst:n], False, True)],
                     reads=(r_kn, r_kr, r_QTn, r_QTr), writes=(r_st,))

            def stage_s_exp(n_, st, r_st, pslot):
                idx, h, si, j = items[n_]
                kp, cst, diag, bcls = item_geom(n_)
                pt_, r_pt = PT[pslot], r_PT[pslot]
                k.op("act", lambda e: e.activation(out=pt_[0:kp, cst:n], in_=st[0:kp, cst:n], func=AF.Exp,
                                                   bias=(kbias[0:kp, j:j + 1] if t >= NT else zb[0:kp, 0:1]), scale=SCALE),
                     reads=(r_st, r_const), writes=(r_pt,))
                dend = min(n, 128 * j + 128 - qs)
                if diag:
                    c0 = qs + cst - 128 * j
                    k.op("pool", lambda e: e.tensor_tensor(out=pt_[0:kp, cst:dend], in0=pt_[0:kp, cst:dend], in1=tri[0:kp, c0:c0 + dend - cst], op=ALU.mult),
                         reads=(r_pt, r_const), writes=(r_pt,))
                stash[n_] = (kp, cst, pt_, r_pt)

            def stage_pv(n_):
                idx, h, si, j = items[n_]
                b0, b1 = segs[si]
                (kn, r_kn), (kr, r_kr), (vv, r_vv) = Kn_s[idx % 2], Kr_s[idx % 2], V_s[idx % 2]
                kp, cst, pt_, r_pt = stash.pop(n_)
                first = (j == 0)
                last = (j == nkb - 1)
                k.mm([(O[:, cst:n], vv[0:kp, j - b0, :], pt_[0:kp, cst:n], first, last),
                      (Dn[:, cst:n], ones[0:kp, :], pt_[0:kp, cst:n], first, last)], reads=(r_vv, r_pt, r_const), writes=(r_O, r_Dn))
                if last:
                    k.op("dve", lambda e: e.reciprocal(out=rden[:, :n], in_=Dn[:, :n]), reads=(r_Dn,), writes=(r_rden,))
                    k.op("dve", lambda e: e.tensor_tensor(out=ybT[:, h, :n], in0=O[:, :n], in1=rden[:, :n], op=ALU.mult),
                         reads=(r_O, r_rden), writes=(r_ybT,))
                if last_item_of_load[idx] == n_ and idx + 2 < len(loads):
                    issue_load(idx + 2)

            groups = []
            cur = []
            for n_ in range(len(items)):
                kp, cst, diag, bcls = item_geom(n_)
                full = (kp == 128 and cst == 0 and not diag)
                if cur:
                    kp0, cst0, diag0, bcls0 = item_geom(cur[0])
                    full0 = (kp0 == 128 and cst0 == 0 and not diag0)
                    if len(cur) == G or not (full and full0 and bcls == bcls0):
                        groups.append(cur)
                        cur = []
                cur.append(n_)
            if cur:
                groups.append(cur)
            for g_ in range(len(groups) + 1):
                if g_ < len(groups):
                    grp = groups[g_]
                    base = 3 * (g_ % 2)
                    pbase = 3 * (g_ % 3)
                    bks = [(banks[base + i], r_bank[base + i]) for i in range(len(grp))]
                    k.prewait("pe", reads=(), writes=tuple(rb for (_, rb) in bks))
                    for n_, (st, r_st) in zip(grp, bks):
                        stage_s_mm(n_, st, r_st)
                    kp, cst, diag, bcls = item_geom(grp[0])
                    if len(grp) > 1:
                        ng = len(grp)
                        j0 = items[grp[0]][3]
                        src = psall[:, 512 * base:512 * (base + ng)].rearrange("p (g c) -> p g c", g=ng)[:, :, 0:n]
                        dstp = PTall[:, pbase:pbase + ng, 0:n]
                        k.op("act", lambda e: e.activation(out=dstp, in_=src, func=AF.Exp,
                                                           bias=(kbias[:, j0:j0 + 1] if t >= NT else zb[:, 0:1]), scale=SCALE),
                             reads=tuple(rb for (_, rb) in bks) + (r_const,), writes=tuple(r_PT[pbase + i] for i in range(ng)))
                        for i, n_ in enumerate(grp):
                            stash[n_] = (128, 0, PT[pbase + i], r_PT[pbase + i])
                    else:
                        stage_s_exp(grp[0], bks[0][0], bks[0][1], pbase)
                if g_ >= 1:
                    grp = groups[g_ - 1]
                    k.prewait("pe", reads=tuple(stash[n_][3] for n_ in grp), writes=())
                    for n_ in grp:
                        stage_pv(n_)
            ring["set"] = list(range(8))

        def conv_pool(li, t, n):
            k.dma("sp", invc[:, :, :n], I["invc"][:, :, t * TT:t * TT + n], writes=(r_invc,))
            for ch in range(8):
                tmp = ctmp[:, ch % 2, :]
                k.op("pool", lambda e, ch=ch, tmp=tmp: e.tensor_scalar(out=tmp[:, :n], in0=cu[:, ch, 0:n], scalar1=pcol(li, P_CW + 3 * ch), scalar2=None, op0=ALU.mult),
                     reads=(r_cu, r_par), writes=(r_ctmp,))
                k.op("dve", lambda e, ch=ch, tmp=tmp: e.scalar_tensor_tensor(out=tmp[:, :n], in0=cu[:, ch, 1:n + 1], scalar=pcol(li, P_CW + 3 * ch + 1),
                                                                             in1=tmp[:, :n], op0=ALU.mult, op1=ALU.add),
                     reads=(r_cu, r_par, r_ctmp), writes=(r_ctmp,))
                k.op("dve", lambda e, ch=ch, tmp=tmp: e.scalar_tensor_tensor(out=tmp[:, :n], in0=cu[:, ch, 2:n + 2], scalar=pcol(li, P_CW + 3 * ch + 2),
                                                                             in1=tmp[:, :n], op0=ALU.mult, op1=ALU.add),
                     reads=(r_cu, r_par, r_ctmp), writes=(r_ctmp,))
                k.op("dve", lambda e, ch=ch, tmp=tmp: e.tensor_tensor(out=yaT[:, ch, :n], in0=tmp[:, :n], in1=bsb[:, ch, :n], op=ALU.mult),
                     reads=(r_ctmp, r_bsb), writes=(r_yaT,))
            for g in range(4):
                src = pin[:, 2 * g:2 * g + 2, :]
                cur = src
                r_cur = r_pin
                lo = 1
                bufs = [(pw1, r_pw1), (pw2, r_pw2)]
                for step in range(g + 1):
                    sh = 1 << step
                    dstb, r_dst = bufs[step % 2]
                    lo2 = lo + sh
                    k.op("pool", lambda e, cur=cur, dstb=dstb, lo2=lo2, sh=sh: e.tensor_tensor(out=dstb[:, :, lo2:n + 16], in0=cur[:, :, lo2:n + 16],
                                                                                             in1=cur[:, :, lo2 - sh:n + 16 - sh], op=ALU.add),
                         reads=(r_cur,), writes=(r_dst,))
                    cur, r_cur, lo = dstb, r_dst, lo2
                for m in range(2):
                    ch = 2 * g + m
                    tmp = ctmp[:, m, :]
                    k.op("dve", lambda e, cur=cur, m=m, g=g, tmp=tmp: e.tensor_tensor(out=tmp[:, :n], in0=cur[:, m, 16:16 + n], in1=invc[:, g, :n], op=ALU.mult),
                         reads=(r_cur, r_invc), writes=(r_ctmp,))
                    k.op("dve", lambda e, ch=ch, tmp=tmp: e.tensor_tensor(out=pooled[:, ch, :n], in0=tmp[:, :n], in1=pin[:, ch, 16:16 + n], op=ALU.subtract),
                         reads=(r_ctmp, r_pin), writes=(r_pooled,))
            def pw_loader(s):
                return [(s[:, 0:2048].rearrange("p (g k n) -> p g k n", g=4, k=2), I["pool_w"][li].rearrange("g (k p) n -> p g k n", p=128))]
            pw_loader.key = ("pool_w", li)
            pw_loader.nel = 2048
            s, r_s = ws.get(pw_loader)
            sv = s[:, 0:2048].rearrange("p (g k n) -> p g k n", g=4, k=2)
            for g in range(4):
                for m in range(2):
                    ps, r_ps = ps_next()
                    k.mm([(ps[:, :n], sv[:, g, kc, m * 128:(m + 1) * 128], pooled[:, 2 * g + kc, :n], kc == 0, kc == 1) for kc in range(2)],
                         reads=(r_s, r_pooled), writes=(r_ps,))
                    ch = 2 * g + m
                    k.op("act", lambda e, ps=ps, ch=ch: e.mul(out=ycT[:, ch, :n], in_=ps[:, :n], mul=pcol(li, P_PS + ch)),
                         reads=(r_ps, r_par), writes=(r_ycT,))

        def merge_and_wo(li, n):
            for jj in range(8):
                for i in range(3):
                    sg, r_sg = ws.get(w_in_loader(li, C_GATE + 2048 * i + 256 * jj, 256))
                    svg = slot_view(sg, NCD, 256)
                    wsrc, nk, src, r_src = [(I["w_a"], 8, yaT, r_yaT), (I["w_b"], 16, ybT, r_ybT), (I["w_c"], 8, ycT, r_ycT)][i]
                    sw_, r_sw = ws.get(gen_loader(wsrc[li][:, 256 * jj:256 * (jj + 1)], nk, 256, ("w_br", li, i, jj)), keep=1)
                    svb = slot_view(sw_, nk, 256)
                    for m in range(2):
                        ps, r_ps = ps_next()
                        k.mm([(ps[:, :n], svg[:, kc, m * 128:(m + 1) * 128], hT[:, kc, :n], kc == 0, kc == NCD - 1) for kc in range(NCD)],
                             reads=(r_sg, r_hT), writes=(r_ps,))
                        k.op("act", lambda e, ps=ps, i=i: e.activation(out=gate[i][:, :n], in_=ps[:, :n], func=AF.Sigmoid), reads=(r_ps,), writes=(r_gate[i],))
                        ps2, r_ps2 = ps_next()
                        k.mm([(ps2[:, :n], svb[:, kc, m * 128:(m + 1) * 128], src[:, kc, :n], kc == 0, kc == nk - 1) for kc in range(nk)],
                             reads=(r_sw, r_src), writes=(r_ps2,))
                        mt, r_mt = mtmp[2 * i + m], r_mtmp[2 * i + m]
                        k.op("dve", lambda e, ps2=ps2, i=i, mt=mt: e.tensor_tensor(out=mt[:, :n], in0=ps2[:, :n], in1=gate[i][:, :n], op=ALU.mult),
                             reads=(r_ps2, r_gate[i]), writes=(r_mt,))
                for m in range(2):
                    oc = 2 * jj + m
                    k.op("pool", lambda e, m=m: e.tensor_tensor(out=mtmp[m][:, :n], in0=mtmp[m][:, :n], in1=mtmp[2 + m][:, :n], op=ALU.add),
                         reads=(r_mtmp[m], r_mtmp[2 + m]), writes=(r_mtmp[m],))
                    k.op("pool", lambda e, oc=oc, m=m: e.tensor_tensor(out=merged[:, oc, :n], in0=mtmp[m][:, :n], in1=mtmp[4 + m][:, :n], op=ALU.add),
                         reads=(r_mtmp[m], r_mtmp[4 + m]), writes=(r_merged,))
            for jj in range(8):
                s, r_s = ws.get(gen_loader(I["w_o"][li][:, 256 * jj:256 * (jj + 1)], 16, 256, ("w_o", li, jj)))
                sv = slot_view(s, 16, 256)
                for m in range(2):
                    oc = 2 * jj + m
                    ps, r_ps = ps_next()
                    k.mm([(ps[:, :n], sv[:, kc, m * 128:(m + 1) * 128], merged[:, kc, :n], kc == 0, kc == NCD - 1) for kc in range(NCD)],
                         reads=(r_s, r_merged), writes=(r_ps,))
                    k.op("dve", lambda e, ps=ps, oc=oc: e.tensor_tensor(out=xT[:, oc, :n], in0=ps[:, :n], in1=xT[:, oc, :n], op=ALU.add),
                         reads=(r_ps, r_xT), writes=(r_xT,))

        def mlp(li, n):
            norm_fm(xT, r_xT, NCD, n, sq16, r_sq16, li, P_MN, hT, r_hT, float(D), 2)
            for jj in range(32):
                s, r_s = ws.get(gen_loader(I["w_up"][li][:, 256 * jj:256 * (jj + 1)], 16, 256, ("w_up", li, jj)))
                sv = slot_view(s, 16, 256)
                for m in range(2):
                    oc = 2 * jj + m
                    ps, r_ps = ps_next()
                    k.mm([(ps[:, :n], sv[:, kc, m * 128:(m + 1) * 128], hT[:, kc, :n], kc == 0, kc == NCD - 1) for kc in range(NCD)],
                         reads=(r_s, r_hT), writes=(r_ps,))
                    rr, r_rr = rl[oc % 2], r_rl[oc % 2]
                    k.op("act", lambda e, ps=ps, rr=rr: e.activation(out=rr[:, :n], in_=ps[:, :n], func=AF.Relu), reads=(r_ps,), writes=(r_rr,))
                    k.op("pool", lambda e, rr=rr, oc=oc: e.tensor_tensor(out=act[:, oc, :n], in0=rr[:, :n], in1=rr[:, :n], op=ALU.mult),
                         reads=(r_rr,), writes=(r_act,))
            for oc in range(NCD):
                ps, r_ps = ps_next()
                for half in range(2):
                    s, r_s = ws.get(gen_loader(I["w_down"][li][4096 * half:4096 * (half + 1), 128 * oc:128 * (oc + 1)], 32, 128, ("w_down", li, oc, half)))
                    sv = slot_view(s, 32, 128)
                    k.mm([(ps[:, :n], sv[:, kc, :], act[:, 32 * half + kc, :n], (half == 0 and kc == 0), (half == 1 and kc == 31)) for kc in range(32)],
                         reads=(r_s, r_act), writes=(r_ps,))
                k.op("dve", lambda e, ps=ps, oc=oc: e.tensor_tensor(out=xT[:, oc, :n], in0=ps[:, :n], in1=xT[:, oc, :n], op=ALU.add),
                     reads=(r_ps, r_xT), writes=(r_xT,))

        def layer(li, src, r_src_list, kv_only_tiles, dst_fn):
            k.op("pool", lambda e: e.memset(halo_cu[:], 0.0), writes=(r_hcu,))
            k.op("pool", lambda e: e.memset(halo_pin[:], 0.0), writes=(r_hpin,))
            for t in range(2 * NT):
                n = TT
                k.dma("sp", xT[:, :, :n], src.rearrange("(c p) t -> p c t", p=128)[:, :, t * TT:t * TT + n],
                      reads=tuple(r_src_list[t:t + 1]), writes=(r_xT,))
                load_rope(t * TT, n)
                norm_fm(xT, r_xT, NCD, n, sq16, r_sq16, li, P_AN, hT, r_hT, float(D), 2)
                kv_path(li, n, t * TT, r_kv_d[li][t])
                if t < kv_only_tiles:
                    if t == kv_only_tiles - 1:
                        mixer_inputs(li, n, n - 16, False)
                        save_halo(n)
                    continue
                q_path(li, n)
                attention(li, t, n)
                restore_halo(t == NT)
                mixer_inputs(li, n, 0, True)
                save_halo(n)
                conv_pool(li, t, n)
                merge_and_wo(li, n)
                mlp(li, n)
                d_ap, r_d = dst_fn(t)
                k.dma("sp", d_ap, xT[:, :, :n], reads=(r_xT,), writes=(r_d,))

        def program():
            ring["p"] = 0
            ring["set"] = list(range(8))
            load_consts()
            for li in range(NL):
                last = (li == NL - 1)
                if li == 0:
                    src, r_src = I["xall"], []
                    kvo = KV_ONLY0
                else:
                    src, r_src = X1_d[li - 1], r_x1[li - 1]
                    kvo = NT
                if last:
                    dst_fn = lambda t: (OUT.rearrange("(c p) t -> p c t", p=128)[:, :, (t - NT) * TT:(t - NT + 1) * TT], r_out[t - NT])
                else:
                    dst_fn = lambda t, li=li: (X1_d[li].rearrange("(c p) t -> p c t", p=128)[:, :, t * TT:(t + 1) * TT], r_x1[li][t])
                layer(li, src, r_src, kvo, dst_fn)
            k.finish(r_out)

        k.dry = True
        program()
        ws.pos = 0
        k.dry = False
        program()
        print("[kernel] sbuf bytes remaining", nc.sbuf_bytes_remaining, flush=True)
        print(f"[kernel] instructions={k.n_ins} waits={k.n_wait} weight_slots={len(ws.reqs)}", flush=True)
    return nc


def _rope_tables():
    pos = np.arange(TALL, dtype=np.float32)
    inv = (np.float32(10000.0) ** (-np.arange(0, 64, 2, dtype=np.float32) / np.float32(64))).astype(np.float32)
    ang = (pos[:, None] * inv[None, :]).astype(np.float32)
    c, s = np.cos(ang).astype(np.float32), np.sin(ang).astype(np.float32)
    C = np.concatenate([c, c], 1).T
    S = np.concatenate([-s, s], 1).T
    return np.ascontiguousarray(C), np.ascontiguousarray(S)


def _params(inp, l):
    P = np.zeros((128, NPARAM), np.float32)
    P[:, P_AN:P_AN + 16] = inp["attn_norm"][l].reshape(16, 128).T
    P[:, P_MN:P_MN + 16] = inp["mlp_norm"][l].reshape(16, 128).T
    P[:, P_QLN:P_QLN + 4] = inp["q_lat_norm"][l].reshape(4, 128).T
    P[:, P_KVLN:P_KVLN + 4] = inp["kv_lat_norm"][l].reshape(4, 128).T
    P[:, P_QN] = inp["q_norm"][l][:128]
    P[:64, P_QR] = inp["q_norm"][l][128:]
    P[:, P_KN] = inp["k_norm"][l][:128]
    P[:64, P_KR] = inp["k_norm"][l][128:]
    cw = inp["conv_w"][l]
    for ch in range(8):
        for j in range(3):
            P[:, P_CW + 3 * ch + j] = cw[j, ch * 128:(ch + 1) * 128]
    P[:, P_PS:P_PS + 8] = inp["pool_scale"][l].reshape(8, 128).T
    return P


_NC_CACHE = {}


def _get_nc(layers, fused):
    key = (tuple(layers), fused)
    if key not in _NC_CACHE:
        _NC_CACHE[key] = build_program(list(layers), fused)
    return _NC_CACHE[key]


def _run(layers, inp, x_own, x_pre, fused):
    nc = _get_nc(layers, fused)
    C, S = _rope_tables()
    bf = ml_dtypes.bfloat16
    tri = (np.arange(128)[None, :] >= np.arange(128)[:, None]).astype(bf)
    ones = np.ones((128, 128), bf)
    swap = np.zeros((64, 64), bf)
    for i in range(32):
        swap[i + 32, i] = 1
        swap[i, i + 32] = 1
    ls = list(layers)
    common = {
        "params": np.stack([_params(inp, l) for l in ls]),
        "tri": tri, "ones": ones, "swap": swap,
    }
    for i, l in enumerate(ls):
        for nm in ["w_in", "w_uq", "w_ukv", "pool_w", "w_branch_a", "w_branch_b", "w_branch_c", "w_o", "w_up", "w_down"]:
            common[f"{nm}{i}"] = np.ascontiguousarray(inp[nm][l])
    in_maps = []
    for c in range(8):
        r = c % 2
        pos0 = r * NTOK
        ropeC = np.concatenate([C[:, 0:NPRE], C[:, pos0:pos0 + NTOK]], 1)
        ropeS = np.concatenate([S[:, 0:NPRE], S[:, pos0:pos0 + NTOK]], 1)
        tpos = np.concatenate([np.arange(0, NPRE), np.arange(pos0, pos0 + NTOK)]).astype(np.float32) + 1.0
        invc = np.stack([np.float32(1.0) / np.minimum(tpos, np.float32(w)) for w in (2, 4, 8, 16)]).astype(np.float32)
        invc = np.ascontiguousarray(np.broadcast_to(invc[None], (128, 4, NKEY)))
        kb = np.zeros((128, NKB), np.float32)
        if r == 0:
            keyidx = np.arange(NKB * 128).reshape(NKB, 128).T
            kb[keyidx < NPRE] = -30000.0
        m = dict(common)
        m.update({"xall": np.ascontiguousarray(np.concatenate([x_pre[c], x_own[c]], axis=1)),
                  "ropeC": np.ascontiguousarray(ropeC), "ropeS": np.ascontiguousarray(ropeS), "invc": invc,
                  "kbias": kb, "flag": np.full((128, 1), float(r), np.float32)})
        in_maps.append(m)
    res = run_bass_kernel_spmd(nc, in_maps, core_ids=list(range(8)))
    if DEBUG:
        global DBG_OUT
        DBG_OUT = [{nm: np.asarray(res.results[c][nm]) for nm in DBG_NAMES} for c in range(8)]
    return [np.asarray(res.results[c]["outT"]) for c in range(8)]


FUSED = True


def kernel(**inp):
    inp = {k_: np.asarray(v) for k_, v in inp.items()}
    x = inp["x"].astype(np.float32)
    B = x.shape[0]
    meta = np.broadcast_to(inp["meta_tokens"][None].astype(np.float32), (B, NMETA, D))
    hseq = np.concatenate([meta, x], axis=1)
    own = [np.ascontiguousarray(hseq[c // 2, (c % 2) * NTOK:(c % 2 + 1) * NTOK].T) for c in range(8)]
    zero = np.zeros((D, NPRE), np.float32)
    if FUSED:
        pre = [zero if c % 2 == 0 else own[c - 1] for c in range(8)]
        outs = _run([0, 1], inp, own, pre, True)
    else:
        cur = own
        for l in range(2):
            pre = [zero if c % 2 == 0 else cur[c - 1] for c in range(8)]
            cur = _run([l], inp, cur, pre, False)
        outs = cur
    full = np.stack([np.concatenate([outs[2 * b].T, outs[2 * b + 1].T], axis=0) for b in range(B)])
    return np.ascontiguousarray(full[:, NMETA:, :]).astype(np.float32)
```

```python
import contextlib
import numpy as np
import ml_dtypes
import concourse.bass as bass
import concourse.mybir as mybir
from concourse.bass_utils import run_bass_kernel_spmd

F32 = mybir.dt.float32
BF16 = mybir.dt.bfloat16
AF = mybir.ActivationFunctionType
ALU = mybir.AluOpType

D = 2048
NCD = 16
H = 16
SEQ = 4096
NMETA = 16
TALL = SEQ + NMETA
NTOK = TALL // 2
NPRE = NTOK
TT = 257
NT = NTOK // TT
NKEY = NPRE + NTOK
NKB = (NKEY + 127) // 128
SEG0_KB = 16
SEGK = 2064
D_IN = 11328
C_U, C_B, C_C, C_QL, C_KVL, C_KR, C_POOL, C_GATE = 0, 1024, 2048, 3072, 3584, 4096, 4160, 5184
DFF = 8192
EPS = 1e-6
SCALE = 192.0 ** -0.5
SLOT = 4096
NSLOT = 6
P_AN, P_MN, P_QLN, P_KVLN, P_QN, P_QR, P_KN, P_KR, P_CW, P_PS = 0, 16, 32, 36, 40, 41, 42, 43, 44, 68
NPARAM = 76


class Res:
    __slots__ = ("name", "excl", "w", "rs", "ov")

    def __init__(self, name, excl=False):
        self.name = name
        self.excl = excl
        self.w = None
        self.rs = {}
        self.ov = [self]


class Eng:
    def __init__(self, name, eng, sem):
        self.name, self.eng, self.sem = name, eng, sem
        self.cnt = 0
        self.seen = {}


class KB:
    def __init__(self, nc, es):
        self.nc = nc
        self.dry = False
        mk = lambda n: es.enter_context(nc.semaphore(n))
        self.E = {
            "pe": Eng("pe", nc.tensor, mk("s_pe")),
            "act": Eng("act", nc.scalar, mk("s_act")),
            "dve": Eng("dve", nc.vector, mk("s_dve")),
            "pool": Eng("pool", nc.gpsimd, mk("s_pool")),
            "sp": Eng("sp", nc.sync, mk("s_sp")),
        }
        self.dma_sems = {"sp": [mk(f"d_sp{i}") for i in range(12)], "pool": [mk(f"d_pl{i}") for i in range(8)]}
        self.dma_cnt = {}
        self.dma_rr = {"sp": 0, "pool": 0}
        self.n_wait = 0
        self.n_ins = 0

    def _wait(self, e, tk):
        sem, val = tk
        if e.seen.get(sem.num, 0) >= val:
            return
        e.eng.wait_ge(sem, val)
        e.seen[sem.num] = val
        self.n_wait += 1

    def _deps(self, reads, writes):
        tks = []
        for r in reads:
            for res in r.ov:
                if res.w is not None:
                    tks.append((res.w, False))
                if res.excl:
                    tks.extend((t, True) for t in res.rs.values())
        for w in writes:
            for res in w.ov:
                if res.w is not None:
                    tks.append((res.w, False))
                tks.extend((t, False) for t in res.rs.values())
        return tks

    def _mark(self, tk, reads, writes):
        for r in reads:
            r.rs[tk[0].num] = tk
        for w in writes:
            w.w = tk
            w.rs = {}

    def op(self, en, fn, reads=(), writes=()):
        if self.dry:
            return
        e = self.E[en]
        for tk, rr in self._deps(reads, writes):
            if tk[0] is e.sem and (rr or en == "pe"):
                continue
            self._wait(e, tk)
        ins = fn(e.eng)
        e.cnt += 1
        ins.then_inc(e.sem, 1)
        self.n_ins += 1
        self._mark((e.sem, e.cnt), reads, writes)

    def prewait(self, en, reads=(), writes=()):
        if self.dry:
            return
        e = self.E[en]
        best = {}
        for tk, rr in self._deps(reads, writes):
            if tk[0] is e.sem:
                continue
            if tk[0].num not in best or best[tk[0].num][1] < tk[1]:
                best[tk[0].num] = tk
        for tk in best.values():
            self._wait(e, tk)

    def mm(self, mms, reads, writes):
        if self.dry:
            return
        e = self.E["pe"]
        for tk, rr in self._deps(reads, writes):
            if tk[0] is e.sem:
                continue
            self._wait(e, tk)
        ins = None
        for (o, l, r, st, sp) in mms:
            ins = e.eng.matmul(o, lhsT=l, rhs=r, start=st, stop=sp)
            self.n_ins += 1
        e.cnt += 1
        ins.then_inc(e.sem, 1)
        self._mark((e.sem, e.cnt), reads, writes)

    def dma(self, q, out, in_, reads=(), writes=()):
        if self.dry:
            return
        e = self.E[q]
        for tk, rr in self._deps(reads, writes):
            self._wait(e, tk)
        pool = self.dma_sems[q]
        i = self.dma_rr[q]
        self.dma_rr[q] = (i + 1) % len(pool)
        sem = pool[i]
        prev = self.dma_cnt.get(sem.num, 0)
        if prev:
            self._wait(e, (sem, prev))
        ins = e.eng.dma_start(out=out, in_=in_)
        ins.then_inc(sem, 16)
        self.dma_cnt[sem.num] = prev + 16
        self.n_ins += 1
        self._mark((sem, prev + 16), reads, writes)

    def finish(self, res_list):
        e = self.E["sp"]
        for r in res_list:
            if r.w is not None:
                self._wait(e, r.w)


class Region:
    def __init__(self, tile, nelem):
        self.t = tile
        self.n = nelem
        self.off = 0
        self.phase = None
        self.bufs = []

    def set_phase(self, ph):
        self.phase = ph
        self.off = 0

    def take(self, name, free_shape, dtype):
        n = int(np.prod(free_shape))
        nb = n * (2 if dtype == F32 else 1)
        self.off = (self.off + 15) // 16 * 16
        off = self.off
        self.off += nb
        assert self.off <= self.n, (name, self.off, self.n)
        v = self.t[:, off:off + nb]
        if dtype == F32:
            v = v.bitcast(F32)
        if len(free_shape) == 2:
            v = v.rearrange("p (a b) -> p a b", a=free_shape[0])
        elif len(free_shape) == 3:
            v = v.rearrange("p (a b c) -> p a b c", a=free_shape[0], b=free_shape[1])
        r = Res(name)
        for (ph, r2) in self.bufs:
            if ph != self.phase:
                r.ov.append(r2)
                r2.ov.append(r)
        self.bufs.append((self.phase, r))
        return v, r


class WStream:
    def __init__(self, k, slots, slot_res, cache_fn):
        self.k = k
        self.slots = slots
        self.res = slot_res
        self.reqs = []
        self.issued = 0
        self.pos = 0
        self.cache_fn = cache_fn
        self.cache_idx = {}
        self.cache_res = {}
        self.loaded = set()

    def get(self, loader, keep=0):
        k = self.k
        i = self.pos
        self.pos += 1
        if k.dry:
            self.reqs.append(loader)
            if loader.key not in self.cache_idx:
                self.cache_idx[loader.key] = len(self.cache_idx)
                self.cache_res[loader.key] = Res("wc%d" % len(self.cache_idx))
            return self.slots[i % NSLOT], self.res[i % NSLOT]
        assert keep < NSLOT - 1
        lim = min(len(self.reqs), i - keep + NSLOT)
        while self.issued < lim:
            j = self.issued
            s, r = self.slots[j % NSLOT], self.res[j % NSLOT]
            ld = self.reqs[j]
            ci = self.cache_idx[ld.key]
            cr = self.cache_res[ld.key]
            if ld.key not in self.loaded:
                self.loaded.add(ld.key)
                for (o, src) in ld(s):
                    k.dma("pool", o, src, reads=(), writes=(r,))
                k.dma("sp", self.cache_fn(ci)[:, 0:ld.nel], s[:, 0:ld.nel], reads=(r,), writes=(cr,))
            else:
                k.dma("sp" if (j % 2) else "pool", s[:, 0:ld.nel], self.cache_fn(ci)[:, 0:ld.nel], reads=(cr,), writes=(r,))
            self.issued += 1
        return self.slots[i % NSLOT], self.res[i % NSLOT]


def slot_view(s, kc, ncols):
    return s[:, 0:kc * ncols].rearrange("p (k n) -> p k n", k=kc)


DEBUG = False
DBG_NAMES = []


def build_program(layers, fused):
    KV_ONLY0 = 0 if len(layers) > 1 else NT
    nc = bass.Bass("TRN2", target_bir_lowering=False)
    NL = len(layers)
    dt = lambda name, shape, dtype=F32, kind="ExternalInput": nc.dram_tensor(name, list(shape), dtype, kind=kind).ap()
    I = {}
    I["xall"] = dt("xall", [D, NKEY])
    I["w_in"] = [dt(f"w_in{l}", [D, D_IN]) for l in range(NL)]
    I["w_uq"] = [dt(f"w_uq{l}", [512, 3072]) for l in range(NL)]
    I["w_ukv"] = [dt(f"w_ukv{l}", [512, 4096]) for l in range(NL)]
    I["pool_w"] = [dt(f"pool_w{l}", [4, 256, 256]) for l in range(NL)]
    I["w_a"] = [dt(f"w_branch_a{l}", [1024, D]) for l in range(NL)]
    I["w_b"] = [dt(f"w_branch_b{l}", [D, D]) for l in range(NL)]
    I["w_c"] = [dt(f"w_branch_c{l}", [1024, D]) for l in range(NL)]
    I["w_o"] = [dt(f"w_o{l}", [D, D]) for l in range(NL)]
    I["w_up"] = [dt(f"w_up{l}", [D, DFF]) for l in range(NL)]
    I["w_down"] = [dt(f"w_down{l}", [DFF, D]) for l in range(NL)]
    I["params"] = dt("params", [NL, 128, NPARAM])
    I["ropeC"] = dt("ropeC", [64, NKEY])
    I["ropeS"] = dt("ropeS", [64, NKEY])
    I["invc"] = dt("invc", [128, 4, NKEY])
    I["kbias"] = dt("kbias", [128, NKB])
    I["flag"] = dt("flag", [128, 1])
    I["tri"] = dt("tri", [128, 128], BF16)
    I["ones"] = dt("ones", [128, 128], BF16)
    I["swap"] = dt("swap", [64, 64], BF16)
    OUT = dt("outT", [D, NTOK], F32, kind="ExternalOutput")
    KTn_d = [dt(f"ktn{l}", [H, 128, NKEY], BF16, kind="Internal") for l in range(NL)]
    KTr_d = [dt(f"ktr{l}", [H, 64, NKEY], BF16, kind="Internal") for l in range(NL)]
    V_d = [dt(f"v{l}", [H, 128, NKB, 128], BF16, kind="Internal") for l in range(NL)]
    X1_d = [dt(f"x1_{l}", [D, NKEY], F32, kind="Internal") for l in range(NL - 1)]

    with contextlib.ExitStack() as es:
        k = KB(nc, es)
        sb = lambda name, shape, dtype: es.enter_context(nc.sbuf_tensor(name, list(shape), dtype))
        xT = sb("xT_sb", [128, NCD, TT], F32); r_xT = Res("xT")
        hT = sb("hT_sb", [128, NCD, TT], BF16); r_hT = Res("hT")
        ybT = sb("ybT_sb", [128, H, TT], BF16); r_ybT = Res("ybT")
        wsl = [sb(f"wslot{i}", [128, SLOT], BF16) for i in range(NSLOT)]
        r_wsl = [Res(f"wslot{i}") for i in range(NSLOT)]
        params = sb("params_sb", [128, NL, NPARAM], F32); r_par = Res("params")
        tri = sb("tri_sb", [128, 128], BF16)
        ones = sb("ones_sb", [128, 128], BF16)
        swp = sb("swap_sb", [64, 64], BF16)
        kbias = sb("kbias_sb", [128, NKB], F32)
        flag = sb("flag_sb", [128, 1], F32)
        epsb = sb("eps_sb", [128, 1], F32)
        zb = sb("zb_sb", [128, 1], F32)
        r_const = Res("consts")
        rstd = [sb(f"rstd{i}", [128, TT], F32) for i in range(3)]
        r_rstd = [Res(f"rstd{i}") for i in range(3)]
        lnv = [sb(f"lnv{i}", [128, TT], F32) for i in range(2)]
        r_lnv = [Res(f"lnv{i}") for i in range(2)]
        ropeC = sb("ropeC_sb", [64, TT], F32)
        ropeS = sb("ropeS_sb", [64, TT], F32)
        r_rope = Res("rope")
        invc = sb("invc_sb", [128, 4, TT], F32); r_invc = Res("invc")
        PTall = sb("ptall", [128, 9, TT], BF16)
        PT = [PTall[:, i, :] for i in range(9)]
        r_PT = [Res(f"pt{i}") for i in range(9)]
        rden = sb("rden", [128, TT], F32); r_rden = Res("rden")
        rl = [sb(f"relu{i}", [128, TT], F32) for i in range(2)]
        r_rl = [Res(f"relu{i}") for i in range(2)]
        halo_cu = sb("halo_cu", [128, 8, 2], BF16); r_hcu = Res("halo_cu")
        halo_pin = sb("halo_pin", [128, 8, 16], F32); r_hpin = Res("halo_pin")
        U1N = 24 * 1024
        U2N = 28 * 1024
        U1 = Region(sb("U1", [128, U1N], BF16), U1N)
        U2 = Region(sb("U2", [128, U2N], BF16), U2N)
        U1.set_phase("kv")
        kvl, r_kvl = U1.take("kvl", [5, TT], F32)
        sq5, r_sq5 = U1.take("sq5", [5, TT], BF16)
        kvnT, r_kvnT = U1.take("kvnT", [4, TT], BF16)
        KTn_all, r_KTn = U1.take("KTn_all", [H, TT], BF16)
        KTr_all, r_KTr = U1.take("KTr_all", [H, TT], BF16)
        Vsb, r_Vsb = U1.take("Vsb", [3, D], BF16)
        krg, r_krg = U1.take("krg", [TT], F32)
        kx, r_kx = U1.take("kx", [2, TT], F32)
        kxb, r_kxb = U1.take("kxb", [TT], BF16)
        sqh, r_sqh = U1.take("sqh", [2, 2, TT], BF16)
        U1.set_phase("mix")
        usb, r_usb = U1.take("usb", [2, TT], F32)
        cu, r_cu = U1.take("cu", [8, TT + 2], BF16)
        bsb, r_bsb = U1.take("bsb", [8, TT], BF16)
        ctmp, r_ctmp = U1.take("ctmp", [2, TT], F32)
        yaT, r_yaT = U1.take("yaT", [8, TT], BF16)
        pin, r_pin = U1.take("pin", [8, TT + 16], F32)
        pw1, r_pw1 = U1.take("pw1", [2, TT + 16], F32)
        pw2, r_pw2 = U1.take("pw2", [2, TT + 16], F32)
        pooled, r_pooled = U1.take("pooled", [8, TT], BF16)
        ycT, r_ycT = U1.take("ycT", [8, TT], BF16)
        merged, r_merged = U1.take("merged", [NCD, TT], BF16)
        U2.set_phase("att")
        qlat, r_qlat = U2.take("qlat", [4, TT], F32)
        sq4, r_sq4 = U2.take("sq4", [4, TT], BF16)
        qnT, r_qnT = U2.take("qnT", [4, TT], BF16)
        QTn, r_QTn = U2.take("QTn", [H, TT], BF16)
        QTr, r_QTr = U2.take("QTr", [H, TT], BF16)
        qx, r_qx = U2.take("qx", [2, TT], F32)
        qxb, r_qxb = U2.take("qxb", [TT], BF16)
        sqq, r_sqq = U2.take("sqq", [2, 2, TT], BF16)
        Kn_s = []; Kr_s = []; V_s = []
        for i in range(2):
            a, ra = U2.take(f"Kn_s{i}", [SEGK], BF16)
            b_, rb = U2.take(f"Kr_s{i}", [SEGK], BF16)
            c_, rc = U2.take(f"V_s{i}", [17, 128], BF16)
            Kn_s.append((a, ra)); Kr_s.append((b_, rb)); V_s.append((c_, rc))
        U2.set_phase("mrg")
        gate = []; r_gate = []; mtmp = []; r_mtmp = []
        for i in range(3):
            a, ra = U2.take(f"gate{i}", [TT], F32)
            gate.append(a); r_gate.append(ra)
        for i in range(6):
            a, ra = U2.take(f"mtmp{i}", [TT], F32)
            mtmp.append(a); r_mtmp.append(ra)
        U2.set_phase("mlp")
        sq16, r_sq16 = U2.take("sq16", [NCD // 2, TT], BF16)
        sq16b, r_sq16b = U2.take("sq16b", [NCD // 2, TT], BF16)
        act, r_act = U2.take("act", [64, TT], BF16)
        psall = es.enter_context(nc.psum_tensor("psall", [128, 8 * 512], F32))
        banks = [psall[:, 512 * i:512 * (i + 1)] for i in range(8)]
        r_bank = [Res(f"bank{i}", excl=True) for i in range(8)]
        ring = {"set": list(range(8)), "p": 0}

        def ps_next():
            i = ring["set"][ring["p"] % len(ring["set"])]
            ring["p"] += 1
            return banks[i], r_bank[i]

        WC_d = [dt(f"wcache{i}", [150, 128, SLOT], BF16, kind="Internal") for i in range(NL)]
        ws = WStream(k, wsl, r_wsl, lambda ci: WC_d[ci // 150][ci % 150])
        cc_sem = es.enter_context(nc.semaphore("cc_sem"))

        def dump(name, ap, shape, dtype, r):
            if not DEBUG or k.dry:
                return
            dram = nc.dram_tensor("dbg_" + name, list(shape), dtype, kind="ExternalOutput").ap()
            DBG_NAMES.append("dbg_" + name)
            k.dma("sp", dram, ap, reads=(r,), writes=(Res("dbg_" + name),))

        r_kv_d = [[Res(f"kvd{l}_{t}") for t in range(2 * NT)] for l in range(NL)]
        r_x1 = [[Res(f"x1_{l}_{t}") for t in range(2 * NT)] for l in range(NL)]
        r_out = [Res(f"out_{t}") for t in range(NT)]

        def V(fn, **kw):
            return lambda e: fn(e, **kw)

        def load_consts():
            k.dma("sp", params[:], I["params"].rearrange("l p c -> p l c"), writes=(r_par,))
            k.dma("sp", tri[:], I["tri"], writes=(r_const,))
            k.dma("sp", ones[:], I["ones"], writes=(r_const,))
            k.dma("sp", swp[:], I["swap"], writes=(r_const,))
            k.dma("sp", kbias[:], I["kbias"], writes=(r_const,))
            k.dma("sp", flag[:], I["flag"], writes=(r_const,))
            k.op("dve", lambda e: e.memset(epsb[:], EPS), writes=(r_const,))
            k.op("dve", lambda e: e.memset(zb[:], 0.0), writes=(r_const,))

        def pcol(li, c):
            return params[:, li, c:c + 1]

        def rstd_from_ps(ps, r_ps, n, dim, slot):
            lv, r_lv = lnv[slot % 2], r_lnv[slot % 2]
            rs, r_rs = rstd[slot], r_rstd[slot]
            k.op("act", lambda e: e.activation(out=lv[:, :n], in_=ps[:, :n], func=AF.Ln, bias=epsb[:], scale=1.0 / dim),
                 reads=(r_ps, r_const), writes=(r_lv,))
            k.op("act", lambda e: e.activation(out=rs[:, :n], in_=lv[:, :n], func=AF.Exp, scale=-0.5),
                 reads=(r_lv,), writes=(r_rs,))
            return rs, r_rs

        def norm_fm(src, r_src, nch, n, sq, r_sq, li, gcol, dst, r_dst, dim, slot):
            split = (nch == NCD)

            def sqc(c):
                if split:
                    return (sq16b[:, c // 2, :n], r_sq16b) if c % 2 else (sq16[:, c // 2, :n], r_sq16)
                return sq[:, c, :n], r_sq
            for c in range(nch):
                o_, r_o = sqc(c)
                if split and c % 2 == 1:
                    k.op("pool", lambda e, c=c, o_=o_: e.tensor_tensor(out=o_, in0=src[:, c, :n], in1=src[:, c, :n], op=ALU.mult),
                         reads=(r_src,), writes=(r_o,))
                else:
                    k.op("act", lambda e, c=c, o_=o_: e.activation(out=o_, in_=src[:, c, :n], func=AF.Square),
                         reads=(r_src,), writes=(r_o,))
            ps, r_ps = ps_next()
            rd = (r_sq16, r_sq16b, r_const) if split else (r_sq, r_const)
            k.mm([(ps[:, :n], ones[:, :], sqc(c)[0], c == 0, c == nch - 1) for c in range(nch)],
                 reads=rd, writes=(r_ps,))
            rs, r_rs = rstd_from_ps(ps, r_ps, n, dim, slot)
            for c in range(nch):
                k.op("dve", lambda e, c=c: e.scalar_tensor_tensor(out=dst[:, c, :n], in0=src[:, c, :n], scalar=pcol(li, gcol + c),
                                                                  in1=rs[:, :n], op0=ALU.mult, op1=ALU.mult),
                     reads=(r_src, r_rs, r_par), writes=(r_dst,))

        def w_in_loader(li, c0, ncols):
            def f(s):
                return [(slot_view(s, NCD, ncols), I["w_in"][li][:, c0:c0 + ncols].rearrange("(k p) n -> p k n", p=128))]
            f.key = ("w_in", li, c0, ncols)
            f.nel = NCD * ncols
            return f

        def gen_loader(ap2d, kc, ncols, key):
            def f(s):
                return [(slot_view(s, kc, ncols), ap2d.rearrange("(k p) n -> p k n", p=128))]
            f.key = key
            f.nel = kc * ncols
            return f

        def rope_apply(xps, r_xps, n, li, gcolr, x_f, r_x_f, xb, r_xb, dst_fn):
            k.op("act", lambda e: e.mul(out=xb[0:64, :n], in_=xps[0:64, :n], mul=pcol(li, gcolr)[0:64, :]),
                 reads=(r_xps, r_par), writes=(r_xb,))
            k.op("dve", lambda e: e.scalar_tensor_tensor(out=x_f[0:64, 0, :n], in0=xps[0:64, :n], scalar=pcol(li, gcolr)[0:64, :],
                                                         in1=ropeC[0:64, :n], op0=ALU.mult, op1=ALU.mult),
                 reads=(r_xps, r_par, r_rope), writes=(r_x_f,))
            ps2, r_ps2 = ps_next()
            k.mm([(ps2[0:64, :n], swp[:, :], xb[0:64, :n], True, True)], reads=(r_xb, r_const), writes=(r_ps2,))
            k.op("dve", lambda e: e.tensor_tensor(out=x_f[0:64, 1, :n], in0=ps2[0:64, :n], in1=ropeS[0:64, :n], op=ALU.mult),
                 reads=(r_ps2, r_rope, r_x_f), writes=(r_x_f,))
            k.op("pool", lambda e: e.tensor_tensor(out=x_f[0:64, 0, :n], in0=x_f[0:64, 0, :n], in1=x_f[0:64, 1, :n], op=ALU.add),
                 reads=(r_x_f,), writes=(r_x_f,))

        def load_rope(key0, n):
            k.dma("sp", ropeC[:, :n], I["ropeC"][:, key0:key0 + n], writes=(r_rope,))
            k.dma("sp", ropeS[:, :n], I["ropeS"][:, key0:key0 + n], writes=(r_rope,))

        def kv_path(li, n, key0, r_kvdst):
            for g in range(2):
                s, r_s = ws.get(w_in_loader(li, C_KVL + 256 * g, 256))
                sv = slot_view(s, NCD, 256)
                for m in range(2):
                    ps, r_ps = ps_next()
                    k.mm([(ps[:, :n], sv[:, kc, m * 128:(m + 1) * 128], hT[:, kc, :n], kc == 0, kc == NCD - 1) for kc in range(NCD)],
                         reads=(r_s, r_hT), writes=(r_ps,))
                    c = 2 * g + m
                    k.op("dve", lambda e, c=c, ps=ps: e.tensor_copy(out=kvl[:, c, :n], in_=ps[:, :n]), reads=(r_ps,), writes=(r_kvl,))
            s, r_s = ws.get(w_in_loader(li, C_KR, 64))
            sv = slot_view(s, NCD, 64)
            psr, r_psr = ps_next()
            k.mm([(psr[0:64, :n], sv[:, kc, :], hT[:, kc, :n], kc == 0, kc == NCD - 1) for kc in range(NCD)],
                 reads=(r_s, r_hT), writes=(r_psr,))
            k.op("act", lambda e: e.activation(out=sq5[0:64, 4, :n], in_=psr[0:64, :n], func=AF.Square), reads=(r_psr,), writes=(r_sq5,))
            rope_apply(psr, r_psr, n, li, P_KR, kx, r_kx, kxb, r_kxb, None)
            k.op("pool", lambda e: e.tensor_copy(out=krg[0:64, :n], in_=kx[0:64, 0, :n]), reads=(r_kx,), writes=(r_krg,))
            norm_fm(kvl, r_kvl, 4, n, sq5, r_sq5, li, P_KVLN, kvnT, r_kvnT, 512.0, 2)
            ntb = (n + 127) // 128
            for hg in range(4):
                s, r_s = ws.get(gen_loader(I["w_ukv"][li][:, 1024 * hg:1024 * (hg + 1)], 4, 1024, ("w_ukv", li, hg)))
                sv = slot_view(s, 4, 1024)
                for hh in range(4):
                    h = 4 * hg + hh
                    ps, r_ps = ps_next()
                    k.mm([(ps[:, :n], sv[:, kc, 256 * hh:256 * hh + 128], kvnT[:, kc, :n], kc == 0, kc == 3) for kc in range(4)],
                         reads=(r_s, r_kvnT), writes=(r_ps,))
                    sq, r_sq = sqh[:, h % 2, 0, :], r_sqh
                    k.op("act", lambda e, ps=ps, sq=sq: e.activation(out=sq[:, :n], in_=ps[:, :n], func=AF.Square),
                         reads=(r_ps,), writes=(r_sq,))
                    pss, r_pss = ps_next()
                    k.mm([(pss[:, :n], ones[:, :], sq[:, :n], True, False),
                          (pss[:, :n], ones[0:64, :], sq5[0:64, 4, :n], False, True)],
                         reads=(r_sq, r_sq5, r_const), writes=(r_pss,))
                    rs, r_rs = rstd_from_ps(pss, r_pss, n, 192.0, h % 2)
                    k.op("dve", lambda e, ps=ps, rs=rs, h=h: e.scalar_tensor_tensor(out=KTn_all[:, h, :n], in0=ps[:, :n], scalar=pcol(li, P_KN),
                                                                                   in1=rs[:, :n], op0=ALU.mult, op1=ALU.mult),
                         reads=(r_ps, r_rs, r_par), writes=(r_KTn,))
                    k.op("pool", lambda e, rs=rs, h=h: e.tensor_tensor(out=KTr_all[0:64, h, :n], in0=krg[0:64, :n], in1=rs[0:64, :n], op=ALU.mult),
                         reads=(r_krg, r_rs), writes=(r_KTr,))
                for tb in range(ntb):
                    m = min(128, n - 128 * tb)
                    ps, r_ps = ps_next()
                    mms = []
                    for hh in range(4):
                        for kc in range(4):
                            mms.append((ps[0:m, 128 * hh:128 * (hh + 1)], kvnT[:, kc, 128 * tb:128 * tb + m],
                                        sv[:, kc, 256 * hh + 128:256 * hh + 256], kc == 0, kc == 3))
                    k.mm(mms, reads=(r_s, r_kvnT), writes=(r_ps,))
                    k.op("act", lambda e, ps=ps, m=m, tb=tb, hg=hg: e.activation(out=Vsb[0:m, tb, 512 * hg:512 * (hg + 1)], in_=ps[0:m, 0:512], func=AF.Copy),
                         reads=(r_ps,), writes=(r_Vsb,))
            k.dma("sp", KTn_d[li][:, :, key0:key0 + n].rearrange("h p t -> p h t"), KTn_all[:, :, :n], reads=(r_KTn,), writes=(r_kvdst,))
            k.dma("sp", KTr_d[li][:, :, key0:key0 + n].rearrange("h p t -> p h t"), KTr_all[0:64, :, :n], reads=(r_KTr,), writes=(r_kvdst,))
            cuts = sorted(set([0, n] + [128 * i for i in range(1, ntb)] + [g * 128 - key0 for g in range(key0 // 128 + 1, (key0 + n) // 128 + 1) if 0 < g * 128 - key0 < n]))
            for a_, b_ in zip(cuts[:-1], cuts[1:]):
                tb, lp0 = divmod(a_, 128)
                gb, gp0 = divmod(key0 + a_, 128)
                np_ = b_ - a_
                k.dma("sp", V_d[li][:, gp0:gp0 + np_, gb, :].rearrange("h p d -> p h d"),
                      Vsb[lp0:lp0 + np_, tb, :].rearrange("p (h d) -> p h d", h=H), reads=(r_Vsb,), writes=(r_kvdst,))

        def mixer_inputs(li, n, c0, with_b):
            nn = n - c0
            for pr in range(4):
                su, r_su = ws.get(w_in_loader(li, C_U + 256 * pr, 256))
                sc, r_sc = ws.get(w_in_loader(li, C_C + 256 * pr, 256), keep=1)
                for m in range(2):
                    ch = 2 * pr + m
                    ps, r_ps = ps_next()
                    svu = slot_view(su, NCD, 256)
                    k.mm([(ps[:, :nn], svu[:, kc, m * 128:(m + 1) * 128], hT[:, kc, c0:n], kc == 0, kc == NCD - 1) for kc in range(NCD)],
                         reads=(r_su, r_hT), writes=(r_ps,))
                    ub = usb[:, ch % 2, :]
                    k.op("act", lambda e, ps=ps, ub=ub: e.activation(out=ub[:, :nn], in_=ps[:, :nn], func=AF.Copy), reads=(r_ps,), writes=(r_usb,))
                    ps2, r_ps2 = ps_next()
                    svc = slot_view(sc, NCD, 256)
                    k.mm([(ps2[:, :nn], svc[:, kc, m * 128:(m + 1) * 128], hT[:, kc, c0:n], kc == 0, kc == NCD - 1) for kc in range(NCD)],
                         reads=(r_sc, r_hT), writes=(r_ps2,))
                    k.op("dve", lambda e, ps2=ps2, ub=ub, ch=ch: e.tensor_tensor(out=cu[:, ch, 2 + c0:2 + n], in0=ps2[:, :nn], in1=ub[:, :nn], op=ALU.mult),
                         reads=(r_ps2, r_usb), writes=(r_cu,))
                if with_b:
                    sbb, r_sbb = ws.get(w_in_loader(li, C_B + 256 * pr, 256))
                    svb = slot_view(sbb, NCD, 256)
                    for m in range(2):
                        ch = 2 * pr + m
                        ps3, r_ps3 = ps_next()
                        k.mm([(ps3[:, :nn], svb[:, kc, m * 128:(m + 1) * 128], hT[:, kc, c0:n], kc == 0, kc == NCD - 1) for kc in range(NCD)],
                             reads=(r_sbb, r_hT), writes=(r_ps3,))
                        k.op("act", lambda e, ps3=ps3, ch=ch: e.activation(out=bsb[:, ch, c0:n], in_=ps3[:, :nn], func=AF.Copy),
                             reads=(r_ps3,), writes=(r_bsb,))
            for pr in range(4):
                sp_, r_sp = ws.get(w_in_loader(li, C_POOL + 256 * pr, 256))
                svp = slot_view(sp_, NCD, 256)
                for m in range(2):
                    ch = 2 * pr + m
                    ps, r_ps = ps_next()
                    k.mm([(ps[:, :nn], svp[:, kc, m * 128:(m + 1) * 128], hT[:, kc, c0:n], kc == 0, kc == NCD - 1) for kc in range(NCD)],
                         reads=(r_sp, r_hT), writes=(r_ps,))
                    k.op("act", lambda e, ps=ps, ch=ch: e.activation(out=pin[:, ch, 16 + c0:16 + n], in_=ps[:, :nn], func=AF.Copy),
                         reads=(r_ps,), writes=(r_pin,))

        def save_halo(n):
            k.op("pool", lambda e: e.tensor_copy(out=halo_cu[:, :, :], in_=cu[:, :, n:n + 2]), reads=(r_cu,), writes=(r_hcu,))
            k.op("pool", lambda e: e.tensor_copy(out=halo_pin[:, :, 1:16], in_=pin[:, :, n + 1:n + 16]), reads=(r_pin,), writes=(r_hpin,))

        def restore_halo(scale_flag):
            if scale_flag:
                k.op("pool", lambda e: e.tensor_scalar(out=cu[:, :, 0:2], in0=halo_cu[:, :, :], scalar1=flag[:, 0:1], scalar2=None, op0=ALU.mult),
                     reads=(r_hcu, r_const), writes=(r_cu,))
                k.op("pool", lambda e: e.tensor_scalar(out=pin[:, :, 1:16], in0=halo_pin[:, :, 1:16], scalar1=flag[:, 0:1], scalar2=None, op0=ALU.mult),
                     reads=(r_hpin, r_const), writes=(r_pin,))
            else:
                k.op("pool", lambda e: e.tensor_copy(out=cu[:, :, 0:2], in_=halo_cu[:, :, :]), reads=(r_hcu,), writes=(r_cu,))
                k.op("pool", lambda e: e.tensor_copy(out=pin[:, :, 1:16], in_=halo_pin[:, :, 1:16]), reads=(r_hpin,), writes=(r_pin,))

        def q_path(li, n):
            for g in range(2):
                s, r_s = ws.get(w_in_loader(li, C_QL + 256 * g, 256))
                sv = slot_view(s, NCD, 256)
                for m in range(2):
                    ps, r_ps = ps_next()
                    k.mm([(ps[:, :n], sv[:, kc, m * 128:(m + 1) * 128], hT[:, kc, :n], kc == 0, kc == NCD - 1) for kc in range(NCD)],
                         reads=(r_s, r_hT), writes=(r_ps,))
                    c = 2 * g + m
                    k.op("dve", lambda e, c=c, ps=ps: e.tensor_copy(out=qlat[:, c, :n], in_=ps[:, :n]), reads=(r_ps,), writes=(r_qlat,))
            norm_fm(qlat, r_qlat, 4, n, sq4, r_sq4, li, P_QLN, qnT, r_qnT, 512.0, 2)
            for hg in range(4):
                s, r_s = ws.get(gen_loader(I["w_uq"][li][:, 768 * hg:768 * (hg + 1)], 4, 768, ("w_uq", li, hg)))
                sv = slot_view(s, 4, 768)
                for hh in range(4):
                    h = 4 * hg + hh
                    psn, r_psn = ps_next()
                    k.mm([(psn[:, :n], sv[:, kc, 192 * hh:192 * hh + 128], qnT[:, kc, :n], kc == 0, kc == 3) for kc in range(4)],
                         reads=(r_s, r_qnT), writes=(r_psn,))
                    psr, r_psr = ps_next()
                    k.mm([(psr[0:64, :n], sv[:, kc, 192 * hh + 128:192 * hh + 192], qnT[:, kc, :n], kc == 0, kc == 3) for kc in range(4)],
                         reads=(r_s, r_qnT), writes=(r_psr,))
                    sqn = sqq[:, h % 2, 0, :]
                    sqr = sqq[:, h % 2, 1, :]
                    k.op("act", lambda e, psn=psn, sqn=sqn: e.activation(out=sqn[:, :n], in_=psn[:, :n], func=AF.Square), reads=(r_psn,), writes=(r_sqq,))
                    k.op("act", lambda e, psr=psr, sqr=sqr: e.activation(out=sqr[0:64, :n], in_=psr[0:64, :n], func=AF.Square), reads=(r_psr,), writes=(r_sqq,))
                    pss, r_pss = ps_next()
                    k.mm([(pss[:, :n], ones[:, :], sqn[:, :n], True, False), (pss[:, :n], ones[0:64, :], sqr[0:64, :n], False, True)],
                         reads=(r_sqq, r_const), writes=(r_pss,))
                    rs, r_rs = rstd_from_ps(pss, r_pss, n, 192.0, h % 2)
                    k.op("dve", lambda e, psn=psn, rs=rs, h=h: e.scalar_tensor_tensor(out=QTn[:, h, :n], in0=psn[:, :n], scalar=pcol(li, P_QN),
                                                                                     in1=rs[:, :n], op0=ALU.mult, op1=ALU.mult),
                         reads=(r_psn, r_rs, r_par), writes=(r_QTn,))
                    rope_apply(psr, r_psr, n, li, P_QR, qx, r_qx, qxb, r_qxb, None)
                    k.op("pool", lambda e, rs=rs, h=h: e.tensor_tensor(out=QTr[0:64, h, :n], in0=qx[0:64, 0, :n], in1=rs[0:64, :n], op=ALU.mult),
                         reads=(r_qx, r_rs), writes=(r_QTr,))

        def attention(li, t, n):
            qs = t * TT
            qe = qs + n
            nkb = (qe + 127) // 128
            segs = [(0, min(nkb, SEG0_KB))]
            if nkb > SEG0_KB:
                segs.append((SEG0_KB, nkb))
            kv_reads = tuple(r_kv_d[li][0:t + 1])
            ring["set"] = [0, 1, 2, 3]
            loads = [(h, si) for h in range(H) for si in range(len(segs))]

            def issue_load(idx):
                h, si = loads[idx]
                b0, b1 = segs[si]
                k0 = 128 * b0
                k1 = min(128 * b1, qe)
                nk = k1 - k0
                (kn, r_kn), (kr, r_kr), (vv, r_vv) = Kn_s[idx % 2], Kr_s[idx % 2], V_s[idx % 2]
                k.dma("sp", kn[:, 0:nk], KTn_d[li][h, :, k0:k1], reads=kv_reads, writes=(r_kn,))
                k.dma("sp", kr[0:64, 0:nk], KTr_d[li][h, :, k0:k1], reads=kv_reads, writes=(r_kr,))
                nfull = nk // 128
                rem = nk - 128 * nfull
                if nfull:
                    k.dma("sp", vv[:, 0:nfull, :], V_d[li][h, :, b0:b0 + nfull, :], reads=kv_reads, writes=(r_vv,))
                if rem:
                    k.dma("sp", vv[0:rem, nfull, :], V_d[li][h, 0:rem, b0 + nfull, :], reads=kv_reads, writes=(r_vv,))

            items = []
            for idx, (h, si) in enumerate(loads):
                b0, b1 = segs[si]
                for j in range(b0, b1):
                    items.append((idx, h, si, j))
            last_item_of_load = {}
            for n_, it in enumerate(items):
                last_item_of_load[it[0]] = n_
            G = 3
            issue_load(0)
            if len(loads) > 1:
                issue_load(1)
            stash = {}
            O, r_O = banks[6], r_bank[6]
            Dn, r_Dn = banks[7], r_bank[7]
            ring["set"] = [0, 1, 2, 3, 4, 5]

            def item_geom(n_):
                idx, h, si, j = items[n_]
                kp = min(128, qe - 128 * j, NKEY - 128 * j)
                cst = max(0, 128 * j - qs)
                diag = (128 * j + 128 > qs)
                bcls = 0 if t < NT else (0 if j < 16 else (1 if j == 16 else 2))
                return kp, cst, diag, bcls

            def stage_s_mm(n_, st, r_st):
                idx, h, si, j = items[n_]
                b0, b1 = segs[si]
                (kn, r_kn), (kr, r_kr), (vv, r_vv) = Kn_s[idx % 2], Kr_s[idx % 2], V_s[idx % 2]
                kp, cst, diag, bcls = item_geom(n_)
                kl = 128 * (j - b0)
                k.mm([(st[0:kp, cst:n], kn[:, kl:kl + kp], QTn[:, h, cst:n], True, False),
                      (st[0:kp, cst:n], kr[0:64, kl:kl + kp], QTr[0:64, h, cst:n], False, True)],
                     reads=(r_kn, r_kr, r_QTn, r_QTr), writes=(r_st,))

            def stage_s_exp(n_, st, r_st, pslot):
                idx, h, si, j = items[n_]
                kp, cst, diag, bcls = item_geom(n_)
                pt_, r_pt = PT[pslot], r_PT[pslot]
                k.op("act", lambda e: e.activation(out=pt_[0:kp, cst:n], in_=st[0:kp, cst:n], func=AF.Exp,
                                                   bias=(kbias[0:kp, j:j + 1] if t >= NT else zb[0:kp, 0:1]), scale=SCALE),
                     reads=(r_st, r_const), writes=(r_pt,))
                dend = min(n, 128 * j + 128 - qs)
                if diag:
                    c0 = qs + cst - 128 * j
                    k.op("pool", lambda e: e.tensor_tensor(out=pt_[0:kp, cst:dend], in0=pt_[0:kp, cst:dend], in1=tri[0:kp, c0:c0 + dend - cst], op=ALU.mult),
                         reads=(r_pt, r_const), writes=(r_pt,))
                stash[n_] = (kp, cst, pt_, r_pt)

            def stage_pv(n_):
                idx, h, si, j = items[n_]
                b0, b1 = segs[si]
                (kn, r_kn), (kr, r_kr), (vv, r_vv) = Kn_s[idx % 2], Kr_s[idx % 2], V_s[idx % 2]
                kp, cst, pt_, r_pt = stash.pop(n_)
                first = (j == 0)
                last = (j == nkb - 1)
                k.mm([(O[:, cst:n], vv[0:kp, j - b0, :], pt_[0:kp, cst:n], first, last),
                      (Dn[:, cst:n], ones[0:kp, :], pt_[0:kp, cst:n], first, last)], reads=(r_vv, r_pt, r_const), writes=(r_O, r_Dn))
                if last:
                    k.op("dve", lambda e: e.reciprocal(out=rden[:, :n], in_=Dn[:, :n]), reads=(r_Dn,), writes=(r_rden,))
                    k.op("dve", lambda e: e.tensor_tensor(out=ybT[:, h, :n], in0=O[:, :n], in1=rden[:, :n], op=ALU.mult),
                         reads=(r_O, r_rden), writes=(r_ybT,))
                if last_item_of_load[idx] == n_ and idx + 2 < len(loads):
                    issue_load(idx + 2)

            groups = []
            cur = []
            for n_ in range(len(items)):
                kp, cst, diag, bcls = item_geom(n_)
                full = (kp == 128 and cst == 0 and not diag)
                if cur:
                    kp0, cst0, diag0, bcls0 = item_geom(cur[0])
                    full0 = (kp0 == 128 and cst0 == 0 and not diag0)
                    if len(cur) == G or not (full and full0 and bcls == bcls0):
                        groups.append(cur)
                        cur = []
                cur.append(n_)
            if cur:
                groups.append(cur)
            for g_ in range(len(groups) + 1):
                if g_ < len(groups):
                    grp = groups[g_]
                    base = 3 * (g_ % 2)
                    pbase = 3 * (g_ % 3)
                    bks = [(banks[base + i], r_bank[base + i]) for i in range(len(grp))]
                    k.prewait("pe", reads=(), writes=tuple(rb for (_, rb) in bks))
                    for n_, (st, r_st) in zip(grp, bks):
                        stage_s_mm(n_, st, r_st)
                    kp, cst, diag, bcls = item_geom(grp[0])
                    if len(grp) > 1:
                        ng = len(grp)
                        j0 = items[grp[0]][3]
                        src = psall[:, 512 * base:512 * (base + ng)].rearrange("p (g c) -> p g c", g=ng)[:, :, 0:n]
                        dstp = PTall[:, pbase:pbase + ng, 0:n]
                        k.op("act", lambda e: e.activation(out=dstp, in_=src, func=AF.Exp,
                                                           bias=(kbias[:, j0:j0 + 1] if t >= NT else zb[:, 0:1]), scale=SCALE),
                             reads=tuple(rb for (_, rb) in bks) + (r_const,), writes=tuple(r_PT[pbase + i] for i in range(ng)))
                        for i, n_ in enumerate(grp):
                            stash[n_] = (128, 0, PT[pbase + i], r_PT[pbase + i])
                    else:
                        stage_s_exp(grp[0], bks[0][0], bks[0][1], pbase)
                if g_ >= 1:
                    grp = groups[g_ - 1]
                    k.prewait("pe", reads=tuple(stash[n_][3] for n_ in grp), writes=())
                    for n_ in grp:
                        stage_pv(n_)
            ring["set"] = list(range(8))

        def conv_pool(li, t, n):
            k.dma("sp", invc[:, :, :n], I["invc"][:, :, t * TT:t * TT + n], writes=(r_invc,))
            for ch in range(8):
                tmp = ctmp[:, ch % 2, :]
                k.op("pool", lambda e, ch=ch, tmp=tmp: e.tensor_scalar(out=tmp[:, :n], in0=cu[:, ch, 0:n], scalar1=pcol(li, P_CW + 3 * ch), scalar2=None, op0=ALU.mult),
                     reads=(r_cu, r_par), writes=(r_ctmp,))
                k.op("dve", lambda e, ch=ch, tmp=tmp: e.scalar_tensor_tensor(out=tmp[:, :n], in0=cu[:, ch, 1:n + 1], scalar=pcol(li, P_CW + 3 * ch + 1),
                                                                             in1=tmp[:, :n], op0=ALU.mult, op1=ALU.add),
                     reads=(r_cu, r_par, r_ctmp), writes=(r_ctmp,))
                k.op("dve", lambda e, ch=ch, tmp=tmp: e.scalar_tensor_tensor(out=tmp[:, :n], in0=cu[:, ch, 2:n + 2], scalar=pcol(li, P_CW + 3 * ch + 2),
                                                                             in1=tmp[:, :n], op0=ALU.mult, op1=ALU.add),
                     reads=(r_cu, r_par, r_ctmp), writes=(r_ctmp,))
                k.op("dve", lambda e, ch=ch, tmp=tmp: e.tensor_tensor(out=yaT[:, ch, :n], in0=tmp[:, :n], in1=bsb[:, ch, :n], op=ALU.mult),
                     reads=(r_ctmp, r_bsb), writes=(r_yaT,))
            for g in range(4):
                src = pin[:, 2 * g:2 * g + 2, :]
                cur = src
                r_cur = r_pin
                lo = 1
                bufs = [(pw1, r_pw1), (pw2, r_pw2)]
                for step in range(g + 1):
                    sh = 1 << step
                    dstb, r_dst = bufs[step % 2]
                    lo2 = lo + sh
                    k.op("pool", lambda e, cur=cur, dstb=dstb, lo2=lo2, sh=sh: e.tensor_tensor(out=dstb[:, :, lo2:n + 16], in0=cur[:, :, lo2:n + 16],
                                                                                             in1=cur[:, :, lo2 - sh:n + 16 - sh], op=ALU.add),
                         reads=(r_cur,), writes=(r_dst,))
                    cur, r_cur, lo = dstb, r_dst, lo2
                for m in range(2):
                    ch = 2 * g + m
                    tmp = ctmp[:, m, :]
                    k.op("dve", lambda e, cur=cur, m=m, g=g, tmp=tmp: e.tensor_tensor(out=tmp[:, :n], in0=cur[:, m, 16:16 + n], in1=invc[:, g, :n], op=ALU.mult),
                         reads=(r_cur, r_invc), writes=(r_ctmp,))
                    k.op("dve", lambda e, ch=ch, tmp=tmp: e.tensor_tensor(out=pooled[:, ch, :n], in0=tmp[:, :n], in1=pin[:, ch, 16:16 + n], op=ALU.subtract),
                         reads=(r_ctmp, r_pin), writes=(r_pooled,))
            def pw_loader(s):
                return [(s[:, 0:2048].rearrange("p (g k n) -> p g k n", g=4, k=2), I["pool_w"][li].rearrange("g (k p) n -> p g k n", p=128))]
            pw_loader.key = ("pool_w", li)
            pw_loader.nel = 2048
            s, r_s = ws.get(pw_loader)
            sv = s[:, 0:2048].rearrange("p (g k n) -> p g k n", g=4, k=2)
            for g in range(4):
                for m in range(2):
                    ps, r_ps = ps_next()
                    k.mm([(ps[:, :n], sv[:, g, kc, m * 128:(m + 1) * 128], pooled[:, 2 * g + kc, :n], kc == 0, kc == 1) for kc in range(2)],
                         reads=(r_s, r_pooled), writes=(r_ps,))
                    ch = 2 * g + m
                    k.op("act", lambda e, ps=ps, ch=ch: e.mul(out=ycT[:, ch, :n], in_=ps[:, :n], mul=pcol(li, P_PS + ch)),
                         reads=(r_ps, r_par), writes=(r_ycT,))

        def merge_and_wo(li, n):
            for jj in range(8):
                for i in range(3):
                    sg, r_sg = ws.get(w_in_loader(li, C_GATE + 2048 * i + 256 * jj, 256))
                    svg = slot_view(sg, NCD, 256)
                    wsrc, nk, src, r_src = [(I["w_a"], 8, yaT, r_yaT), (I["w_b"], 16, ybT, r_ybT), (I["w_c"], 8, ycT, r_ycT)][i]
                    sw_, r_sw = ws.get(gen_loader(wsrc[li][:, 256 * jj:256 * (jj + 1)], nk, 256, ("w_br", li, i, jj)), keep=1)
                    svb = slot_view(sw_, nk, 256)
                    for m in range(2):
                        ps, r_ps = ps_next()
                        k.mm([(ps[:, :n], svg[:, kc, m * 128:(m + 1) * 128], hT[:, kc, :n], kc == 0, kc == NCD - 1) for kc in range(NCD)],
                             reads=(r_sg, r_hT), writes=(r_ps,))
                        k.op("act", lambda e, ps=ps, i=i: e.activation(out=gate[i][:, :n], in_=ps[:, :n], func=AF.Sigmoid), reads=(r_ps,), writes=(r_gate[i],))
                        ps2, r_ps2 = ps_next()
                        k.mm([(ps2[:, :n], svb[:, kc, m * 128:(m + 1) * 128], src[:, kc, :n], kc == 0, kc == nk - 1) for kc in range(nk)],
                             reads=(r_sw, r_src), writes=(r_ps2,))
                        mt, r_mt = mtmp[2 * i + m], r_mtmp[2 * i + m]
                        k.op("dve", lambda e, ps2=ps2, i=i, mt=mt: e.tensor_tensor(out=mt[:, :n], in0=ps2[:, :n], in1=gate[i][:, :n], op=ALU.mult),
                             reads=(r_ps2, r_gate[i]), writes=(r_mt,))
                for m in range(2):
                    oc = 2 * jj + m
                    k.op("pool", lambda e, m=m: e.tensor_tensor(out=mtmp[m][:, :n], in0=mtmp[m][:, :n], in1=mtmp[2 + m][:, :n], op=ALU.add),
                         reads=(r_mtmp[m], r_mtmp[2 + m]), writes=(r_mtmp[m],))
                    k.op("pool", lambda e, oc=oc, m=m: e.tensor_tensor(out=merged[:, oc, :n], in0=mtmp[m][:, :n], in1=mtmp[4 + m][:, :n], op=ALU.add),
                         reads=(r_mtmp[m], r_mtmp[4 + m]), writes=(r_merged,))
            for jj in range(8):
                s, r_s = ws.get(gen_loader(I["w_o"][li][:, 256 * jj:256 * (jj + 1)], 16, 256, ("w_o", li, jj)))
                sv = slot_view(s, 16, 256)
                for m in range(2):
                    oc = 2 * jj + m
                    ps, r_ps = ps_next()
                    k.mm([(ps[:, :n], sv[:, kc, m * 128:(m + 1) * 128], merged[:, kc, :n], kc == 0, kc == NCD - 1) for kc in range(NCD)],
                         reads=(r_s, r_merged), writes=(r_ps,))
                    k.op("dve", lambda e, ps=ps, oc=oc: e.tensor_tensor(out=xT[:, oc, :n], in0=ps[:, :n], in1=xT[:, oc, :n], op=ALU.add),
                         reads=(r_ps, r_xT), writes=(r_xT,))

        def mlp(li, n):
            norm_fm(xT, r_xT, NCD, n, sq16, r_sq16, li, P_MN, hT, r_hT, float(D), 2)
            for jj in range(32):
                s, r_s = ws.get(gen_loader(I["w_up"][li][:, 256 * jj:256 * (jj + 1)], 16, 256, ("w_up", li, jj)))
                sv = slot_view(s, 16, 256)
                for m in range(2):
                    oc = 2 * jj + m
                    ps, r_ps = ps_next()
                    k.mm([(ps[:, :n], sv[:, kc, m * 128:(m + 1) * 128], hT[:, kc, :n], kc == 0, kc == NCD - 1) for kc in range(NCD)],
                         reads=(r_s, r_hT), writes=(r_ps,))
                    rr, r_rr = rl[oc % 2], r_rl[oc % 2]
                    k.op("act", lambda e, ps=ps, rr=rr: e.activation(out=rr[:, :n], in_=ps[:, :n], func=AF.Relu), reads=(r_ps,), writes=(r_rr,))
                    k.op("pool", lambda e, rr=rr, oc=oc: e.tensor_tensor(out=act[:, oc, :n], in0=rr[:, :n], in1=rr[:, :n], op=ALU.mult),
                         reads=(r_rr,), writes=(r_act,))
            for oc in range(NCD):
                ps, r_ps = ps_next()
                for half in range(2):
                    s, r_s = ws.get(gen_loader(I["w_down"][li][4096 * half:4096 * (half + 1), 128 * oc:128 * (oc + 1)], 32, 128, ("w_down", li, oc, half)))
                    sv = slot_view(s, 32, 128)
                    k.mm([(ps[:, :n], sv[:, kc, :], act[:, 32 * half + kc, :n], (half == 0 and kc == 0), (half == 1 and kc == 31)) for kc in range(32)],
                         reads=(r_s, r_act), writes=(r_ps,))
                k.op("dve", lambda e, ps=ps, oc=oc: e.tensor_tensor(out=xT[:, oc, :n], in0=ps[:, :n], in1=xT[:, oc, :n], op=ALU.add),
                     reads=(r_ps, r_xT), writes=(r_xT,))

        def layer(li, src, r_src_list, kv_only_tiles, dst_fn):
            k.op("pool", lambda e: e.memset(halo_cu[:], 0.0), writes=(r_hcu,))
            k.op("pool", lambda e: e.memset(halo_pin[:], 0.0), writes=(r_hpin,))
            for t in range(2 * NT):
                n = TT
                k.dma("sp", xT[:, :, :n], src.rearrange("(c p) t -> p c t", p=128)[:, :, t * TT:t * TT + n],
                      reads=tuple(r_src_list[t:t + 1]), writes=(r_xT,))
                load_rope(t * TT, n)
                norm_fm(xT, r_xT, NCD, n, sq16, r_sq16, li, P_AN, hT, r_hT, float(D), 2)
                kv_path(li, n, t * TT, r_kv_d[li][t])
                if t < kv_only_tiles:
                    if t == kv_only_tiles - 1:
                        mixer_inputs(li, n, n - 16, False)
                        save_halo(n)
                    continue
                q_path(li, n)
                attention(li, t, n)
                restore_halo(t == NT)
                mixer_inputs(li, n, 0, True)
                save_halo(n)
                conv_pool(li, t, n)
                merge_and_wo(li, n)
                mlp(li, n)
                d_ap, r_d = dst_fn(t)
                k.dma("sp", d_ap, xT[:, :, :n], reads=(r_xT,), writes=(r_d,))

        def program():
            ring["p"] = 0
            ring["set"] = list(range(8))
            load_consts()
            for li in range(NL):
                last = (li == NL - 1)
                if li == 0:
                    src, r_src = I["xall"], []
                    kvo = KV_ONLY0
                else:
                    src, r_src = X1_d[li - 1], r_x1[li - 1]
                    kvo = NT
                if last:
                    dst_fn = lambda t: (OUT.rearrange("(c p) t -> p c t", p=128)[:, :, (t - NT) * TT:(t - NT + 1) * TT], r_out[t - NT])
                else:
                    dst_fn = lambda t, li=li: (X1_d[li].rearrange("(c p) t -> p c t", p=128)[:, :, t * TT:(t + 1) * TT], r_x1[li][t])
                layer(li, src, r_src, kvo, dst_fn)
            k.finish(r_out)

        k.dry = True
        program()
        ws.pos = 0
        k.dry = False
        program()
        print("[kernel] sbuf bytes remaining", nc.sbuf_bytes_remaining, flush=True)
        print(f"[kernel] instructions={k.n_ins} waits={k.n_wait} weight_slots={len(ws.reqs)}", flush=True)
    return nc


def _rope_tables():
    pos = np.arange(TALL, dtype=np.float32)
    inv = (np.float32(10000.0) ** (-np.arange(0, 64, 2, dtype=np.float32) / np.float32(64))).astype(np.float32)
    ang = (pos[:, None] * inv[None, :]).astype(np.float32)
    c, s = np.cos(ang).astype(np.float32), np.sin(ang).astype(np.float32)
    C = np.concatenate([c, c], 1).T
    S = np.concatenate([-s, s], 1).T
    return np.ascontiguousarray(C), np.ascontiguousarray(S)


def _params(inp, l):
    P = np.zeros((128, NPARAM), np.float32)
    P[:, P_AN:P_AN + 16] = inp["attn_norm"][l].reshape(16, 128).T
    P[:, P_MN:P_MN + 16] = inp["mlp_norm"][l].reshape(16, 128).T
    P[:, P_QLN:P_QLN + 4] = inp["q_lat_norm"][l].reshape(4, 128).T
    P[:, P_KVLN:P_KVLN + 4] = inp["kv_lat_norm"][l].reshape(4, 128).T
    P[:, P_QN] = inp["q_norm"][l][:128]
    P[:64, P_QR] = inp["q_norm"][l][128:]
    P[:, P_KN] = inp["k_norm"][l][:128]
    P[:64, P_KR] = inp["k_norm"][l][128:]
    cw = inp["conv_w"][l]
    for ch in range(8):
        for j in range(3):
            P[:, P_CW + 3 * ch + j] = cw[j, ch * 128:(ch + 1) * 128]
    P[:, P_PS:P_PS + 8] = inp["pool_scale"][l].reshape(8, 128).T
    return P


_NC_CACHE = {}


def _get_nc(layers, fused):
    key = (tuple(layers), fused)
    if key not in _NC_CACHE:
        _NC_CACHE[key] = build_program(list(layers), fused)
    return _NC_CACHE[key]


def _run(layers, inp, x_own, x_pre, fused):
    nc = _get_nc(layers, fused)
    C, S = _rope_tables()
    bf = ml_dtypes.bfloat16
    tri = (np.arange(128)[None, :] >= np.arange(128)[:, None]).astype(bf)
    ones = np.ones((128, 128), bf)
    swap = np.zeros((64, 64), bf)
    for i in range(32):
        swap[i + 32, i] = 1
        swap[i, i + 32] = 1
    ls = list(layers)
    common = {
        "params": np.stack([_params(inp, l) for l in ls]),
        "tri": tri, "ones": ones, "swap": swap,
    }
    for i, l in enumerate(ls):
        for nm in ["w_in", "w_uq", "w_ukv", "pool_w", "w_branch_a", "w_branch_b", "w_branch_c", "w_o", "w_up", "w_down"]:
            common[f"{nm}{i}"] = np.ascontiguousarray(inp[nm][l])
    in_maps = []
    for c in range(8):
        r = c % 2
        pos0 = r * NTOK
        ropeC = np.concatenate([C[:, 0:NPRE], C[:, pos0:pos0 + NTOK]], 1)
        ropeS = np.concatenate([S[:, 0:NPRE], S[:, pos0:pos0 + NTOK]], 1)
        tpos = np.concatenate([np.arange(0, NPRE), np.arange(pos0, pos0 + NTOK)]).astype(np.float32) + 1.0
        invc = np.stack([np.float32(1.0) / np.minimum(tpos, np.float32(w)) for w in (2, 4, 8, 16)]).astype(np.float32)
        invc = np.ascontiguousarray(np.broadcast_to(invc[None], (128, 4, NKEY)))
        kb = np.zeros((128, NKB), np.float32)
        if r == 0:
            keyidx = np.arange(NKB * 128).reshape(NKB, 128).T
            kb[keyidx < NPRE] = -30000.0
        m = dict(common)
        m.update({"xall": np.ascontiguousarray(np.concatenate([x_pre[c], x_own[c]], axis=1)),
                  "ropeC": np.ascontiguousarray(ropeC), "ropeS": np.ascontiguousarray(ropeS), "invc": invc,
                  "kbias": kb, "flag": np.full((128, 1), float(r), np.float32)})
        in_maps.append(m)
    res = run_bass_kernel_spmd(nc, in_maps, core_ids=list(range(8)))
    if DEBUG:
        global DBG_OUT
        DBG_OUT = [{nm: np.asarray(res.results[c][nm]) for nm in DBG_NAMES} for c in range(8)]
    return [np.asarray(res.results[c]["outT"]) for c in range(8)]


FUSED = True


def kernel(**inp):
    inp = {k_: np.asarray(v) for k_, v in inp.items()}
    x = inp["x"].astype(np.float32)
    B = x.shape[0]
    meta = np.broadcast_to(inp["meta_tokens"][None].astype(np.float32), (B, NMETA, D))
    hseq = np.concatenate([meta, x], axis=1)
    own = [np.ascontiguousarray(hseq[c // 2, (c % 2) * NTOK:(c % 2 + 1) * NTOK].T) for c in range(8)]
    zero = np.zeros((D, NPRE), np.float32)
    if FUSED:
        pre = [zero if c % 2 == 0 else own[c - 1] for c in range(8)]
        outs = _run([0, 1], inp, own, pre, True)
    else:
        cur = own
        for l in range(2):
            pre = [zero if c % 2 == 0 else cur[c - 1] for c in range(8)]
            cur = _run([l], inp, cur, pre, False)
        outs = cur
    full = np.stack([np.concatenate([outs[2 * b].T, outs[2 * b + 1].T], axis=0) for b in range(B)])
    return np.ascontiguousarray(full[:, NMETA:, :]).astype(np.float32)
```
